# Optimizing a Trainium2 kernel written in Bass

```python
import math
import jax, jax.numpy as jnp
from jax import lax
import numpy as np

D_MODEL = 1024
BATCH = 4
SEQ = 4096
DEPTH = 2
DEC_BATCH = 32
DEC_SEQ = 2048
PAST_LEN = 128

HEAD_DIM = 64
N_RET_HEADS = 4
N_FNET_GROUPS = 4
N_DIL_HEADS = 4
DIL_PAIRS = ((128, 1), (512, 4), (2048, 16))
N_DIL_GROUPS = 3
N_MLA_HEADS = 4
MLA_NOPE = 64
MLA_ROPE = 32
MLA_V = 64
Q_LORA = 256
KV_LORA = 128
D_FF = 2816
RET_CHUNK = 128
Q_BLOCK = 128
ROPE_THETA = 500000.0
RET_THETA = 10000.0
PARTIAL_ROT = HEAD_DIM // 4
EPS = 1e-6
NEG = -1e30

RET_W = N_RET_HEADS * HEAD_DIM
FNET_W = N_FNET_GROUPS * HEAD_DIM
DIL_W = N_DIL_GROUPS * N_DIL_HEADS * HEAD_DIM
D_IN = 4 * RET_W + FNET_W + 3 * DIL_W + Q_LORA + KV_LORA + MLA_ROPE
D_MIX = RET_W + FNET_W + N_DIL_HEADS * HEAD_DIM + N_MLA_HEADS * MLA_V
SPLIT_SIZES = (RET_W, RET_W, RET_W, RET_W, FNET_W, DIL_W, DIL_W, DIL_W, Q_LORA, KV_LORA, MLA_ROPE)

kernel_name = 'hybrid_bidir_parallel_heads_encoder'


def rms_norm(x, g):
    xf = x.astype(jnp.float32)
    y = xf * lax.rsqrt(jnp.mean(xf * xf, axis=-1, keepdims=True) + EPS)
    return (y * g.astype(jnp.float32)).astype(x.dtype)


def rope(x, pos, theta, rot_dim):
    half = rot_dim // 2
    inv = jnp.power(theta, -jnp.arange(half, dtype=jnp.float32) * 2.0 / rot_dim)
    ang = pos.astype(jnp.float32)[:, None] * inv[None, :]
    cos = jnp.cos(ang)[:, None, :]
    sin = jnp.sin(ang)[:, None, :]
    xf = x.astype(jnp.float32)
    x1 = xf[..., :half]
    x2 = xf[..., half:rot_dim]
    out = jnp.concatenate([x1 * cos - x2 * sin, x2 * cos + x1 * sin, xf[..., rot_dim:]], axis=-1)
    return out.astype(x.dtype)


def retention_one_direction(q, k, v, log_gamma, include_diag):
    B, H, S, d = q.shape
    C = RET_CHUNK
    N = S // C
    qc = q.reshape(B, H, N, C, d)
    kc = k.reshape(B, H, N, C, d)
    vc = v.reshape(B, H, N, C, d)
    idx = jnp.arange(C, dtype=jnp.float32)
    diff = idx[:, None] - idx[None, :]
    mask = (diff >= 0) if include_diag else (diff > 0)
    lg = log_gamma[:, None, None]
    decay = jnp.where(mask[None], jnp.exp(jnp.where(mask, diff, 0.0)[None] * lg), 0.0)
    scores = jnp.einsum('bhncd,bhnjd->bhncj', qc, kc) * decay[None, :, None]
    intra = jnp.einsum('bhncj,bhnje->bhnce', scores, vc)
    k_dec = kc * jnp.exp((C - 1 - idx)[None, :] * log_gamma[:, None])[None, :, None, :, None]
    kv = jnp.einsum('bhnjd,bhnje->nbhde', k_dec, vc)
    chunk_decay = jnp.exp(C * log_gamma)[None, :, None, None]

    def step(state, kv_n):
        return state * chunk_decay + kv_n, state

    _, prev = lax.scan(step, jnp.zeros(kv.shape[1:], jnp.float32), kv)
    q_dec = qc * jnp.exp((idx + 1.0)[None, :] * log_gamma[:, None])[None, :, None, :, None]
    cross = jnp.einsum('bhncd,nbhde->bhnce', q_dec, prev)
    return (intra + cross).reshape(B, H, S, d)


def retention_mixer(q, k, v, g, pos, dec_f, dec_b):
    B, S, _ = q.shape
    shp = (B, S, N_RET_HEADS, HEAD_DIM)
    q = rope(q.reshape(shp), pos, RET_THETA, HEAD_DIM)
    k = rope(k.reshape(shp), pos, RET_THETA, HEAD_DIM) * (HEAD_DIM ** -0.5)
    qh = q.transpose(0, 2, 1, 3).astype(jnp.float32)
    kh = k.transpose(0, 2, 1, 3).astype(jnp.float32)
    vh = v.reshape(shp).transpose(0, 2, 1, 3).astype(jnp.float32)
    lf = jax.nn.log_sigmoid(dec_f.astype(jnp.float32))
    lb = jax.nn.log_sigmoid(dec_b.astype(jnp.float32))
    fwd = retention_one_direction(qh, kh, vh, lf, True)
    bwd = retention_one_direction(qh[:, :, ::-1], kh[:, :, ::-1], vh[:, :, ::-1], lb, False)[:, :, ::-1]
    o = fwd + bwd
    o = o * lax.rsqrt(jnp.mean(o * o, axis=-1, keepdims=True) + EPS)
    o = o.transpose(0, 2, 1, 3).reshape(B, S, RET_W).astype(g.dtype)
    return jax.nn.silu(g) * o


def fourier_mixer(u, w_fmix):
    B, S, _ = u.shape
    ug = u.reshape(B, S, N_FNET_GROUPS, HEAD_DIM).astype(jnp.float32)
    f = jnp.fft.fft2(ug, axes=(1, 3), norm='ortho').real.astype(u.dtype)
    return jnp.einsum('bsgc,gce->bsge', f, w_fmix).reshape(B, S, FNET_W)


def dilated_group(q, k, v, dil, radius):
    B, S, H, dh = q.shape
    L = S // dil
    R = radius
    nblk = -(-L // R)
    Lp = nblk * R

    def to_sub(t, left, right):
        t = t.reshape(B, L, dil, H, dh).transpose(0, 2, 1, 3, 4)
        return jnp.pad(t, ((0, 0), (0, 0), (left, right), (0, 0), (0, 0)))

    qs = to_sub(q, 0, Lp - L).reshape(B, dil, nblk, R, H, dh)
    ks = to_sub(k, R, Lp - L + R).reshape(B, dil, nblk + 2, R, H, dh)
    vs = to_sub(v, R, Lp - L + R).reshape(B, dil, nblk + 2, R, H, dh)
    kw = jnp.concatenate([ks[:, :, :-2], ks[:, :, 1:-1], ks[:, :, 2:]], axis=3)
    vw = jnp.concatenate([vs[:, :, :-2], vs[:, :, 1:-1], vs[:, :, 2:]], axis=3)
    qi = jnp.arange(nblk)[:, None] * R + jnp.arange(R)[None, :]
    kj = jnp.arange(nblk)[:, None] * R - R + jnp.arange(3 * R)[None, :]
    valid = (jnp.abs(qi[:, :, None] - kj[:, None, :]) <= R) & (kj[:, None, :] >= 0) & (kj[:, None, :] < L)
    s = jnp.einsum('bgnqhd,bgnkhd->bgnhqk', qs, kw).astype(jnp.float32) * (dh ** -0.5)
    s = jnp.where(valid[None, None, :, None], s, NEG)
    m = jnp.max(s, axis=-1, keepdims=True)
    p = jnp.exp(s - m)
    den = jnp.sum(p, axis=-1)
    o = jnp.einsum('bgnhqk,bgnkhd->bgnqhd', p, vw.astype(jnp.float32))
    o = o / den.transpose(0, 1, 2, 4, 3)[..., None]
    lse = (m[..., 0] + jnp.log(den)).transpose(0, 1, 2, 4, 3)
    o = o.reshape(B, dil, Lp, H, dh)[:, :, :L].transpose(0, 2, 1, 3, 4).reshape(B, S, H, dh)
    lse = lse.reshape(B, dil, Lp, H)[:, :, :L].transpose(0, 2, 1, 3).reshape(B, S, H)
    return o, lse


def dilated_mixer(q, k, v, pos):
    B, S, _ = q.shape
    shp = (B, S, N_DIL_GROUPS, N_DIL_HEADS, HEAD_DIM)
    q = q.reshape(shp)
    k = k.reshape(shp)
    v = v.reshape(shp)
    outs, lses = [], []
    for gi, (win, dil) in enumerate(DIL_PAIRS):
        qg = rope(q[:, :, gi], pos, ROPE_THETA, PARTIAL_ROT)
        kg = rope(k[:, :, gi], pos, ROPE_THETA, PARTIAL_ROT)
        o, lse = dilated_group(qg, kg, v[:, :, gi], dil, win // (2 * dil))
        outs.append(o)
        lses.append(lse)
    wgt = jax.nn.softmax(jnp.stack(lses, 0), axis=0)
    o = jnp.sum(wgt[..., None] * jnp.stack(outs, 0), axis=0)
    return o.reshape(B, S, N_DIL_HEADS * HEAD_DIM).astype(q.dtype)


def mla_mixer(c_q, c_kv, k_rope, pos, q_norm, w_qb, kv_norm, w_kvb):
    B, S, _ = c_q.shape
    H = N_MLA_HEADS
    q = (rms_norm(c_q, q_norm) @ w_qb).reshape(B, S, H, MLA_NOPE + MLA_ROPE)
    q_nope = q[..., :MLA_NOPE]
    q_pe = rope(q[..., MLA_NOPE:], pos, ROPE_THETA, MLA_ROPE)
    kv = (rms_norm(c_kv, kv_norm) @ w_kvb).reshape(B, S, H, MLA_NOPE + MLA_V)
    k_nope = kv[..., :MLA_NOPE]
    v = kv[..., MLA_NOPE:]
    k_pe = rope(k_rope[:, :, None, :], pos, ROPE_THETA, MLA_ROPE)[:, :, 0]
    scale = (MLA_NOPE + MLA_ROPE) ** -0.5
    nq = S // Q_BLOCK
    qn_b = q_nope.reshape(B, nq, Q_BLOCK, H, MLA_NOPE).transpose(1, 0, 2, 3, 4)
    qp_b = q_pe.reshape(B, nq, Q_BLOCK, H, MLA_ROPE).transpose(1, 0, 2, 3, 4)

    def block(args):
        qn, qp = args
        s = (jnp.einsum('bqhd,bkhd->bhqk', qn, k_nope) + jnp.einsum('bqhr,bkr->bhqk', qp, k_pe)).astype(jnp.float32) * scale
        p = jax.nn.softmax(s, axis=-1).astype(v.dtype)
        return jnp.einsum('bhqk,bkhd->bqhd', p, v)

    o = lax.map(block, (qn_b, qp_b))
    return o.transpose(1, 0, 2, 3, 4).reshape(B, S, H * MLA_V)


def dwconv3(u, w, b):
    S = u.shape[1]
    up = jnp.pad(u, ((0, 0), (1, 1), (0, 0)))
    return up[:, :S] * w[0] + up[:, 1:S + 1] * w[1] + up[:, 2:] * w[2] + b


def trunk(x, c, params):
    S = x.shape[1]
    pos = jnp.arange(S)
    cond = jax.nn.silu(c)
    split_pts = [int(i) for i in np.cumsum(SPLIT_SIZES)[:-1]]
    for l in range(DEPTH):
        (w_ada, b_ada, n_pre_mix, w_in, dec_f, dec_b, w_fmix, q_norm, w_qb, kv_norm, w_kvb,
         w_out, n_post_mix, n_pre_ffn, w_up, conv_w, conv_b, w_down, n_post_ffn) = [p[l] for p in params]
        mod = cond @ w_ada + b_ada
        sh1, sc1, g1, sh2, sc2, g2 = [m[:, None, :] for m in jnp.split(mod, 6, axis=-1)]
        h = rms_norm(x, n_pre_mix) * (1.0 + sc1) + sh1
        z = h @ w_in
        rq, rk, rv, rg, fu, dq, dk, dv, cq, ckv, kr = jnp.split(z, split_pts, axis=-1)
        o = jnp.concatenate([
            retention_mixer(rq, rk, rv, rg, pos, dec_f, dec_b),
            fourier_mixer(fu, w_fmix),
            dilated_mixer(dq, dk, dv, pos),
            mla_mixer(cq, ckv, kr, pos, q_norm, w_qb, kv_norm, w_kvb)], axis=-1)
        x = x + g1 * rms_norm(o @ w_out, n_post_mix)
        h = rms_norm(x, n_pre_ffn) * (1.0 + sc2) + sh2
        u = dwconv3(h @ w_up, conv_w, conv_b)
        a, bu = jnp.split(u, 2, axis=-1)
        x = x + g2 * rms_norm((jax.nn.silu(a) * bu) @ w_down, n_post_ffn)
    return x


def setup_inputs(seed: int = 0) -> dict:
    key = jax.random.key(seed)
    ks = jax.random.split(key, 24)
    f32 = jnp.float32

    def nrm(k, shape, scale):
        return jax.random.normal(k, shape, f32) * scale

    L = DEPTH
    hh = jnp.arange(N_RET_HEADS, dtype=f32)
    decay_init = jnp.log(jnp.power(2.0, 5.0 + hh) - 1.0)
    return {
        'x_prompt': nrm(ks[0], (BATCH, SEQ, D_MODEL), 1.0),
        'x_sample': nrm(ks[1], (DEC_BATCH, DEC_SEQ, D_MODEL), 1.0),
        'c_prompt': nrm(ks[2], (BATCH, D_MODEL), 1.0),
        'c_sample': nrm(ks[3], (DEC_BATCH, D_MODEL), 1.0),
        'w_ada': nrm(ks[4], (L, D_MODEL, 6 * D_MODEL), D_MODEL ** -0.5),
        'b_ada': nrm(ks[5], (L, 6 * D_MODEL), 0.01),
        'norm_pre_mix': 1.0 + nrm(ks[6], (L, D_MODEL), 0.05),
        'w_in': nrm(ks[7], (L, D_MODEL, D_IN), D_MODEL ** -0.5),
        'ret_decay_fwd': decay_init[None] + nrm(ks[8], (L, N_RET_HEADS), 0.1),
        'ret_decay_bwd': decay_init[None] + nrm(ks[9], (L, N_RET_HEADS), 0.1),
        'w_fmix': nrm(ks[10], (L, N_FNET_GROUPS, HEAD_DIM, HEAD_DIM), HEAD_DIM ** -0.5),
        'mla_q_norm': 1.0 + nrm(ks[11], (L, Q_LORA), 0.05),
        'mla_w_qb': nrm(ks[12], (L, Q_LORA, N_MLA_HEADS * (MLA_NOPE + MLA_ROPE)), Q_LORA ** -0.5),
        'mla_kv_norm': 1.0 + nrm(ks[13], (L, KV_LORA), 0.05),
        'mla_w_kvb': nrm(ks[14], (L, KV_LORA, N_MLA_HEADS * (MLA_NOPE + MLA_V)), KV_LORA ** -0.5),
        'w_out': nrm(ks[15], (L, D_MIX, D_MODEL), D_MIX ** -0.5),
        'norm_post_mix': 1.0 + nrm(ks[16], (L, D_MODEL), 0.05),
        'norm_pre_ffn': 1.0 + nrm(ks[17], (L, D_MODEL), 0.05),
        'w_up': nrm(ks[18], (L, D_MODEL, 2 * D_FF), D_MODEL ** -0.5),
        'conv_w': nrm(ks[19], (L, 3, 2 * D_FF), 3 ** -0.5),
        'conv_b': nrm(ks[20], (L, 2 * D_FF), 0.01),
        'w_down': nrm(ks[21], (L, D_FF, D_MODEL), D_FF ** -0.5),
        'norm_post_ffn': 1.0 + nrm(ks[22], (L, D_MODEL), 0.05),
    }


def reference(x_prompt, x_sample, c_prompt, c_sample, w_ada, b_ada, norm_pre_mix, w_in,
              ret_decay_fwd, ret_decay_bwd, w_fmix, mla_q_norm, mla_w_qb, mla_kv_norm, mla_w_kvb,
              w_out, norm_post_mix, norm_pre_ffn, w_up, conv_w, conv_b, w_down, norm_post_ffn):
    params = (w_ada, b_ada, norm_pre_mix, w_in, ret_decay_fwd, ret_decay_bwd, w_fmix,
              mla_q_norm, mla_w_qb, mla_kv_norm, mla_w_kvb, w_out, norm_post_mix,
              norm_pre_ffn, w_up, conv_w, conv_b, w_down, norm_post_ffn)
    y_prompt = trunk(x_prompt, c_prompt, params)
    y_sample = trunk(x_sample, c_sample, params)
    return (y_prompt, y_sample)
```

```python
import contextlib
import numpy as np
import ml_dtypes
import concourse.bass as bass
import concourse.mybir as mybir
from concourse.bass_utils import run_bass_kernel_spmd
from concourse.ap import AP

F32 = mybir.dt.float32
BF16 = mybir.dt.bfloat16
AF = mybir.ActivationFunctionType
ALU = mybir.AluOpType
AX = mybir.AxisListType
NPBF = ml_dtypes.bfloat16

D = 1024
DIN = 4000
DFF = 2816
ZW = 4512
Z_RQ, Z_RK, Z_RV, Z_RG, Z_FU, Z_DQ, Z_DK, Z_DV = 0, 256, 512, 768, 1024, 1280, 2048, 2816
Z_MQ, Z_MKV, Z_KPE = 3584, 3968, 4480
EPS = 1e-6
RTW = 512 + 384 + 160
DILS = (1, 4, 16)
MLA_SCALE = 96.0 ** -0.5
DIL_SCALE = 64.0 ** -0.5
WIN = 510


class Res:
    __slots__ = ("w", "r", "psum")

    def __init__(self):
        self.w = None
        self.r = []
        self.psum = False


class Buf:
    def __init__(self, h):
        self.h = h
        self.res = Res()

    def __getitem__(self, k):
        return self.h[k]


def bcast_last(ap, n):
    return AP(ap.tensor, ap.offset, [list(a) for a in ap.ap] + [[0, n]])


def bcast_mid(ap, n):
    l = [list(a) for a in ap.ap]
    return AP(ap.tensor, ap.offset, [l[0], [0, n]] + l[1:])


class KB:
    def __init__(self, seqs, n_layers=2, debug=False):
        self.seqs = list(seqs)
        self.nseq = len(seqs)
        self.T = sum(seqs)
        self.L = n_layers
        self.debug = debug
        self.t0 = [sum(seqs[:i]) for i in range(self.nseq)]
        self.Tp = self.T + 2 * self.nseq
        nc = bass.Bass("TRN2", target_bir_lowering=False)
        self.nc = nc
        self.eng = {"pe": nc.tensor, "act": nc.scalar, "dve": nc.vector, "pool": nc.gpsimd, "sp": nc.sync}
        self.sem = {k: nc.alloc_semaphore("s_" + k) for k in ("pe", "act", "dve", "pool")}
        self.cnt = {k: 0 for k in self.sem}
        self.waited = {k: {} for k in self.eng}
        self.ND = 24
        self.dsem = [nc.alloc_semaphore("d%d" % i) for i in range(self.ND)]
        self.dcnt = [0] * self.ND
        self.drr = 0
        self.uid = 0
        self.dram = {}
        self.stack = contextlib.ExitStack()
        self.cut = 0
        self.stop = 1000

    def _wait(self, e, deps):
        best = {}
        for key, val in deps:
            if key == "pe" and e == "pe":
                continue
            if best.get(key, 0) < val:
                best[key] = val
        w = self.waited[e]
        for key, val in best.items():
            if w.get(key, 0) >= val:
                continue
            sem = self.sem[key] if isinstance(key, str) else self.dsem[key[1]]
            self.eng[e].wait_ge(sem, val)
            w[key] = val

    def _deps(self, reads, writes, e=None):
        deps = []
        for r in reads:
            if r.res.w is not None:
                deps.append(r.res.w)
        for w in writes:
            if w.res.w is not None and w.res.w[0] != e:
                deps.append(w.res.w)
            deps.extend(t for t in w.res.r if t[0] != e)
        return deps

    def op(self, e, fn, reads=(), writes=(), inc=True):
        pr = [r for r in reads if r.res.psum]
        deps = []
        if pr:
            deps = [r.res.w for r in pr if r.res.w is not None]
            reads = [r for r in reads if not r.res.psum]
            writes = list(writes) + pr
        self._wait(e, deps + self._deps(reads, writes, e))
        ins = fn(self.eng[e])
        if inc:
            self.cnt[e] += 1
            ins.then_inc(self.sem[e], 1)
            tok = (e, self.cnt[e])
        else:
            tok = (e, self.cnt[e] + 1)
        for r in reads:
            r.res.r.append(tok)
        for w in writes:
            w.res.w = tok
            w.res.r = []
        return ins

    def dma(self, q, out, in_, reads=(), writes=(), slow=False):
        self._wait(q, self._deps(reads, writes, q))
        k = self.drr
        self.drr = (self.drr + 1) % self.ND
        if self.dcnt[k] > 0:
            self._wait(q, [(("d", k), 16 * self.dcnt[k])])
        self.dcnt[k] += 1
        self.eng[q].dma_start(out=out, in_=in_, allow_slow_non_contiguous=slow).then_inc(self.dsem[k], 16)
        tok = (("d", k), 16 * self.dcnt[k])
        for r in reads:
            r.res.r.append(tok)
        for w in writes:
            w.res.w = tok
            w.res.r = []

    def barrier(self):
        toks = [(k, self.cnt[k]) for k in self.cnt if self.cnt[k] > 0]
        toks += [(("d", k), 16 * self.dcnt[k]) for k in range(self.ND) if self.dcnt[k] > 0]
        for e in self.eng:
            self._wait(e, [t for t in toks if t[0] != e])

    def sb(self, shape, dt, name=None):
        self.uid += 1
        nm = "%s_%d" % (name or "sb", self.uid)
        return Buf(self.stack.enter_context(self.nc.sbuf_tensor(nm, list(shape), dt)))

    def begin_phase(self):
        self.gstack = self.stack
        self.stack = contextlib.ExitStack()

    def end_phase(self):
        self.barrier()
        self.stack.close()
        self.stack = self.gstack

    def ps(self, shape, dt, name=None):
        self.uid += 1
        b = Buf(self.nc.alloc_psum_tensor(name or ("ps%d" % self.uid), list(shape), dt))
        b.res.psum = True
        return b

    def din(self, name, shape, dt):
        t = self.nc.dram_tensor(name, list(shape), dt, kind="ExternalInput").ap()
        self.dram[name] = t
        return t

    def dscr(self, name, shape, dt):
        kind = "ExternalOutput" if self.debug else "Internal"
        t = self.nc.dram_tensor(name, list(shape), dt, kind=kind).ap()
        self.dram[name] = t
        return t

    def rstd(self, ss, out, n, tmp):
        self.op("act", lambda e: e.activation(out=tmp[:], in_=ss[:], func=AF.Sqrt, bias=self.epsb[:, 0:1],
                                              scale=1.0 / n), [ss, self.epsb], [tmp])
        self.op("dve", lambda e: e.reciprocal(out[:], tmp[:]), [tmp], [out])

    def transposes(self, pt, src_aps, srcbuf, width=128):
        n = len(src_aps)
        for i, a in enumerate(src_aps):
            self.op("pe", lambda e, i=i, a=a: e.transpose(pt[0:width, i, :], a, self.ident[:]),
                    [srcbuf, self.ident], [pt], inc=(i == n - 1))

    def build(self):
        nc = self.nc
        L, T, NS = self.L, self.T, self.nseq
        x_in = self.din("x", [T, D], F32)
        cT_in = self.din("cT", [D, NS], F32)
        w_ada = self.din("w_ada", [L, D, 6 * D], F32)
        b_ada = self.din("b_ada", [L, 6 * D], F32)
        n_pre_mix = self.din("norm_pre_mix", [L, D], F32)
        w_in = self.din("w_in", [L, D, DIN], F32)
        dec_f = self.din("ret_decay_fwd", [L, 4], F32)
        dec_b = self.din("ret_decay_bwd", [L, 4], F32)
        w_fmix = self.din("w_fmix", [L, 4, 64, 64], F32)
        q_norm = self.din("mla_q_norm", [L, 256], F32)
        w_qb = self.din("mla_w_qb", [L, 256, 384], F32)
        kv_norm = self.din("mla_kv_norm", [L, 128], F32)
        w_kvb = self.din("mla_w_kvb", [L, 128, 512], F32)
        w_out = self.din("w_out", [L, D, D], F32)
        n_post_mix = self.din("norm_post_mix", [L, D], F32)
        n_pre_ffn = self.din("norm_pre_ffn", [L, D], F32)
        w_up = self.din("w_up", [L, D, 2 * DFF], F32)
        conv_w = self.din("conv_w", [L, 3, 2 * DFF], F32)
        conv_b = self.din("conv_b", [L, 2 * DFF], F32)
        w_down = self.din("w_down", [L, DFF, D], F32)
        n_post_ffn = self.din("norm_post_ffn", [L, D], F32)
        ident_in = self.din("c_ident", [128, 128], BF16)
        rope_in = self.din("c_rope", [4096, RTW], F32)
        dft_in = self.din("c_dft", [2, 4096, 4096], BF16)
        c64_in = self.din("c_c64", [64, 2, 128], F32)
        ret_in = self.din("c_ret", [128, 4, 128], F32)
        retq_in = self.din("c_retq", [128, 2, 128], F32)
        retw_in = self.din("c_retw", [128, 8], F32)
        dmask_in = self.din("c_dmask", [4, 128, 256], F32)
        y_out = self.nc.dram_tensor("y", [T, D], F32, kind="ExternalOutput").ap()
        zs = self.dscr("zs", [T, ZW], BF16)
        osx = self.dscr("os", [T, D], BF16)
        dsc = self.dscr("dsc", [T, 3, 256], BF16)
        dstt = self.dscr("dst", [T, 3, 8], F32)
        xs = self.dscr("xs", [T, D], F32)
        x1s = self.dscr("x1s", [T, D], F32)
        h2s = self.dscr("h2s", [D, self.Tp], BF16)
        vts = self.dscr("vts", [DFF, T], BF16)
        modd = self.dscr("modd", [L, NS, 6 * D], F32)
        self.zs, self.osx, self.dsc, self.dstt = zs, osx, dsc, dstt

        self.ident = self.sb([128, 128], BF16, "ident")
        self.dma("sp", self.ident[:], ident_in[:, :], [], [self.ident])
        self.epsb = self.sb([128, 1], F32, "epsb")
        self.op("dve", lambda e: e.memset(self.epsb[:], EPS), [], [self.epsb])
        self.zero = self.sb([128, 512], BF16, "zero")
        self.op("dve", lambda e: e.memset(self.zero[:], 0.0), [], [self.zero])
        self.bank = [self.ps([128, 512], F32, "bank%d" % i) for i in range(8)]

        self._zero_halo(h2s)

        self.nph = 0

        def run(fn, *a):
            self.nph += 1
            if self.nph > getattr(self, "stop", 1000):
                return
            self.begin_phase()
            fn(*a)
            self.end_phase()

        run(self.prephase, cT_in, w_ada, b_ada, n_pre_mix, n_post_mix, n_pre_ffn, n_post_ffn, modd)
        for l in range(L):
            xsrc = x_in if l == 0 else xs
            xdst = y_out if l == L - 1 else xs
            run(self.phase_a, l, xsrc, w_in, q_norm, w_qb, kv_norm, w_kvb, rope_in, modd)
            run(self.retention, l, dec_f, dec_b, ret_in, retq_in, retw_in)
            run(self.fourier, l, w_fmix, c64_in, dft_in)
            run(self.dilated, l, dmask_in)
            run(self.mla, l)
            run(self.phase_c1, l, xsrc, w_out, modd, x1s, h2s)
            run(self.phase_c2a, l, w_up, conv_w, conv_b, h2s, vts)
            run(self.phase_c2b, l, w_down, modd, x1s, vts, xdst)
        self.barrier()
        return nc

    def _zero_halo(self, h2s):
        for q in range(self.nseq):
            for col in (self.t0[q] + 2 * q, self.t0[q] + 2 * q + self.seqs[q] + 1):
                dst = AP(h2s.tensor, col, [[self.Tp, 128], [128 * self.Tp, 8], [1, 1]])
                src = self.zero[:, 0:8].rearrange("p (a b) -> p a b", b=1)
                self.dma("sp", dst, src, [self.zero], [], slow=True)

    def prephase(self, cT_in, w_ada, b_ada, n_pre_mix, n_post_mix, n_pre_ffn, n_post_ffn, modd):
        NS, L = self.nseq, self.L
        cT = self.sb([128, 8, NS], F32, "cT")
        self.dma("sp", cT[:], cT_in.rearrange("(k p) u -> p k u", p=128), [], [cT], slow=True)
        sg = self.sb([128, 8, NS], F32, "sg")
        cTb = self.sb([128, 8, NS], BF16, "cTb")
        self.op("act", lambda e: e.activation(out=sg[:], in_=cT[:], func=AF.Sigmoid), [cT], [sg])
        self.op("dve", lambda e: e.tensor_tensor(cTb[:], cT[:], sg[:], ALU.mult), [cT, sg], [cTb])
        wa = [self.sb([128, 8, 512], BF16, "wa%d" % i) for i in range(2)]
        mod = self.sb([NS, 6 * D], F32, "mod")
        brow = self.sb([NS, 6 * D], F32, "brow")
        nrm = self.sb([NS, 4, D], F32, "nrm")
        out6 = self.sb([NS, 6, D], F32, "out6")
        for l in range(L):
            self.dma("sp", brow[:], AP(b_ada.tensor, l * 6 * D, [[0, NS], [1, 6 * D]]), [], [brow])
            for i, nt in enumerate((n_pre_mix, n_post_mix, n_pre_ffn, n_post_ffn)):
                self.dma("sp", nrm[:, i, :], AP(nt.tensor, l * D, [[0, NS], [1, D]]), [], [nrm])
            for nb in range(12):
                w = wa[nb % 2]
                self.dma("pool", w[:], w_ada[l, :, nb * 512:(nb + 1) * 512].rearrange("(k p) n -> p k n", p=128),
                         [], [w])
                pb = self.bank[nb % 2]
                for k in range(8):
                    self.op("pe", lambda e, k=k, w=w, pb=pb: e.matmul(pb[0:NS, :], cTb[:, k, :], w[:, k, :],
                                                                     start=(k == 0), stop=(k == 7)),
                            [cTb, w], [pb], inc=(k == 7))
                self.op("dve", lambda e, pb=pb, nb=nb: e.tensor_tensor(mod[:, nb * 512:(nb + 1) * 512], pb[0:NS, :],
                                                                     brow[:, nb * 512:(nb + 1) * 512], ALU.add),
                        [pb, brow], [mod])
            m = lambda i: mod[:, i * D:(i + 1) * D]
            self.op("dve", lambda e: e.scalar_tensor_tensor(out6[:, 0, :], m(1), 1.0, nrm[:, 0, :], ALU.add, ALU.mult),
                    [mod, nrm], [out6])
            self.op("dve", lambda e: e.tensor_copy(out6[:, 1, :], m(0)), [mod], [out6])
            self.op("dve", lambda e: e.tensor_tensor(out6[:, 2, :], m(2), nrm[:, 1, :], ALU.mult), [mod, nrm], [out6])
            self.op("dve", lambda e: e.scalar_tensor_tensor(out6[:, 3, :], m(4), 1.0, nrm[:, 2, :], ALU.add, ALU.mult),
                    [mod, nrm], [out6])
            self.op("dve", lambda e: e.tensor_copy(out6[:, 4, :], m(3)), [mod], [out6])
            self.op("dve", lambda e: e.tensor_tensor(out6[:, 5, :], m(5), nrm[:, 3, :], ALU.mult), [mod, nrm], [out6])
            self.dma("sp", modd[l, :, :], out6[:].rearrange("u a d -> u (a d)"), [out6], [])

    def load_rows(self, buf, modd, l, q, idxs):
        for i, ix in enumerate(idxs):
            src = AP(modd.tensor, (l * self.nseq + q) * 6 * D + ix * D, [[0, 128], [1, D]])
            self.dma("sp", buf[:, i, :], src, [], [buf])

    def phase_a(self, l, xsrc, w_in, q_norm, w_qb, kv_norm, w_kvb, rope_in, modd):
        zs = self.zs
        win = self.sb([128, 8, DIN], BF16, "win")
        for k in range(8):
            self.dma("pool", win[:, k, :], w_in[l, k * 128:(k + 1) * 128, :], [], [win])
        wqb = self.sb([128, 2, 384], BF16, "wqb")
        self.dma("pool", wqb[:], w_qb[l].rearrange("(k p) n -> p k n", p=128), [], [wqb])
        wkvb = self.sb([128, 512], BF16, "wkvb")
        self.dma("pool", wkvb[:], w_kvb[l], [], [wkvb])
        qn = self.sb([128, 256], F32, "qn")
        self.dma("sp", qn[:], AP(q_norm.tensor, l * 256, [[0, 128], [1, 256]]), [], [qn])
        kvn = self.sb([128, 128], F32, "kvn")
        self.dma("sp", kvn[:], AP(kv_norm.tensor, l * 128, [[0, 128], [1, 128]]), [], [kvn])
        rows = self.sb([128, 2, D], F32, "arows")
        xt = [self.sb([128, D], F32, "a_xt%d" % i) for i in range(2)]
        rt = [self.sb([128, RTW], F32, "a_rt%d" % i) for i in range(2)]
        junk = self.sb([128, D], BF16, "a_junk")
        ss = self.sb([128, 4], F32, "a_ss")
        sst = self.sb([128, 4], F32, "a_sst")
        rs = self.sb([128, 4], F32, "a_rs")
        hf = self.sb([128, D], F32, "a_hf")
        hb = self.sb([128, D], BF16, "a_hb")
        hT = [self.sb([128, 8, 128], BF16, "a_hT%d" % i) for i in range(2)]
        zt = [self.sb([128, ZW], BF16, "a_zt%d" % i) for i in range(2)]
        cn = self.sb([128, 384], BF16, "a_cn")
        cnT = self.sb([128, 3, 128], BF16, "a_cnT")
        tmp = [self.sb([128, 256], F32, "a_tmp%d" % i) for i in range(4)]
        pT = self.ps_bf(0)
        pT2 = self.ps_bf(1)
        pz = [self.bank[2], self.bank[3], self.bank[4], self.bank[5]]
        pq = self.bank[6]
        pkv = self.bank[7]

        tiles = [(q, i) for q in range(self.nseq) for i in range(self.seqs[q] // 128)]

        def load_x(n):
            q, i = tiles[n]
            r0 = self.t0[q] + i * 128
            self.dma("sp", xt[n % 2][:], xsrc[r0:r0 + 128, :], [], [xt[n % 2]])

        def load_rt(n):
            q, i = tiles[n]
            self.dma("sp", rt[n % 2][:], rope_in[i * 128:(i + 1) * 128, :], [], [rt[n % 2]])

        def rope(e, buf, x1, x2, cos, sin, w):
            shp = None
            t1, t2, t3, t4 = [t[:, 0:w].rearrange("p (h d) -> p h d", h=x1.shape[1]) for t in tmp]
            self.op(e, lambda g: g.tensor_tensor(t1, x1, cos, ALU.mult), [buf] + rtb, [tmp[0]])
            self.op(e, lambda g: g.tensor_tensor(t2, x2, sin, ALU.mult), [buf] + rtb, [tmp[1]])
            self.op(e, lambda g: g.tensor_tensor(t3, x2, cos, ALU.mult), [buf] + rtb, [tmp[2]])
            self.op(e, lambda g: g.tensor_tensor(t4, x1, sin, ALU.mult), [buf] + rtb, [tmp[3]])
            self.op(e, lambda g: g.tensor_tensor(x1, t1, t2, ALU.subtract), [tmp[0], tmp[1]], [buf])
            self.op(e, lambda g: g.tensor_tensor(x2, t3, t4, ALU.add), [tmp[2], tmp[3]], [buf])

        ss1 = self.sb([128, 4], F32, "a_ss1")
        sst1 = self.sb([128, 4], F32, "a_sst1")
        rs1 = self.sb([128, 4], F32, "a_rs1")
        state = {"q": -1}

        def stage1(n):
            q, i = tiles[n]
            if q != state["q"]:
                self.load_rows(rows, modd, l, q, (0, 1))
                state["q"] = q
            X, HT = xt[n % 2], hT[n % 2]
            self.op("act", lambda e: e.activation(out=junk[:], in_=X[:], func=AF.Square, accum_out=ss1[:, 0:1]),
                    [X], [junk, ss1])
            self.rstd_cols(ss1, rs1, sst1, 0, 1, D)
            self.op("dve", lambda e: e.scalar_tensor_tensor(hf[:], X[:], rs1[:, 0:1], rows[:, 0, :], ALU.mult, ALU.mult),
                    [X, rs1, rows], [hf])
            self.op("pool", lambda e: e.tensor_tensor(hb[:], hf[:], rows[:, 1, :], ALU.add), [hf, rows], [hb])

        def stage1b(n):
            HT = hT[n % 2]
            self.transposes(pT, [hb[:, k * 128:(k + 1) * 128] for k in range(8)], hb)
            self.op("act", lambda e: e.activation(out=HT[:], in_=pT[:], func=AF.Copy), [pT], [HT])

        def stage2(n, hook):
            q, i = tiles[n]
            R, Z, HT = rt[n % 2], zt[n % 2], hT[n % 2]
            rtb.clear()
            rtb.append(R)
            r0 = self.t0[q] + i * 128
            for nb in range(8):
                c0 = nb * 512
                cw = min(512, DIN - c0)
                pb = pz[nb % 4]
                for k in range(8):
                    self.op("pe", lambda e, k=k, pb=pb, c0=c0, cw=cw: e.matmul(
                        pb[:, 0:cw], HT[:, k, :], win[:, k, c0:c0 + cw], start=(k == 0), stop=(k == 7)),
                        [HT, win], [pb], inc=(k == 7))
                if nb == 1:
                    self.op("act", lambda e, pb=pb: e.activation(out=Z[:, 512:768], in_=pb[:, 0:256], func=AF.Copy),
                            [pb], [Z])
                    self.op("act", lambda e, pb=pb: e.activation(out=Z[:, 768:1024], in_=pb[:, 256:512], func=AF.Silu),
                            [pb], [Z])
                elif nb < 7:
                    self.op("act", lambda e, pb=pb, c0=c0: e.activation(out=Z[:, c0:c0 + 512], in_=pb[:], func=AF.Copy),
                            [pb], [Z])
                    if nb == 3:
                        hook()
                else:
                    self.op("act", lambda e, pb=pb: e.activation(out=junk[:, 0:256], in_=pb[:, 0:256], func=AF.Square,
                                                                accum_out=ss[:, 1:2]), [pb], [junk, ss])
                    self.op("act", lambda e, pb=pb: e.activation(out=junk[:, 256:384], in_=pb[:, 256:384],
                                                                func=AF.Square, accum_out=ss[:, 2:3]), [pb], [junk, ss])
                    self.op("act", lambda e, pb=pb: e.activation(out=Z[:, Z_KPE:Z_KPE + 32], in_=pb[:, 384:416],
                                                                func=AF.Copy), [pb], [Z])
                    self.rstd_cols(ss, rs, sst, 1, 2, 256)
                    self.rstd_cols(ss, rs, sst, 2, 3, 128)
                    self.op("dve", lambda e, pb=pb: e.scalar_tensor_tensor(cn[:, 0:256], pb[:, 0:256], rs[:, 1:2], qn[:],
                                                                          ALU.mult, ALU.mult), [pb, rs, qn], [cn])
                    self.op("dve", lambda e, pb=pb: e.scalar_tensor_tensor(cn[:, 256:384], pb[:, 256:384], rs[:, 2:3],
                                                                          kvn[:], ALU.mult, ALU.mult), [pb, rs, kvn], [cn])
                    self.transposes(pT2, [cn[:, k * 128:(k + 1) * 128] for k in range(3)], cn)
                    self.op("act", lambda e: e.activation(out=cnT[:], in_=pT2[:, 0:3, :], func=AF.Copy), [pT2], [cnT])
                    for k in range(2):
                        self.op("pe", lambda e, k=k: e.matmul(pq[:, 0:384], cnT[:, k, :], wqb[:, k, :], start=(k == 0),
                                                             stop=(k == 1)), [cnT, wqb], [pq], inc=(k == 1))
                    self.op("pe", lambda e: e.matmul(pkv[:, 0:512], cnT[:, 2, :], wkvb[:], start=True, stop=True),
                            [cnT, wkvb], [pkv])
                    self.op("act", lambda e: e.activation(out=Z[:, Z_MQ:Z_MQ + 384], in_=pq[:, 0:384], func=AF.Copy),
                            [pq], [Z])
                    self.op("act", lambda e: e.activation(out=Z[:, Z_MKV:Z_MKV + 512], in_=pkv[:], func=AF.Copy),
                            [pkv], [Z])
            v = Z[:, 0:512].rearrange("p (h d) -> p h d", h=8)
            c = R[:, 0:256].rearrange("p (h d) -> p h d", h=8)
            s = R[:, 256:512].rearrange("p (h d) -> p h d", h=8)
            rope("pool", Z, v[:, :, 0:32], v[:, :, 32:64], c, s, 256)
            v = Z[:, Z_DQ:Z_DQ + 1536].rearrange("p (h d) -> p h d", h=24)
            c = R[:, 512:704].rearrange("p (h d) -> p h d", h=24)
            s = R[:, 704:896].rearrange("p (h d) -> p h d", h=24)
            rope("dve", Z, v[:, :, 0:8], v[:, :, 8:16], c, s, 192)
            v = Z[:, Z_MQ:Z_MQ + 384].rearrange("p (h d) -> p h d", h=4)
            c = R[:, 896:960].rearrange("p (h d) -> p h d", h=4)
            s = R[:, 976:1040].rearrange("p (h d) -> p h d", h=4)
            rope("dve", Z, v[:, :, 64:80], v[:, :, 80:96], c, s, 64)
            v = Z[:, Z_KPE:Z_KPE + 32].rearrange("p (h d) -> p h d", h=1)
            c = R[:, 960:976].rearrange("p (h d) -> p h d", h=1)
            s = R[:, 1040:1056].rearrange("p (h d) -> p h d", h=1)
            rope("dve", Z, v[:, :, 0:16], v[:, :, 16:32], c, s, 16)
            self.dma("sp", zs[r0:r0 + 128, :], Z[:], [Z], [])


        rtb = []
        NTL = len(tiles)
        load_x(0)
        if NTL > 1:
            load_x(1)
        load_rt(0)
        stage1(0)
        stage1b(0)
        for n in range(NTL):
            if n + 1 < NTL:
                load_rt(n + 1)
                stage1(n + 1)
            if n + 2 < NTL:
                load_x(n + 2)
            stage2(n, (lambda n=n: stage1b(n + 1)) if n + 1 < NTL else (lambda: None))

    def ps_bf(self, i):
        return _View(self.bank[i].h[:].bitcast(BF16).rearrange("p (a b) -> p a b", b=128), self.bank[i].res)

    def rstd_cols(self, ss, rs, tmp, a, b, n):
        self.op("act", lambda e: e.activation(out=tmp[:, a:b], in_=ss[:, a:b], func=AF.Sqrt, bias=self.epsb[:, 0:1],
                                              scale=1.0 / n), [ss, self.epsb], [tmp])
        self.op("dve", lambda e: e.reciprocal(rs[:, a:b], tmp[:, a:b]), [tmp], [rs])


    def retention(self, l, dec_f, dec_b, ret_in, retq_in, retw_in):
        zs, osx = self.zs, self.osx
        raw8 = self.sb([128, 8], F32, "r_raw8")
        self.dma("sp", raw8[:, 0:4], AP(dec_f.tensor, l * 4, [[0, 128], [1, 4]]), [], [raw8])
        self.dma("sp", raw8[:, 4:8], AP(dec_b.tensor, l * 4, [[0, 128], [1, 4]]), [], [raw8])
        rawp = self.sb([128, 2, 2], F32, "r_rawp")
        for half in range(2):
            for di, dt_ in enumerate((dec_f, dec_b)):
                self.dma("sp", rawp[half * 64:(half + 1) * 64, di, :],
                         AP(dt_.tensor, l * 4 + half, [[0, 64], [2, 2]]), [], [rawp], slow=True)
        lg8 = self.sb([128, 8], F32, "r_lg8")
        lgp = self.sb([128, 4], F32, "r_lgp")

        def logsig(dst, src):
            self.op("act", lambda e: e.activation(out=dst, in_=src, func=AF.Exp, scale=-1.0), [raw8, rawp], [lg8, lgp])
            self.op("dve", lambda e: e.tensor_scalar(dst, dst, 1.0, None, ALU.add), [lg8, lgp], [lg8, lgp])
            self.op("act", lambda e: e.activation(out=dst, in_=dst, func=AF.Ln), [lg8, lgp], [lg8, lgp])
            self.op("dve", lambda e: e.tensor_scalar(dst, dst, -1.0, None, ALU.mult), [lg8, lgp], [lg8, lgp])

        logsig(lg8[:], raw8[:])
        logsig(lgp[:], rawp[:].rearrange("p a b -> p (a b)"))
        rc = self.sb([128, 4, 128], F32, "r_rc")
        self.dma("sp", rc[:], ret_in[:, :, :], [], [rc])
        rq = self.sb([128, 2, 128], F32, "r_rq")
        self.dma("sp", rq[:], retq_in[:, :, :], [], [rq])
        rw = self.sb([128, 8], F32, "r_rw")
        self.dma("sp", rw[:], retw_in[:, :], [], [rw])
        DT = self.sb([128, 4, 128], F32, "r_DT")
        e1 = self.sb([128, 128], F32, "r_e1")
        e2 = self.sb([128, 128], F32, "r_e2")
        for h in range(4):
            self.op("act", lambda e, h=h: e.activation(out=e1[:], in_=rc[:, 0, :], func=AF.Exp, scale=lg8[:, h:h + 1]),
                    [rc, lg8], [e1])
            self.op("act", lambda e, h=h: e.activation(out=e2[:], in_=rc[:, 1, :], func=AF.Exp,
                                                      scale=lg8[:, 4 + h:5 + h]), [rc, lg8], [e2])
            self.op("dve", lambda e: e.tensor_tensor(e1[:], e1[:], rc[:, 2, :], ALU.mult), [e1, rc], [e1])
            self.op("dve", lambda e: e.tensor_tensor(e2[:], e2[:], rc[:, 3, :], ALU.mult), [e2, rc], [e2])
            self.op("dve", lambda e, h=h: e.tensor_tensor(DT[:, h, :], e1[:], e2[:], ALU.add), [e1, e2], [DT])
        QFB = self.sb([128, 2, 2, 128], F32, "r_QFB")
        for di in range(2):
            for pr in range(2):
                self.op("act", lambda e, di=di, pr=pr: e.activation(out=QFB[:, di, pr, :], in_=rq[:, di, :], func=AF.Exp,
                                                                  scale=lgp[:, di * 2 + pr:di * 2 + pr + 1]),
                        [rq, lgp], [QFB])
        WFB = self.sb([128, 8], F32, "r_WFB")
        self.op("dve", lambda e: e.tensor_tensor(WFB[:], rw[:], lg8[:], ALU.mult), [rw, lg8], [WFB])
        self.op("act", lambda e: e.activation(out=WFB[:], in_=WFB[:], func=AF.Exp), [WFB], [WFB])
        dexp = self.sb([128, 4], F32, "r_dexp")
        self.op("act", lambda e: e.activation(out=dexp[:], in_=lgp[:], func=AF.Exp, scale=128.0), [lgp], [dexp])
        DEC = self.sb([128, 4, 64], F32, "r_DEC")
        self.op("dve", lambda e: e.tensor_copy(DEC[:], bcast_last(dexp[:], 64)), [dexp], [DEC])

        if getattr(self, 'cut', 0) == 1:
            return
        NMAX = max(self.seqs) // 128
        kvall = self.sb([128, NMAX, 4, 64], F32, "r_kvall")
        stF = self.sb([128, NMAX, 2, 64], BF16, "r_stF")
        stB = self.sb([128, NMAX, 2, 64], BF16, "r_stB")
        curF = self.sb([128, 2, 64], F32, "r_curF")
        curB = self.sb([128, 2, 64], F32, "r_curB")
        kvt = [self.sb([128, 512], BF16, "r_kvt%d" % i) for i in range(2)]
        kw = [self.sb([128, 2, 256], BF16, "r_kw%d" % i) for i in range(2)]
        qk = [self.sb([128, 1024], BF16, "r_qk%d" % i) for i in range(2)]
        qkT = [self.sb([128, 4, 128], BF16, "r_qkT%d" % i) for i in range(2)]
        qfT = [self.sb([128, 2, 128], BF16, "r_qfT%d" % i) for i in range(2)]
        qbT = [self.sb([128, 2, 128], BF16, "r_qbT%d" % i) for i in range(2)]
        stt = [self.sb([128, 4, 128], BF16, "r_stt%d" % i) for i in range(2)]
        ot = [self.sb([128, 256], BF16, "r_ot%d" % i) for i in range(2)]
        o1 = self.sb([128, 256], F32, "r_o1")
        junk = self.sb([128, 64], BF16, "r_junk")
        ssr = self.sb([128, 4], F32, "r_ssr")
        sst = self.sb([128, 4], F32, "r_sst")
        rr = self.sb([128, 4], F32, "r_rr")

        for q in range(self.nseq):
            S = self.seqs[q]
            N = S // 128
            t0 = self.t0[q]
            def load1(n):
                self.dma("sp", kvt[n % 2][:], zs[t0 + n * 128:t0 + (n + 1) * 128, 256:768], [], [kvt[n % 2]])
            load1(0)
            for n in range(N):
                if n + 1 < N:
                    load1(n + 1)
                KV, KW = kvt[n % 2], kw[n % 2]
                pk = self.bank[(n % 2) * 4]
                k3 = KV[:, 0:256].rearrange("p (h d) -> p h d", h=4)
                for di in range(2):
                    self.op("dve" if di == 0 else "pool", lambda e, di=di: e.tensor_tensor(
                        KW[:, di, :].rearrange("p (h d) -> p h d", h=4), k3, bcast_last(WFB[:, di * 4:di * 4 + 4], 64),
                        ALU.mult), [KV, WFB], [KW])
                for di in range(2):
                    for pr in range(2):
                        c0 = (di * 2 + pr) * 128
                        self.op("pe", lambda e, di=di, pr=pr, c0=c0: e.matmul(
                            pk[:, c0:c0 + 128], KW[:, di, pr * 128:(pr + 1) * 128], KV[:, 256 + pr * 128:256 + (pr + 1) * 128],
                            start=True, stop=True), [KW, KV], [pk], inc=(di == 1 and pr == 1))
                pk3 = pk[:].rearrange("p (a b) -> p a b", a=4)
                self.op("act", lambda e: e.activation(out=kvall[0:64, n, :, :], in_=pk3[0:64, :, 0:64], func=AF.Copy),
                        [pk], [kvall])
                self.op("act", lambda e: e.activation(out=kvall[64:128, n, :, :], in_=pk3[64:128, :, 64:128], func=AF.Copy),
                        [pk], [kvall])
            if getattr(self, 'cut', 0) == 2:
                return
            self.op("dve", lambda e: e.memset(curF[:], 0.0), [], [curF])
            self.op("pool", lambda e: e.memset(curB[:], 0.0), [], [curB])
            for n in range(N):
                self.op("dve", lambda e, n=n: e.tensor_copy(stF[:, n, :, :], curF[:]), [curF], [stF])
                if n < N - 1:
                    self.op("dve", lambda e: e.tensor_tensor(curF[:], curF[:], DEC[:, 0:2, :], ALU.mult), [curF, DEC], [curF])
                    self.op("dve", lambda e, n=n: e.tensor_tensor(curF[:], curF[:], kvall[:, n, 0:2, :], ALU.add),
                            [curF, kvall], [curF])
            for n in range(N - 1, -1, -1):
                self.op("pool", lambda e, n=n: e.tensor_copy(stB[:, n, :, :], curB[:]), [curB], [stB])
                if n > 0:
                    self.op("pool", lambda e: e.tensor_tensor(curB[:], curB[:], DEC[:, 2:4, :], ALU.mult), [curB, DEC], [curB])
                    self.op("pool", lambda e, n=n: e.tensor_tensor(curB[:], curB[:], kvall[:, n, 2:4, :], ALU.add),
                            [curB, kvall], [curB])
            if getattr(self, 'cut', 0) == 3:
                return
            def load2(n):
                self.dma("sp", qk[n % 2][:], zs[t0 + n * 128:t0 + (n + 1) * 128, 0:1024], [], [qk[n % 2]])
            load2(0)
            for n in range(N):
                if n + 1 < N:
                    load2(n + 1)
                s = n % 2
                QK, QKT, QF, QB, STT, OT = qk[s], qkT[s], qfT[s], qbT[s], stt[s], ot[s]
                pT = self.ps_bf(4 * s)
                pstE, pstO = self.bank[4 * s + 1], self.bank[4 * s + 2]
                po = self.bank[4 * s + 3]
                self.transposes(pT, [QK[:, j * 128:(j + 1) * 128] for j in range(4)], QK)
                self.op("act", lambda e: e.activation(out=QKT[:], in_=pT[:, 0:4, :], func=AF.Copy), [pT], [QKT])
                self.op("dve", lambda e: e.tensor_tensor(QF[:], pT[:, 0:2, :], QFB[:, 0, :, :], ALU.mult), [pT, QFB], [QF])
                self.op("pool", lambda e: e.tensor_tensor(QB[:], QKT[:, 0:2, :], QFB[:, 1, :, :], ALU.mult), [QKT, QFB], [QB])
                if self.cut == 4:
                    continue
                for h in range(4):
                    pr, b0 = h // 2, (h % 2) * 64
                    pst = pstE if h % 2 == 0 else pstO
                    self.op("pe", lambda e, h=h, pr=pr, b0=b0, pst=pst: e.matmul(
                        pst[:, pr * 128:(pr + 1) * 128], QKT[b0:b0 + 64, 2 + pr, :], QKT[b0:b0 + 64, pr, :],
                        start=True, stop=True), [QKT], [pst], inc=(h >= 2))
                for par, pst in enumerate((pstE, pstO)):
                    self.op("dve", lambda e, par=par, pst=pst: e.tensor_tensor(
                        STT[:, par::2, :], pst[:, 0:256].rearrange("p (a b) -> p a b", a=2), DT[:, par::2, :], ALU.mult),
                        [pst, DT], [STT])
                if self.cut == 5:
                    continue
                for h in range(4):
                    pr, b0 = h // 2, (h % 2) * 64
                    oc = po[:, h * 64:(h + 1) * 64]
                    self.op("pe", lambda e, h=h, oc=oc: e.matmul(oc, STT[:, h, :], QK[:, 512 + h * 64:512 + (h + 1) * 64],
                                                                 start=True, stop=False), [STT, QK], [po], inc=False)
                    self.op("pe", lambda e, pr=pr, b0=b0, oc=oc: e.matmul(oc, QF[b0:b0 + 64, pr, :], stF[b0:b0 + 64, n, pr, :],
                                                                         start=False, stop=False), [QF, stF], [po], inc=False)
                    self.op("pe", lambda e, pr=pr, b0=b0, oc=oc: e.matmul(oc, QB[b0:b0 + 64, pr, :], stB[b0:b0 + 64, n, pr, :],
                                                                         start=False, stop=True), [QB, stB], [po], inc=True)
                if self.cut == 6:
                    continue
                for h in range(4):
                    self.op("act", lambda e, h=h: e.activation(out=junk[:], in_=po[:, h * 64:(h + 1) * 64], func=AF.Square,
                                                              accum_out=ssr[:, h:h + 1]), [po], [junk, ssr])
                self.rstd_cols(ssr, rr, sst, 0, 4, 64)
                self.op("dve", lambda e: e.tensor_tensor(o1[:].rearrange("p (h d) -> p h d", h=4),
                                                         po[:, 0:256].rearrange("p (h d) -> p h d", h=4),
                                                         bcast_last(rr[:], 64), ALU.mult), [po, rr], [o1])
                self.op("pool", lambda e: e.tensor_tensor(OT[:], o1[:], QK[:, 768:1024], ALU.mult), [o1, QK], [OT])
                self.dma("sp", osx[t0 + n * 128:t0 + (n + 1) * 128, 0:256], OT[:], [OT], [])

    def fourier(self, l, w_fmix, c64_in, dft_in):
        zs, osx = self.zs, self.osx
        wg = self.sb([64, 4, 64], BF16, "f_wg")
        self.dma("pool", wg[:], w_fmix[l].rearrange("g c e -> c g e"), [], [wg])
        c64 = self.sb([64, 2, 128], BF16, "f_c64")
        self.dma("pool", c64[:], c64_in[:, :, :], [], [c64])
        M12 = self.sb([128, 2, 2, 256], BF16, "f_M12")
        self.op("dve", lambda e: e.memset(M12[:], 0.0), [], [M12])
        pm = self.bank[0]
        for cs in range(2):
            for g in range(4):
                c0 = cs * 256 + g * 64
                self.op("pe", lambda e, cs=cs, g=g, c0=c0: e.matmul(pm[:, c0:c0 + 64], c64[:, cs, :], wg[:, g, :],
                                                                   start=True, stop=True), [c64, wg], [pm],
                        inc=(cs == 1 and g == 3))
        for cs in range(2):
            for g in range(4):
                cc, hf = g // 2, g % 2
                c0 = cs * 256 + g * 64
                self.op("act", lambda e, cs=cs, g=g, cc=cc, hf=hf, c0=c0: e.activation(
                    out=M12[hf * 64:(hf + 1) * 64, cs, cc, g * 64:(g + 1) * 64], in_=pm[hf * 64:(hf + 1) * 64, c0:c0 + 64],
                    func=AF.Copy), [pm], [M12])
        SMAX = max(self.seqs)
        groups = []
        for q in range(self.nseq):
            if groups and len(groups[-1]) < 2 and self.seqs[groups[-1][0]] == self.seqs[q] and self.seqs[q] <= 2048:
                groups[-1].append(q)
            else:
                groups.append([q])
        uall = [self.sb([128, SMAX // 128, 256], BF16, "f_uall%d" % i) for i in range(2)]
        PT = [self.sb([128, 2, 2, SMAX if i == 0 else min(SMAX, 2048)], BF16, "f_PT%d" % i) for i in range(2)]
        fo = self.sb([128, SMAX // 128, 256], BF16, "f_fo")
        dtile = [self.sb([128, 512], BF16, "f_dt%d" % i) for i in range(3)]
        ndt = 0
        nev = 0
        for grp in groups:
            S = self.seqs[grp[0]]
            NT = S // 128
            f = 4096 // S
            nrm = 1.0 / np.sqrt(S * 64.0)
            for gi, q in enumerate(grp):
                src = AP(zs.tensor, self.t0[q] * ZW + Z_FU, [[ZW, 128], [128 * ZW, NT], [1, 256]])
                self.dma("sp", uall[gi][:, 0:NT, :], src, [], [uall[gi]])
            it = 0
            for kb in range(S // 512):
                for cs in range(2):
                    bs = (it % 2) * 4
                    it += 1
                    for sc in range(NT):
                        dtb = dtile[ndt % 3]
                        ndt += 1
                        src = AP(dft_in.tensor, cs * 4096 * 4096 + sc * 128 * f * 4096 + kb * 512, [[f * 4096, 128], [1, 512]])
                        self.dma("sp", dtb[:], src, [], [dtb])
                        for gi, q in enumerate(grp):
                            for cc in range(2):
                                bk = self.bank[bs + gi * 2 + cc]
                                self.op("pe", lambda e, gi=gi, cc=cc, bk=bk, sc=sc, dtb=dtb: e.matmul(
                                    bk[:], uall[gi][:, sc, cc * 128:(cc + 1) * 128], dtb[:], start=(sc == 0), stop=(sc == NT - 1)),
                                    [uall[gi], dtb], [bk], inc=(sc == NT - 1 or (gi == len(grp) - 1 and cc == 1)))
                    for gi, q in enumerate(grp):
                        for cc in range(2):
                            bk = self.bank[bs + gi * 2 + cc]
                            dst = PT[gi][:, cs, cc, kb * 512:(kb + 1) * 512]
                            if nev % 2 == 0:
                                self.op("act", lambda e, bk=bk, dst=dst: e.activation(out=dst, in_=bk[:], func=AF.Copy, scale=nrm),
                                        [bk], [PT[gi]])
                            else:
                                self.op("dve", lambda e, bk=bk, dst=dst: e.tensor_scalar(dst, bk[:], nrm, None, ALU.mult),
                                        [bk], [PT[gi]])
                            nev += 1
            for gi, q in enumerate(grp):
                for kb2 in range(NT):
                    bk = self.bank[kb2 % 2]
                    i4 = 0
                    for cs in range(2):
                        for cc in range(2):
                            self.op("pe", lambda e, cs=cs, cc=cc, bk=bk, i4=i4: e.matmul(
                                bk[:, 0:256], PT[gi][:, cs, cc, kb2 * 128:(kb2 + 1) * 128], M12[:, cs, cc, :],
                                start=(i4 == 0), stop=(i4 == 3)), [PT[gi], M12], [bk], inc=(i4 == 3))
                            i4 += 1
                    if kb2 % 2 == 0:
                        self.op("act", lambda e, bk=bk: e.activation(out=fo[:, kb2, :], in_=bk[:, 0:256], func=AF.Copy), [bk], [fo])
                    else:
                        self.op("dve", lambda e, bk=bk: e.tensor_copy(fo[:, kb2, :], bk[:, 0:256]), [bk], [fo])
                dst = AP(osx.tensor, self.t0[q] * D + 256, [[D, 128], [128 * D, NT], [1, 256]])
                self.dma("sp", dst, fo[:, 0:NT, :], [fo], [])

    def dilated(self, l, dmask_in):
        zs, dsc, dstt = self.zs, self.dsc, self.dstt
        masks = self.sb([128, 4, 256], F32, "d_masks")
        self.dma("sp", masks[:], dmask_in.rearrange("v p k -> p v k"), [], [masks])
        kt = [self.sb([128, 256], BF16, "d_kt%d" % i) for i in range(3)]
        vt = [self.sb([128, 256], BF16, "d_vt%d" % i) for i in range(3)]
        kT = [self.sb([128, 2, 128], BF16, "d_kT%d" % i) for i in range(3)]
        for i in range(3):
            self.op("dve", lambda e, i=i: e.memset(kt[i][:], 0.0), [], [kt[i]])
            self.op("pool", lambda e, i=i: e.memset(vt[i][:], 0.0), [], [vt[i]])
        qt = [self.sb([128, 256], BF16, "d_qt%d" % i) for i in range(2)]
        qT = [self.sb([128, 2, 128], BF16, "d_qT%d" % i) for i in range(2)]
        sm = self.sb([128, 4, 256], F32, "d_sm")
        mx = self.sb([128, 4], F32, "d_mx")
        nmx = self.sb([128, 4], F32, "d_nmx")
        pp = [self.sb([128, 4, 256], BF16, "d_p%d" % i) for i in range(2)]
        pTs = [self.sb([128, 8, 128], BF16, "d_pTs%d" % i) for i in range(2)]
        stat = [self.sb([128, 8], F32, "d_stat%d" % i) for i in range(2)]
        dn = [self.sb([128, 256], BF16, "d_dn%d" % i) for i in range(2)]
        pTq = self.ps_bf(7)
        nb_glob = 0
        for q in range(self.nseq):
            S = self.seqs[q]
            for gi, d in enumerate(DILS):
                Lq = S // d
                NB = Lq // 128
                for r in range(d):
                    row = lambda t: self.t0[q] + r + d * t

                    def load_key(m):
                        sl = m % 3
                        ta, tb = max(0, 128 * m - 64), min(Lq, 128 * m + 64)
                        p0 = ta - (128 * m - 64)
                        p1 = p0 + (tb - ta)
                        for (buf, zc) in ((kt[sl], Z_DK), (vt[sl], Z_DV)):
                            src = AP(zs.tensor, row(ta) * ZW + zc + gi * 256, [[d * ZW, tb - ta], [1, 256]])
                            self.dma("sp", buf[p0:p1, :], src, [], [buf])
                        self.transposes(pTq, [kt[sl][:, j * 128:(j + 1) * 128] for j in range(2)], kt[sl])
                        self.op("act", lambda e: e.activation(out=kT[sl][:], in_=pTq[:, 0:2, :], func=AF.Copy), [pTq], [kT[sl]])

                    def load_q(b):
                        src = AP(zs.tensor, row(128 * b) * ZW + Z_DQ + gi * 256, [[d * ZW, 128], [1, 256]])
                        self.dma("sp", qt[b % 2][:], src, [], [qt[b % 2]])

                    load_q(0)
                    load_key(0)
                    load_key(1)
                    for b in range(NB):
                        if b + 1 < NB:
                            load_q(b + 1)
                        s2 = nb_glob % 2
                        nb_glob += 1
                        QT_, P_, PTS, ST, DN = qT[b % 2], pp[s2], pTs[s2], stat[s2], dn[s2]
                        self.transposes(pTq, [qt[b % 2][:, j * 128:(j + 1) * 128] for j in range(2)], qt[b % 2])
                        self.op("dve", lambda e: e.tensor_copy(QT_[:], pTq[:, 0:2, :]), [pTq], [QT_])
                        bkX, bkY = self.bank[2 * s2], self.bank[2 * s2 + 1]
                        for h in range(4):
                            pr, b0 = h // 2, (h % 2) * 64
                            bk = bkX if h % 2 == 0 else bkY
                            for mm in range(2):
                                sl = (b + mm) % 3
                                c0 = pr * 256 + mm * 128
                                self.op("pe", lambda e, bk=bk, c0=c0, pr=pr, b0=b0, sl=sl: e.matmul(
                                    bk[:, c0:c0 + 128], QT_[b0:b0 + 64, pr, :], kT[sl][b0:b0 + 64, pr, :], start=True, stop=True),
                                    [QT_, kT[sl]], [bk], inc=(h >= 2 and mm == 1))
                        var = (1 if b == 0 else 0) + (2 if b == NB - 1 else 0)
                        for bi, bk in enumerate((bkX, bkY)):
                            self.op("dve", lambda e, bi=bi, bk=bk: e.tensor_tensor(
                                sm[:, bi::2, :], bk[:].rearrange("p (a b) -> p a b", a=2),
                                bcast_mid(masks[:, var, :], 2), ALU.add), [bk, masks], [sm])
                        self.op("dve", lambda e: e.tensor_reduce(mx[:], sm[:], AX.X, ALU.max), [sm], [mx])
                        self.op("dve", lambda e: e.tensor_scalar(nmx[:], mx[:], -DIL_SCALE, None, ALU.mult), [mx], [nmx])
                        self.op("pool", lambda e: e.tensor_scalar(ST[:, 0:4], mx[:], DIL_SCALE, None, ALU.mult), [mx], [ST])
                        for h in range(4):
                            self.op("act", lambda e, h=h: e.activation(out=P_[:, h, :], in_=sm[:, h, :], func=AF.Exp,
                                                                      bias=nmx[:, h:h + 1], scale=DIL_SCALE,
                                                                      accum_out=ST[:, 4 + h:5 + h]), [sm, nmx], [P_, ST])
                        pTp = self.ps_bf(4 + s2)
                        self.transposes(pTp, [P_[:, h, mm * 128:(mm + 1) * 128] for h in range(4) for mm in range(2)], P_)
                        if s2 == 0:
                            self.op("act", lambda e: e.activation(out=PTS[:], in_=pTp[:], func=AF.Copy), [pTp], [PTS])
                        else:
                            self.op("dve", lambda e: e.tensor_copy(PTS[:], pTp[:]), [pTp], [PTS])
                        if b + 2 <= NB:
                            load_key(b + 2)
                        po = self.bank[6]
                        for h in range(4):
                            for mm in range(2):
                                sl = (b + mm) % 3
                                self.op("pe", lambda e, h=h, mm=mm, sl=sl: e.matmul(
                                    po[:, h * 64:(h + 1) * 64], PTS[:, h * 2 + mm, :], vt[sl][:, h * 64:(h + 1) * 64],
                                    start=(mm == 0), stop=(mm == 1)), [PTS, vt[sl]], [po], inc=(h == 3 and mm == 1))
                        self.op("act", lambda e: e.activation(out=DN[:], in_=po[:, 0:256], func=AF.Copy), [po], [DN])
                        r0 = row(128 * b)
                        self.dma("sp", AP(dsc.tensor, r0 * 768 + gi * 256, [[d * 768, 128], [1, 256]]), DN[:], [DN], [])
                        self.dma("sp", AP(dstt.tensor, r0 * 24 + gi * 8, [[d * 24, 128], [1, 8]]), ST[:], [ST], [])

    def mla(self, l):
        zs, osx = self.zs, self.osx
        SMAX = max(self.seqs)
        KT = self.sb([128, 4, SMAX], BF16, "m_KT")
        QT = self.sb([128, 4, SMAX], BF16, "m_QT")
        VP = self.sb([128, SMAX // 128, 4, 65], BF16, "m_VP")
        OUT = self.sb([128, SMAX // 128, 256], BF16, "m_OUT")
        kst = [self.sb([128, 4, 98], BF16, "m_kst%d" % i) for i in range(2)]
        qst = [self.sb([128, 4, 98], BF16, "m_qst%d" % i) for i in range(2)]
        vst = [self.sb([128, 4, 64], BF16, "m_vst%d" % i) for i in range(2)]
        sqk = self.sb([128, 4, 96], F32, "m_sqk")
        sqq = self.sb([128, 4, 96], F32, "m_sqq")
        ssk = self.sb([128, 4], F32, "m_ssk")
        ssq = self.sb([128, 4], F32, "m_ssq")
        wk = self.sb([128, 4], F32, "m_wk")
        rden = self.sb([128, 4], F32, "m_rden")
        PT = [self.sb([128, 512], BF16, "m_PT%d" % i) for i in range(3)]
        for i in range(2):
            self.op("dve", lambda e, i=i: e.memset(kst[i][:], 1.0), [], [kst[i]])
            self.op("pool", lambda e, i=i: e.memset(qst[i][:], 1.0), [], [qst[i]])
        col = lambda a: a.rearrange("p (h o) -> p h o", o=1)
        grp = 0
        for q in range(self.nseq):
            S = self.seqs[q]
            NT = S // 128
            t0 = self.t0[q]

            def loadb(i):
                r0 = t0 + i * 128
                K_, Q_, V_ = kst[i % 2], qst[i % 2], vst[i % 2]
                self.dma("sp", K_[:, :, 0:64], AP(zs.tensor, r0 * ZW + Z_MKV, [[ZW, 128], [128, 4], [1, 64]]), [], [K_])
                self.dma("sp", K_[:, :, 64:96], AP(zs.tensor, r0 * ZW + Z_KPE, [[ZW, 128], [0, 4], [1, 32]]), [], [K_])
                self.dma("sp", V_[:], AP(zs.tensor, r0 * ZW + Z_MKV + 64, [[ZW, 128], [128, 4], [1, 64]]), [], [V_])
                self.dma("sp", Q_[:, :, 0:96], AP(zs.tensor, r0 * ZW + Z_MQ, [[ZW, 128], [96, 4], [1, 96]]), [], [Q_])

            loadb(0)
            for i in range(NT):
                if i + 1 < NT:
                    loadb(i + 1)
                K_, Q_, V_ = kst[i % 2], qst[i % 2], vst[i % 2]
                self.op("dve", lambda e: e.tensor_tensor(sqk[:], K_[:, :, 0:96], K_[:, :, 0:96], ALU.mult), [K_], [sqk])
                self.op("dve", lambda e: e.tensor_reduce(ssk[:], sqk[:], AX.X, ALU.add), [sqk], [ssk])
                self.op("dve", lambda e: e.tensor_scalar(K_[:, :, 97:98], col(ssk[:]), -0.5, None, ALU.mult), [ssk], [K_])
                self.op("act", lambda e: e.activation(out=col(wk[:]), in_=K_[:, :, 97:98], func=AF.Exp, scale=-MLA_SCALE), [K_], [wk])
                self.op("dve", lambda e: e.tensor_tensor(VP[:, i, :, 0:64], V_[:], bcast_last(wk[:], 64), ALU.mult), [V_, wk], [VP])
                self.op("dve", lambda e: e.tensor_copy(VP[:, i, :, 64:65], col(wk[:])), [wk], [VP])
                self.op("pool", lambda e: e.tensor_tensor(sqq[:], Q_[:, :, 0:96], Q_[:, :, 0:96], ALU.mult), [Q_], [sqq])
                self.op("dve", lambda e: e.tensor_reduce(ssq[:], sqq[:], AX.X, ALU.add), [sqq], [ssq])
                self.op("dve", lambda e: e.tensor_scalar(Q_[:, :, 96:97], col(ssq[:]), -0.5, None, ALU.mult), [ssq], [Q_])
                pT = self.ps_bf(6 + i % 2)
                srcs = [K_[:, h, :] for h in range(4)] + [Q_[:, h, :] for h in range(4)]
                for j, a in enumerate(srcs):
                    self.op("pe", lambda e, j=j, a=a: e.transpose(pT[0:98, j, :], a, self.ident[:]),
                            [K_, Q_, self.ident], [pT], inc=(j == 7))
                self.op("act", lambda e: e.activation(out=KT[0:98, :, i * 128:(i + 1) * 128], in_=pT[0:98, 0:4, :], func=AF.Copy),
                        [pT], [KT])
                self.op("dve", lambda e: e.tensor_copy(QT[0:98, :, i * 128:(i + 1) * 128], pT[0:98, 4:8, :]), [pT], [QT])
            NKC, NQG = S // 128, S // 512
            steps = [(h, qg, kc) for h in range(4) for qg in range(NQG) for kc in range(NKC)]

            def st_mm(idx):
                h, qg, kc = steps[idx]
                bk = self.bank[idx % 3]
                self.op("pe", lambda e: e.matmul(bk[:], KT[0:98, h, kc * 128:(kc + 1) * 128], QT[0:98, h, qg * 512:(qg + 1) * 512],
                                                 start=True, stop=True), [KT, QT], [bk])

            st_mm(0)
            for idx, (h, qg, kc) in enumerate(steps):
                if idx + 1 < len(steps):
                    st_mm(idx + 1)
                bk, P_ = self.bank[idx % 3], PT[idx % 3]
                if kc == 0:
                    po = self.bank[4 + grp % 2]
                    grp += 1
                    self.op("dve", lambda e, po=po: e.memset(po[:, 0:260], 0.0), [], [po])
                self.op("act", lambda e, bk=bk, P_=P_: e.activation(out=P_[:], in_=bk[:], func=AF.Exp, scale=MLA_SCALE), [bk], [P_])
                for qb in range(4):
                    self.op("pe", lambda e, qb=qb, P_=P_, po=po, h=h, kc=kc: e.matmul(
                        po[:, qb * 65:(qb + 1) * 65], P_[:, qb * 128:(qb + 1) * 128], VP[:, kc, h, :], start=False,
                        stop=(kc == NKC - 1), skip_group_check=True), [P_, VP], [po], inc=(qb == 3))
                if kc == NKC - 1:
                    po3 = po[:, 0:260].rearrange("p (a b) -> p a b", b=65)
                    self.op("dve", lambda e, po3=po3: e.reciprocal(col(rden[:]), po3[:, :, 64:65]), [po], [rden])
                    self.op("dve", lambda e, po3=po3, qg=qg, h=h: e.tensor_tensor(
                        OUT[:, qg * 4:(qg + 1) * 4, h * 64:(h + 1) * 64], po3[:, :, 0:64], bcast_last(rden[:], 64), ALU.mult),
                        [po, rden], [OUT])
            self.dma("sp", AP(osx.tensor, t0 * D + 768, [[D, 128], [128 * D, NT], [1, 256]]), OUT[:, 0:NT, :], [OUT], [])

    def phase_c1(self, l, xsrc, w_out, modd, x1s, h2s):
        osx, dsc, dstt = self.osx, self.dsc, self.dstt
        wout = self.sb([128, 8, D], BF16, "c_wout")
        for k in range(8):
            self.dma("pool", wout[:, k, :], w_out[l, k * 128:(k + 1) * 128, :], [], [wout])
        rows = self.sb([128, 3, D], F32, "c_rows")
        otl = [self.sb([128, D], BF16, "c_ot%d" % i) for i in range(2)]
        dnl = [self.sb([128, 3, 256], BF16, "c_dn%d" % i) for i in range(2)]
        stl = [self.sb([128, 3, 8], F32, "c_st%d" % i) for i in range(2)]
        xtl = [self.sb([128, D], F32, "c_xt%d" % i) for i in range(2)]
        M = self.sb([128, 4], F32, "c_M")
        ee = self.sb([128, 3, 4], F32, "c_ee")
        ww = self.sb([128, 3, 4], F32, "c_ww")
        wd = self.sb([128, 4], F32, "c_wd")
        od = self.sb([128, 3, 256], F32, "c_od")
        oT = self.sb([128, 8, 128], BF16, "c_oT")
        ss = self.sb([128, 4], F32, "c_ss")
        sst = self.sb([128, 4], F32, "c_sst")
        rs = self.sb([128, 4], F32, "c_rs")
        junk = self.sb([128, D], BF16, "c_junk")
        t1 = self.sb([128, D], F32, "c_t1")
        x1t = [self.sb([128, D], F32, "c_x1t%d" % i) for i in range(2)]
        hb = self.sb([128, D], BF16, "c_hb")
        h2T = [self.sb([128, 8, 128], BF16, "c_h2T%d" % i) for i in range(2)]
        tiles = [(q, i) for q in range(self.nseq) for i in range(self.seqs[q] // 128)]

        def load(n):
            q, i = tiles[n]
            r0 = self.t0[q] + i * 128
            s = n % 2
            self.dma("sp", otl[s][:], osx[r0:r0 + 128, :], [], [otl[s]])
            self.dma("sp", dnl[s][:], dsc[r0:r0 + 128, :, :], [], [dnl[s]])
            self.dma("sp", stl[s][:], dstt[r0:r0 + 128, :, :], [], [stl[s]])
            self.dma("sp", xtl[s][:], xsrc[r0:r0 + 128, :], [], [xtl[s]])

        load(0)
        curq = -1
        for n, (q, i) in enumerate(tiles):
            if n + 1 < len(tiles):
                load(n + 1)
            if q != curq:
                self.load_rows(rows, modd, l, q, (2, 3, 4))
                curq = q
            s = n % 2
            OT, DN, ST, X, X1, H2T = otl[s], dnl[s], stl[s], xtl[s], x1t[s], h2T[s]
            r0 = self.t0[q] + i * 128
            self.op("dve", lambda e: e.tensor_tensor(M[:], ST[:, 0, 0:4], ST[:, 1, 0:4], ALU.max), [ST], [M])
            self.op("dve", lambda e: e.tensor_tensor(M[:], M[:], ST[:, 2, 0:4], ALU.max), [ST, M], [M])
            self.op("dve", lambda e: e.tensor_tensor(ee[:], ST[:, :, 0:4], bcast_mid(M[:], 3), ALU.subtract), [ST, M], [ee])
            self.op("act", lambda e: e.activation(out=ee[:], in_=ee[:], func=AF.Exp), [ee], [ee])
            self.op("dve", lambda e: e.tensor_tensor(ww[:], ee[:], ST[:, :, 4:8], ALU.mult), [ee, ST], [ww])
            self.op("dve", lambda e: e.tensor_tensor(wd[:], ww[:, 0, :], ww[:, 1, :], ALU.add), [ww], [wd])
            self.op("dve", lambda e: e.tensor_tensor(wd[:], wd[:], ww[:, 2, :], ALU.add), [ww, wd], [wd])
            self.op("dve", lambda e: e.reciprocal(wd[:], wd[:]), [wd], [wd])
            self.op("dve", lambda e: e.tensor_tensor(ee[:], ee[:], bcast_mid(wd[:], 3), ALU.mult), [ee, wd], [ee])
            for g in range(3):
                self.op("pool", lambda e, g=g: e.tensor_tensor(od[:, g, :].rearrange("p (h d) -> p h d", h=4),
                                                               DN[:, g, :].rearrange("p (h d) -> p h d", h=4),
                                                               bcast_last(ee[:, g, :], 64), ALU.mult), [DN, ee], [od])
            self.op("pool", lambda e: e.tensor_tensor(od[:, 0, :], od[:, 0, :], od[:, 1, :], ALU.add), [od], [od])
            self.op("pool", lambda e: e.tensor_tensor(OT[:, 512:768], od[:, 0, :], od[:, 2, :], ALU.add), [od], [OT])
            pT = self.ps_bf(0)
            self.transposes(pT, [OT[:, k * 128:(k + 1) * 128] for k in range(8)], OT)
            self.op("act", lambda e: e.activation(out=oT[:], in_=pT[:], func=AF.Copy), [pT], [oT])
            py = (self.bank[1], self.bank[2])
            for nb in range(2):
                for k in range(8):
                    self.op("pe", lambda e, nb=nb, k=k: e.matmul(py[nb][:], oT[:, k, :], wout[:, k, nb * 512:(nb + 1) * 512],
                                                                start=(k == 0), stop=(k == 7)), [oT, wout], [py[nb]], inc=(k == 7))
            for nb in range(2):
                self.op("act", lambda e, nb=nb: e.activation(out=junk[:, nb * 512:(nb + 1) * 512], in_=py[nb][:], func=AF.Square,
                                                            accum_out=ss[:, nb:nb + 1]), [py[nb]], [junk, ss])
            self.op("dve", lambda e: e.tensor_tensor(ss[:, 2:3], ss[:, 0:1], ss[:, 1:2], ALU.add), [ss], [ss])
            self.rstd_cols(ss, rs, sst, 2, 3, D)
            for nb in range(2):
                self.op("dve", lambda e, nb=nb: e.scalar_tensor_tensor(t1[:, nb * 512:(nb + 1) * 512], py[nb][:], rs[:, 2:3],
                                                                      rows[:, 0, nb * 512:(nb + 1) * 512], ALU.mult, ALU.mult),
                        [py[nb], rs, rows], [t1])
            self.op("pool", lambda e: e.tensor_tensor(X1[:], t1[:], X[:], ALU.add), [t1, X], [X1])
            self.dma("sp", x1s[r0:r0 + 128, :], X1[:], [X1], [])
            self.op("act", lambda e: e.activation(out=junk[:], in_=X1[:], func=AF.Square, accum_out=ss[:, 3:4]), [X1], [junk, ss])
            self.rstd_cols(ss, rs, sst, 3, 4, D)
            self.op("dve", lambda e: e.scalar_tensor_tensor(t1[:], X1[:], rs[:, 3:4], rows[:, 1, :], ALU.mult, ALU.mult),
                    [X1, rs, rows], [t1])
            self.op("pool", lambda e: e.tensor_tensor(hb[:], t1[:], rows[:, 2, :], ALU.add), [t1, rows], [hb])
            pT2 = self.ps_bf(3)
            self.transposes(pT2, [hb[:, k * 128:(k + 1) * 128] for k in range(8)], hb)
            self.op("act", lambda e: e.activation(out=H2T[:], in_=pT2[:], func=AF.Copy), [pT2], [H2T])
            col0 = self.t0[q] + 2 * q + 1 + i * 128
            self.dma("sp", AP(h2s.tensor, col0, [[self.Tp, 128], [128 * self.Tp, 8], [1, 128]]), H2T[:], [H2T], [])

    def phase_c2a(self, l, w_up, conv_w, conv_b, h2s, vts):
        wup = self.sb([128, 8, 2 * DFF], BF16, "u_wup")
        for k in range(8):
            self.dma("pool", wup[:, k, :], w_up[l, k * 128:(k + 1) * 128, :], [], [wup])
        cw = self.sb([128, 44, 4], F32, "u_cw")
        for j in range(4):
            base = (l * 3 + j) * 2 * DFF if j < 3 else l * 2 * DFF
            tns = conv_w.tensor if j < 3 else conv_b.tensor
            self.dma("sp", cw[:, :, j:j + 1], AP(tns, base, [[1, 128], [128, 44], [1, 1]]), [], [cw], slow=True)
        h2w = [self.sb([128, 8, 512], BF16, "u_h2w%d" % i) for i in range(2)]
        ta = [self.sb([128, 512], F32, "u_ta%d" % i) for i in range(2)]
        tb = [self.sb([128, 512], F32, "u_tb%d" % i) for i in range(2)]
        sa = [self.sb([128, 512], F32, "u_sa%d" % i) for i in range(2)]
        vt = [self.sb([128, 22, 512], BF16, "u_vt%d" % i) for i in range(2)]
        wins = []
        for q in range(self.nseq):
            w0 = 0
            while w0 < self.seqs[q]:
                n = min(WIN, self.seqs[q] - w0)
                wins.append((q, w0, n))
                w0 += n

        def load(wi):
            q, w0, n = wins[wi]
            cb = self.t0[q] + 2 * q + w0
            self.dma("sp", h2w[wi % 2][:, :, 0:n + 2], AP(h2s.tensor, cb, [[self.Tp, 128], [128 * self.Tp, 8], [1, n + 2]]),
                     [], [h2w[wi % 2]])

        load(0)
        it = 0
        for wi, (q, w0, n) in enumerate(wins):
            if wi + 1 < len(wins):
                load(wi + 1)
            H, VT = h2w[wi % 2], vt[wi % 2]
            for j in range(22):
                s = it % 2
                it += 1
                bA, bB = self.bank[2 * s], self.bank[2 * s + 1]
                TA, TB, SA = ta[s], tb[s], sa[s]
                for (bk, ch) in ((bA, j * 128), (bB, DFF + j * 128)):
                    for k in range(8):
                        self.op("pe", lambda e, bk=bk, ch=ch, k=k: e.matmul(bk[:, 0:n + 2], wup[:, k, ch:ch + 128], H[:, k, 0:n + 2],
                                                                           start=(k == 0), stop=(k == 7)), [wup, H], [bk], inc=(k == 7))
                for (bk, T_, c) in ((bA, TA, j), (bB, TB, 22 + j)):
                    self.op("act", lambda e, bk=bk, T_=T_, c=c: e.activation(out=T_[:, 0:n], in_=bk[:, 1:n + 1], func=AF.Identity,
                                                                            bias=cw[:, c, 3:4], scale=cw[:, c, 1:2]), [bk, cw], [T_])
                    self.op("dve", lambda e, bk=bk, T_=T_, c=c: e.scalar_tensor_tensor(T_[:, 0:n], bk[:, 0:n], cw[:, c, 0:1], T_[:, 0:n],
                                                                                      ALU.mult, ALU.add), [bk, cw, T_], [T_])
                    self.op("dve", lambda e, bk=bk, T_=T_, c=c: e.scalar_tensor_tensor(T_[:, 0:n], bk[:, 2:n + 2], cw[:, c, 2:3], T_[:, 0:n],
                                                                                      ALU.mult, ALU.add), [bk, cw, T_], [T_])
                self.op("act", lambda e: e.activation(out=SA[:, 0:n], in_=TA[:, 0:n], func=AF.Silu), [TA], [SA])
                self.op("pool", lambda e, j=j: e.tensor_tensor(VT[:, j, 0:n], SA[:, 0:n], TB[:, 0:n], ALU.mult), [SA, TB], [VT])
            self.dma("sp", AP(vts.tensor, self.t0[q] + w0, [[self.T, 128], [128 * self.T, 22], [1, n]]), VT[:, :, 0:n], [VT], [])

    def phase_c2b(self, l, w_down, modd, x1s, vts, xdst):
        wd = self.sb([128, 22, D], BF16, "w_wd")
        for j in range(22):
            self.dma("pool", wd[:, j, :], w_down[l, j * 128:(j + 1) * 128, :], [], [wd])
        rows = self.sb([128, 1, D], F32, "w_rows")
        vtl = [self.sb([128, 22, 128], BF16, "w_vt%d" % i) for i in range(2)]
        x1l = [self.sb([128, D], F32, "w_x1%d" % i) for i in range(2)]
        x2l = [self.sb([128, D], F32, "w_x2%d" % i) for i in range(2)]
        t1 = self.sb([128, D], F32, "w_t1")
        junk = self.sb([128, D], BF16, "w_junk")
        ss = self.sb([128, 4], F32, "w_ss")
        sst = self.sb([128, 4], F32, "w_sst")
        rs = self.sb([128, 4], F32, "w_rs")
        tiles = [(q, i) for q in range(self.nseq) for i in range(self.seqs[q] // 128)]

        def load(n):
            q, i = tiles[n]
            r0 = self.t0[q] + i * 128
            self.dma("sp", vtl[n % 2][:], AP(vts.tensor, r0, [[self.T, 128], [128 * self.T, 22], [1, 128]]), [], [vtl[n % 2]])
            self.dma("sp", x1l[n % 2][:], x1s[r0:r0 + 128, :], [], [x1l[n % 2]])

        load(0)
        curq = -1
        for n, (q, i) in enumerate(tiles):
            if n + 1 < len(tiles):
                load(n + 1)
            if q != curq:
                self.load_rows(rows, modd, l, q, (5,))
                curq = q
            s = n % 2
            VT, X1, X2 = vtl[s], x1l[s], x2l[s]
            r0 = self.t0[q] + i * 128
            py = (self.bank[2 * s], self.bank[2 * s + 1])
            for nb in range(2):
                for j in range(22):
                    self.op("pe", lambda e, nb=nb, j=j: e.matmul(py[nb][:], VT[:, j, :], wd[:, j, nb * 512:(nb + 1) * 512],
                                                                start=(j == 0), stop=(j == 21)), [VT, wd], [py[nb]], inc=(j == 21))
            for nb in range(2):
                self.op("act", lambda e, nb=nb: e.activation(out=junk[:, nb * 512:(nb + 1) * 512], in_=py[nb][:], func=AF.Square,
                                                            accum_out=ss[:, nb:nb + 1]), [py[nb]], [junk, ss])
            self.op("dve", lambda e: e.tensor_tensor(ss[:, 2:3], ss[:, 0:1], ss[:, 1:2], ALU.add), [ss], [ss])
            self.rstd_cols(ss, rs, sst, 2, 3, D)
            for nb in range(2):
                self.op("dve", lambda e, nb=nb: e.scalar_tensor_tensor(t1[:, nb * 512:(nb + 1) * 512], py[nb][:], rs[:, 2:3],
                                                                      rows[:, 0, nb * 512:(nb + 1) * 512], ALU.mult, ALU.mult),
                        [py[nb], rs, rows], [t1])
            self.op("pool", lambda e: e.tensor_tensor(X2[:], t1[:], X1[:], ALU.add), [t1, X1], [X2])
            self.dma("sp", xdst[r0:r0 + 128, :], X2[:], [X2], [])


class _View:
    def __init__(self, ap, res):
        self.ap = ap
        self.res = res

    def __getitem__(self, k):
        return self.ap[k]


_CONST = {}


def _consts():
    if _CONST:
        return _CONST
    f32 = np.float32
    pos = np.arange(4096, dtype=f32)

    def tab(theta, rot):
        half = rot // 2
        inv = np.power(f32(theta), -np.arange(half, dtype=f32) * f32(2.0) / f32(rot)).astype(f32)
        ang = (pos[:, None] * inv[None, :]).astype(f32)
        return np.cos(ang).astype(f32), np.sin(ang).astype(f32)

    rope = np.zeros((4096, RTW), f32)
    c, s = tab(10000.0, 64)
    sc = np.array([1.0] * 4 + [0.125] * 4, f32)
    rope[:, 0:256] = (c[:, None, :] * sc[None, :, None]).reshape(4096, 256)
    rope[:, 256:512] = (s[:, None, :] * sc[None, :, None]).reshape(4096, 256)
    c, s = tab(500000.0, 16)
    rope[:, 512:704] = np.tile(c, (1, 24))
    rope[:, 704:896] = np.tile(s, (1, 24))
    c, s = tab(500000.0, 32)
    rope[:, 896:960] = np.tile(c, (1, 4))
    rope[:, 960:976] = c
    rope[:, 976:1040] = np.tile(s, (1, 4))
    rope[:, 1040:1056] = s
    _CONST["c_rope"] = rope
    _CONST["c_ident"] = np.eye(128, dtype=f32).astype(NPBF)
    a = np.arange(4096, dtype=np.int64)
    m = (a[:, None] * a[None, :]) % 4096
    ang = (2.0 * np.pi / 4096.0) * np.arange(4096, dtype=np.float64)
    ct, st = np.cos(ang).astype(f32).astype(NPBF), np.sin(ang).astype(f32).astype(NPBF)
    _CONST["c_dft"] = np.stack([ct[m], st[m]], 0)
    k = np.arange(64)
    a64 = 2.0 * np.pi * ((k[:, None] * k[None, :]) % 64) / 64.0
    c64, s64 = np.cos(a64).astype(f32), np.sin(a64).astype(f32)
    cc = np.zeros((64, 2, 128), f32)
    cc[:, 0, :] = np.concatenate([c64, c64], 1)
    cc[:, 1, :] = np.concatenate([-s64, -s64], 1)
    _CONST["c_c64"] = cc
    j = np.arange(128, dtype=f32)[:, None]
    cidx = np.arange(128, dtype=f32)[None, :]
    ret = np.zeros((128, 4, 128), f32)
    ret[:, 0, :] = np.maximum(cidx - j, 0)
    ret[:, 1, :] = np.maximum(j - cidx, 0)
    ret[:, 2, :] = (cidx >= j)
    ret[:, 3, :] = (j > cidx)
    _CONST["c_ret"] = ret
    rq = np.zeros((128, 2, 128), f32)
    rq[:, 0, :] = cidx + 1.0
    rq[:, 1, :] = 128.0 - cidx
    _CONST["c_retq"] = rq
    rw = np.zeros((128, 8), f32)
    rw[:, 0:4] = 127.0 - j
    rw[:, 4:8] = j
    _CONST["c_retw"] = rw
    i = np.arange(128)[:, None]
    jj = np.arange(256)[None, :]
    band = (jj - i >= 0) & (jj - i <= 128)
    dm = np.zeros((4, 128, 256), f32)
    for v in range(4):
        ok = band.copy()
        if v & 1:
            ok &= (jj >= 64)
        if v & 2:
            ok &= (jj < 192)
        dm[v] = np.where(ok, 0.0, -1e30)
    _CONST["c_dmask"] = dm
    return _CONST


_WNAMES = ("w_ada", "b_ada", "norm_pre_mix", "w_in", "ret_decay_fwd", "ret_decay_bwd", "w_fmix", "mla_q_norm", "mla_w_qb",
           "mla_kv_norm", "mla_w_kvb", "w_out", "norm_post_mix", "norm_pre_ffn", "w_up", "conv_w", "conv_b", "w_down",
           "norm_post_ffn")


def run_cores(seq_lists_x, seq_lists_c, weights, seqs, n_layers=2, debug=False, trace=False, stop=1000):
    kb = KB(seqs, n_layers=n_layers, debug=debug)
    kb.stop = stop
    nc = kb.build()
    cst = _consts()
    in_maps = []
    for xs_, cs_ in zip(seq_lists_x, seq_lists_c):
        m = {"x": np.ascontiguousarray(np.concatenate(xs_, 0), dtype=np.float32),
             "cT": np.ascontiguousarray(np.stack(cs_, 1), dtype=np.float32)}
        for k in _WNAMES:
            m[k] = np.ascontiguousarray(weights[k][:n_layers], dtype=np.float32)
        m.update(cst)
        in_maps.append(m)
    res = run_bass_kernel_spmd(nc, in_maps, core_ids=list(range(len(in_maps))), trace=trace)
    return res, kb


def kernel(x_prompt, x_sample, c_prompt, c_sample, **weights):
    x_prompt = np.asarray(x_prompt, np.float32)
    x_sample = np.asarray(x_sample, np.float32)
    c_prompt = np.asarray(c_prompt, np.float32)
    c_sample = np.asarray(c_sample, np.float32)
    weights = {k: np.asarray(v, np.float32) for k, v in weights.items()}
    seqs = [4096, 2048, 2048, 2048, 2048]
    xs_, cs_ = [], []
    for i in range(8):
        xs_.append([x_prompt[i % 4]] + [x_sample[4 * i + j] for j in range(4)])
        cs_.append([c_prompt[i % 4]] + [c_sample[4 * i + j] for j in range(4)])
    res, kb = run_cores(xs_, cs_, weights, seqs)
    y_prompt = np.stack([res.results[i]["y"][0:4096] for i in range(4)], 0)
    y_sample = np.stack([res.results[i]["y"][4096 + 2048 * j:4096 + 2048 * (j + 1)] for i in range(8) for j in range(4)], 0)
    return (np.ascontiguousarray(y_prompt, dtype=np.float32), np.ascontiguousarray(y_sample, dtype=np.float32))
```

```python
import contextlib
import numpy as np
import ml_dtypes
import concourse.bass as bass
import concourse.mybir as mybir
from concourse.bass_utils import run_bass_kernel_spmd
from concourse.ap import AP

F32 = mybir.dt.float32
BF16 = mybir.dt.bfloat16
AF = mybir.ActivationFunctionType
ALU = mybir.AluOpType
AX = mybir.AxisListType
NPBF = ml_dtypes.bfloat16

D = 1024
DIN = 4000
DFF = 2816
ZW = 4512
Z_RQ, Z_RK, Z_RV, Z_RG, Z_FU, Z_DQ, Z_DK, Z_DV = 0, 256, 512, 768, 1024, 1280, 2048, 2816
Z_MQ, Z_MKV, Z_KPE = 3584, 3968, 4480
EPS = 1e-6
RTW = 512 + 384 + 160
DILS = (1, 4, 16)
MLA_SCALE = 96.0 ** -0.5
DIL_SCALE = 64.0 ** -0.5
WIN = 510


class Res:
    __slots__ = ("w", "r", "psum")

    def __init__(self):
        self.w = None
        self.r = []
        self.psum = False


class Buf:
    def __init__(self, h):
        self.h = h
        self.res = Res()

    def __getitem__(self, k):
        return self.h[k]


def bcast_last(ap, n):
    return AP(ap.tensor, ap.offset, [list(a) for a in ap.ap] + [[0, n]])


def bcast_mid(ap, n):
    l = [list(a) for a in ap.ap]
    return AP(ap.tensor, ap.offset, [l[0], [0, n]] + l[1:])


class KB:
    def __init__(self, seqs, n_layers=2, debug=False):
        self.seqs = list(seqs)
        self.nseq = len(seqs)
        self.T = sum(seqs)
        self.L = n_layers
        self.debug = debug
        self.t0 = [sum(seqs[:i]) for i in range(self.nseq)]
        self.Tp = self.T + 2 * self.nseq
        nc = bass.Bass("TRN2", target_bir_lowering=False)
        self.nc = nc
        self.eng = {"pe": nc.tensor, "act": nc.scalar, "dve": nc.vector, "pool": nc.gpsimd, "sp": nc.sync}
        self.sem = {k: nc.alloc_semaphore("s_" + k) for k in ("pe", "act", "dve", "pool")}
        self.cnt = {k: 0 for k in self.sem}
        self.waited = {k: {} for k in self.eng}
        self.ND = 24
        self.dsem = [nc.alloc_semaphore("d%d" % i) for i in range(self.ND)]
        self.dcnt = [0] * self.ND
        self.drr = 0
        self.uid = 0
        self.dram = {}
        self.stack = contextlib.ExitStack()
        self.cut = 0
        self.stop = 1000

    def _wait(self, e, deps):
        best = {}
        for key, val in deps:
            if key == "pe" and e == "pe":
                continue
            if best.get(key, 0) < val:
                best[key] = val
        w = self.waited[e]
        for key, val in best.items():
            if w.get(key, 0) >= val:
                continue
            sem = self.sem[key] if isinstance(key, str) else self.dsem[key[1]]
            self.eng[e].wait_ge(sem, val)
            w[key] = val

    def _deps(self, reads, writes, e=None):
        deps = []
        for r in reads:
            if r.res.w is not None:
                deps.append(r.res.w)
        for w in writes:
            if w.res.w is not None and w.res.w[0] != e:
                deps.append(w.res.w)
            deps.extend(t for t in w.res.r if t[0] != e)
        return deps

    def op(self, e, fn, reads=(), writes=(), inc=True):
        pr = [r for r in reads if r.res.psum]
        deps = []
        if pr:
            deps = [r.res.w for r in pr if r.res.w is not None]
            reads = [r for r in reads if not r.res.psum]
            writes = list(writes) + pr
        self._wait(e, deps + self._deps(reads, writes, e))
        ins = fn(self.eng[e])
        if inc:
            self.cnt[e] += 1
            ins.then_inc(self.sem[e], 1)
            tok = (e, self.cnt[e])
        else:
            tok = (e, self.cnt[e] + 1)
        for r in reads:
            r.res.r.append(tok)
        for w in writes:
            w.res.w = tok
            w.res.r = []
        return ins

    def dma(self, q, out, in_, reads=(), writes=(), slow=False):
        self._wait(q, self._deps(reads, writes, q))
        k = self.drr
        self.drr = (self.drr + 1) % self.ND
        if self.dcnt[k] > 0:
            self._wait(q, [(("d", k), 16 * self.dcnt[k])])
        self.dcnt[k] += 1
        self.eng[q].dma_start(out=out, in_=in_, allow_slow_non_contiguous=slow).then_inc(self.dsem[k], 16)
        tok = (("d", k), 16 * self.dcnt[k])
        for r in reads:
            r.res.r.append(tok)
        for w in writes:
            w.res.w = tok
            w.res.r = []

    def barrier(self):
        toks = [(k, self.cnt[k]) for k in self.cnt if self.cnt[k] > 0]
        toks += [(("d", k), 16 * self.dcnt[k]) for k in range(self.ND) if self.dcnt[k] > 0]
        for e in self.eng:
            self._wait(e, [t for t in toks if t[0] != e])

    def interleave(self, gens, W):
        active = []
        it = iter(gens)
        done = False
        while True:
            while not done and len(active) < W:
                g = next(it, None)
                if g is None:
                    done = True
                    break
                active.append(g)
            if not active:
                break
            for g in list(active):
                try:
                    next(g)
                except StopIteration:
                    active.remove(g)

    def sb(self, shape, dt, name=None):
        self.uid += 1
        nm = "%s_%d" % (name or "sb", self.uid)
        return Buf(self.stack.enter_context(self.nc.sbuf_tensor(nm, list(shape), dt)))

    def begin_phase(self):
        self.gstack = self.stack
        self.stack = contextlib.ExitStack()

    def end_phase(self):
        self.barrier()
        self.stack.close()
        self.stack = self.gstack

    def ps(self, shape, dt, name=None):
        self.uid += 1
        b = Buf(self.nc.alloc_psum_tensor(name or ("ps%d" % self.uid), list(shape), dt))
        b.res.psum = True
        return b

    def din(self, name, shape, dt):
        t = self.nc.dram_tensor(name, list(shape), dt, kind="ExternalInput").ap()
        self.dram[name] = t
        return t

    def dscr(self, name, shape, dt):
        kind = "ExternalOutput" if self.debug else "Internal"
        t = self.nc.dram_tensor(name, list(shape), dt, kind=kind).ap()
        self.dram[name] = t
        return t

    def rstd(self, ss, out, n, tmp):
        self.op("act", lambda e: e.activation(out=tmp[:], in_=ss[:], func=AF.Sqrt, bias=self.epsb[:, 0:1],
                                              scale=1.0 / n), [ss, self.epsb], [tmp])
        self.op("dve", lambda e: e.reciprocal(out[:], tmp[:]), [tmp], [out])

    def transposes(self, pt, src_aps, srcbuf, width=128):
        n = len(src_aps)
        for i, a in enumerate(src_aps):
            self.op("pe", lambda e, i=i, a=a: e.transpose(pt[0:width, i, :], a, self.ident[:]),
                    [srcbuf, self.ident], [pt], inc=(i == n - 1))

    def build(self):
        nc = self.nc
        L, T, NS = self.L, self.T, self.nseq
        x_in = self.din("x", [T, D], F32)
        cT_in = self.din("cT", [D, NS], F32)
        w_ada = self.din("w_ada", [L, D, 6 * D], F32)
        b_ada = self.din("b_ada", [L, 6 * D], F32)
        n_pre_mix = self.din("norm_pre_mix", [L, D], F32)
        w_in = self.din("w_in", [L, D, DIN], F32)
        dec_f = self.din("ret_decay_fwd", [L, 4], F32)
        dec_b = self.din("ret_decay_bwd", [L, 4], F32)
        w_fmix = self.din("w_fmix", [L, 4, 64, 64], F32)
        q_norm = self.din("mla_q_norm", [L, 256], F32)
        w_qb = self.din("mla_w_qb", [L, 256, 384], F32)
        kv_norm = self.din("mla_kv_norm", [L, 128], F32)
        w_kvb = self.din("mla_w_kvb", [L, 128, 512], F32)
        w_out = self.din("w_out", [L, D, D], F32)
        n_post_mix = self.din("norm_post_mix", [L, D], F32)
        n_pre_ffn = self.din("norm_pre_ffn", [L, D], F32)
        w_up = self.din("w_up", [L, D, 2 * DFF], F32)
        conv_w = self.din("conv_w", [L, 3, 2 * DFF], F32)
        conv_b = self.din("conv_b", [L, 2 * DFF], F32)
        w_down = self.din("w_down", [L, DFF, D], F32)
        n_post_ffn = self.din("norm_post_ffn", [L, D], F32)
        ident_in = self.din("c_ident", [128, 128], BF16)
        rope_in = self.din("c_rope", [4096, RTW], F32)
        dft_in = self.din("c_dft", [2, 4096, 4096], BF16)
        c64_in = self.din("c_c64", [64, 2, 128], F32)
        ret_in = self.din("c_ret", [128, 4, 128], F32)
        retq_in = self.din("c_retq", [128, 2, 128], F32)
        retw_in = self.din("c_retw", [128, 8], F32)
        dmask_in = self.din("c_dmask", [4, 2, 128, 512], BF16)
        y_out = self.nc.dram_tensor("y", [T, D], F32, kind="ExternalOutput").ap()
        zs = self.dscr("zs", [T, ZW], BF16)
        osx = self.dscr("os", [T, D], BF16)
        dsc = self.dscr("dsc", [T, 3, 256], BF16)
        dstt = self.dscr("dst", [T, 3, 8], F32)
        xs = self.dscr("xs", [T, D], F32)
        x1s = self.dscr("x1s", [T, D], F32)
        h2s = self.dscr("h2s", [D, self.Tp], BF16)
        vts = self.dscr("vts", [DFF, T], BF16)
        modd = self.dscr("modd", [L, NS, 6 * D], F32)
        self.zs, self.osx, self.dsc, self.dstt = zs, osx, dsc, dstt

        self.ident = self.sb([128, 128], BF16, "ident")
        self.dma("sp", self.ident[:], ident_in[:, :], [], [self.ident])
        self.epsb = self.sb([128, 1], F32, "epsb")
        self.op("dve", lambda e: e.memset(self.epsb[:], EPS), [], [self.epsb])
        self.zero = self.sb([128, 512], BF16, "zero")
        self.op("dve", lambda e: e.memset(self.zero[:], 0.0), [], [self.zero])
        self.bank = [self.ps([128, 512], F32, "bank%d" % i) for i in range(8)]

        self._zero_halo(h2s)

        self.nph = 0

        def run(fn, *a):
            self.nph += 1
            if self.nph > getattr(self, "stop", 1000):
                return
            self.begin_phase()
            fn(*a)
            self.end_phase()

        run(self.prephase, cT_in, w_ada, b_ada, n_pre_mix, n_post_mix, n_pre_ffn, n_post_ffn, modd)
        for l in range(L):
            xsrc = x_in if l == 0 else xs
            xdst = y_out if l == L - 1 else xs
            run(self.phase_a, l, xsrc, w_in, q_norm, w_qb, kv_norm, w_kvb, rope_in, modd)
            run(self.retention, l, dec_f, dec_b, ret_in, retq_in, retw_in)
            run(self.fourier, l, w_fmix, c64_in, dft_in)
            run(self.dilated, l, dmask_in)
            run(self.mla, l)
            run(self.phase_c1, l, xsrc, w_out, modd, x1s, h2s)
            run(self.phase_c2a, l, w_up, conv_w, conv_b, h2s, vts)
            run(self.phase_c2b, l, w_down, modd, x1s, vts, xdst)
        self.barrier()
        return nc

    def _zero_halo(self, h2s):
        for q in range(self.nseq):
            for col in (self.t0[q] + 2 * q, self.t0[q] + 2 * q + self.seqs[q] + 1):
                dst = AP(h2s.tensor, col, [[self.Tp, 128], [128 * self.Tp, 8], [1, 1]])
                src = self.zero[:, 0:8].rearrange("p (a b) -> p a b", b=1)
                self.dma("sp", dst, src, [self.zero], [], slow=True)

    def prephase(self, cT_in, w_ada, b_ada, n_pre_mix, n_post_mix, n_pre_ffn, n_post_ffn, modd):
        NS, L = self.nseq, self.L
        cT = self.sb([128, 8, NS], F32, "cT")
        self.dma("sp", cT[:], cT_in.rearrange("(k p) u -> p k u", p=128), [], [cT], slow=True)
        sg = self.sb([128, 8, NS], F32, "sg")
        cTb = self.sb([128, 8, NS], BF16, "cTb")
        self.op("act", lambda e: e.activation(out=sg[:], in_=cT[:], func=AF.Sigmoid), [cT], [sg])
        self.op("dve", lambda e: e.tensor_tensor(cTb[:], cT[:], sg[:], ALU.mult), [cT, sg], [cTb])
        wa = [self.sb([128, 8, 512], BF16, "wa%d" % i) for i in range(2)]
        mod = self.sb([NS, 6 * D], F32, "mod")
        brow = self.sb([NS, 6 * D], F32, "brow")
        nrm = self.sb([NS, 4, D], F32, "nrm")
        out6 = self.sb([NS, 6, D], F32, "out6")
        for l in range(L):
            self.dma("sp", brow[:], AP(b_ada.tensor, l * 6 * D, [[0, NS], [1, 6 * D]]), [], [brow])
            for i, nt in enumerate((n_pre_mix, n_post_mix, n_pre_ffn, n_post_ffn)):
                self.dma("sp", nrm[:, i, :], AP(nt.tensor, l * D, [[0, NS], [1, D]]), [], [nrm])
            for nb in range(12):
                w = wa[nb % 2]
                self.dma("pool", w[:], w_ada[l, :, nb * 512:(nb + 1) * 512].rearrange("(k p) n -> p k n", p=128),
                         [], [w])
                pb = self.bank[nb % 2]
                for k in range(8):
                    self.op("pe", lambda e, k=k, w=w, pb=pb: e.matmul(pb[0:NS, :], cTb[:, k, :], w[:, k, :],
                                                                     start=(k == 0), stop=(k == 7)),
                            [cTb, w], [pb], inc=(k == 7))
                self.op("dve", lambda e, pb=pb, nb=nb: e.tensor_tensor(mod[:, nb * 512:(nb + 1) * 512], pb[0:NS, :],
                                                                     brow[:, nb * 512:(nb + 1) * 512], ALU.add),
                        [pb, brow], [mod])
            m = lambda i: mod[:, i * D:(i + 1) * D]
            self.op("dve", lambda e: e.scalar_tensor_tensor(out6[:, 0, :], m(1), 1.0, nrm[:, 0, :], ALU.add, ALU.mult),
                    [mod, nrm], [out6])
            self.op("dve", lambda e: e.tensor_copy(out6[:, 1, :], m(0)), [mod], [out6])
            self.op("dve", lambda e: e.tensor_tensor(out6[:, 2, :], m(2), nrm[:, 1, :], ALU.mult), [mod, nrm], [out6])
            self.op("dve", lambda e: e.scalar_tensor_tensor(out6[:, 3, :], m(4), 1.0, nrm[:, 2, :], ALU.add, ALU.mult),
                    [mod, nrm], [out6])
            self.op("dve", lambda e: e.tensor_copy(out6[:, 4, :], m(3)), [mod], [out6])
            self.op("dve", lambda e: e.tensor_tensor(out6[:, 5, :], m(5), nrm[:, 3, :], ALU.mult), [mod, nrm], [out6])
            self.dma("sp", modd[l, :, :], out6[:].rearrange("u a d -> u (a d)"), [out6], [])

    def load_rows(self, buf, modd, l, q, idxs):
        for i, ix in enumerate(idxs):
            src = AP(modd.tensor, (l * self.nseq + q) * 6 * D + ix * D, [[0, 128], [1, D]])
            self.dma("sp", buf[:, i, :], src, [], [buf])

    def phase_a(self, l, xsrc, w_in, q_norm, w_qb, kv_norm, w_kvb, rope_in, modd):
        zs = self.zs
        win = self.sb([128, 8, DIN], BF16, "win")
        for k in range(8):
            self.dma("pool", win[:, k, :], w_in[l, k * 128:(k + 1) * 128, :], [], [win])
        wqb = self.sb([128, 2, 384], BF16, "wqb")
        self.dma("pool", wqb[:], w_qb[l].rearrange("(k p) n -> p k n", p=128), [], [wqb])
        wkvb = self.sb([128, 512], BF16, "wkvb")
        self.dma("pool", wkvb[:], w_kvb[l], [], [wkvb])
        qn = self.sb([128, 256], F32, "qn")
        self.dma("sp", qn[:], AP(q_norm.tensor, l * 256, [[0, 128], [1, 256]]), [], [qn])
        kvn = self.sb([128, 128], F32, "kvn")
        self.dma("sp", kvn[:], AP(kv_norm.tensor, l * 128, [[0, 128], [1, 128]]), [], [kvn])
        rows = self.sb([128, 2, D], F32, "arows")
        xt = [self.sb([128, D], F32, "a_xt%d" % i) for i in range(2)]
        rt = [self.sb([128, RTW], F32, "a_rt%d" % i) for i in range(2)]
        junk = self.sb([128, D], BF16, "a_junk")
        ss = self.sb([128, 4], F32, "a_ss")
        sst = self.sb([128, 4], F32, "a_sst")
        rs = self.sb([128, 4], F32, "a_rs")
        hf = self.sb([128, D], F32, "a_hf")
        hb = self.sb([128, D], BF16, "a_hb")
        hT = [self.sb([128, 8, 128], BF16, "a_hT%d" % i) for i in range(2)]
        zt = [self.sb([128, ZW], BF16, "a_zt%d" % i) for i in range(2)]
        cn = self.sb([128, 384], BF16, "a_cn")
        cnT = self.sb([128, 3, 128], BF16, "a_cnT")
        tmp = [self.sb([128, 256], F32, "a_tmp%d" % i) for i in range(4)]
        pT = self.ps_bf(0)
        pT2 = self.ps_bf(1)
        pz = [self.bank[2], self.bank[3], self.bank[4], self.bank[5]]
        pq = self.bank[6]
        pkv = self.bank[7]

        tiles = [(q, i) for q in range(self.nseq) for i in range(self.seqs[q] // 128)]

        def load_x(n):
            q, i = tiles[n]
            r0 = self.t0[q] + i * 128
            self.dma("sp", xt[n % 2][:], xsrc[r0:r0 + 128, :], [], [xt[n % 2]])

        def load_rt(n):
            q, i = tiles[n]
            self.dma("sp", rt[n % 2][:], rope_in[i * 128:(i + 1) * 128, :], [], [rt[n % 2]])

        def rope(e, buf, x1, x2, cos, sin, w):
            shp = None
            t1, t2, t3, t4 = [t[:, 0:w].rearrange("p (h d) -> p h d", h=x1.shape[1]) for t in tmp]
            self.op(e, lambda g: g.tensor_tensor(t1, x1, cos, ALU.mult), [buf] + rtb, [tmp[0]])
            self.op(e, lambda g: g.tensor_tensor(t2, x2, sin, ALU.mult), [buf] + rtb, [tmp[1]])
            self.op(e, lambda g: g.tensor_tensor(t3, x2, cos, ALU.mult), [buf] + rtb, [tmp[2]])
            self.op(e, lambda g: g.tensor_tensor(t4, x1, sin, ALU.mult), [buf] + rtb, [tmp[3]])
            self.op(e, lambda g: g.tensor_tensor(x1, t1, t2, ALU.subtract), [tmp[0], tmp[1]], [buf])
            self.op(e, lambda g: g.tensor_tensor(x2, t3, t4, ALU.add), [tmp[2], tmp[3]], [buf])

        ss1 = self.sb([128, 4], F32, "a_ss1")
        sst1 = self.sb([128, 4], F32, "a_sst1")
        rs1 = self.sb([128, 4], F32, "a_rs1")
        state = {"q": -1}

        def stage1(n):
            q, i = tiles[n]
            if q != state["q"]:
                self.load_rows(rows, modd, l, q, (0, 1))
                state["q"] = q
            X, HT = xt[n % 2], hT[n % 2]
            self.op("act", lambda e: e.activation(out=junk[:], in_=X[:], func=AF.Square, accum_out=ss1[:, 0:1]),
                    [X], [junk, ss1])
            self.rstd_cols(ss1, rs1, sst1, 0, 1, D)
            self.op("dve", lambda e: e.scalar_tensor_tensor(hf[:], X[:], rs1[:, 0:1], rows[:, 0, :], ALU.mult, ALU.mult),
                    [X, rs1, rows], [hf])
            self.op("pool", lambda e: e.tensor_tensor(hb[:], hf[:], rows[:, 1, :], ALU.add), [hf, rows], [hb])

        def stage1b(n):
            HT = hT[n % 2]
            self.transposes(pT, [hb[:, k * 128:(k + 1) * 128] for k in range(8)], hb)
            self.op("act", lambda e: e.activation(out=HT[:], in_=pT[:], func=AF.Copy), [pT], [HT])

        def stage2(n, hook):
            q, i = tiles[n]
            R, Z, HT = rt[n % 2], zt[n % 2], hT[n % 2]
            rtb.clear()
            rtb.append(R)
            r0 = self.t0[q] + i * 128
            for nb in range(8):
                c0 = nb * 512
                cw = min(512, DIN - c0)
                pb = pz[nb % 4]
                for k in range(8):
                    self.op("pe", lambda e, k=k, pb=pb, c0=c0, cw=cw: e.matmul(
                        pb[:, 0:cw], HT[:, k, :], win[:, k, c0:c0 + cw], start=(k == 0), stop=(k == 7)),
                        [HT, win], [pb], inc=(k == 7))
                if nb == 1:
                    self.op("act", lambda e, pb=pb: e.activation(out=Z[:, 512:768], in_=pb[:, 0:256], func=AF.Copy),
                            [pb], [Z])
                    self.op("act", lambda e, pb=pb: e.activation(out=Z[:, 768:1024], in_=pb[:, 256:512], func=AF.Silu),
                            [pb], [Z])
                elif nb < 7:
                    self.op("act", lambda e, pb=pb, c0=c0: e.activation(out=Z[:, c0:c0 + 512], in_=pb[:], func=AF.Copy),
                            [pb], [Z])
                    if nb == 3:
                        hook()
                else:
                    self.op("act", lambda e, pb=pb: e.activation(out=junk[:, 0:256], in_=pb[:, 0:256], func=AF.Square,
                                                                accum_out=ss[:, 1:2]), [pb], [junk, ss])
                    self.op("act", lambda e, pb=pb: e.activation(out=junk[:, 256:384], in_=pb[:, 256:384],
                                                                func=AF.Square, accum_out=ss[:, 2:3]), [pb], [junk, ss])
                    self.op("act", lambda e, pb=pb: e.activation(out=Z[:, Z_KPE:Z_KPE + 32], in_=pb[:, 384:416],
                                                                func=AF.Copy), [pb], [Z])
                    self.rstd_cols(ss, rs, sst, 1, 2, 256)
                    self.rstd_cols(ss, rs, sst, 2, 3, 128)
                    self.op("dve", lambda e, pb=pb: e.scalar_tensor_tensor(cn[:, 0:256], pb[:, 0:256], rs[:, 1:2], qn[:],
                                                                          ALU.mult, ALU.mult), [pb, rs, qn], [cn])
                    self.op("dve", lambda e, pb=pb: e.scalar_tensor_tensor(cn[:, 256:384], pb[:, 256:384], rs[:, 2:3],
                                                                          kvn[:], ALU.mult, ALU.mult), [pb, rs, kvn], [cn])
                    self.transposes(pT2, [cn[:, k * 128:(k + 1) * 128] for k in range(3)], cn)
                    self.op("act", lambda e: e.activation(out=cnT[:], in_=pT2[:, 0:3, :], func=AF.Copy), [pT2], [cnT])
                    for k in range(2):
                        self.op("pe", lambda e, k=k: e.matmul(pq[:, 0:384], cnT[:, k, :], wqb[:, k, :], start=(k == 0),
                                                             stop=(k == 1)), [cnT, wqb], [pq], inc=(k == 1))
                    self.op("pe", lambda e: e.matmul(pkv[:, 0:512], cnT[:, 2, :], wkvb[:], start=True, stop=True),
                            [cnT, wkvb], [pkv])
                    self.op("act", lambda e: e.activation(out=Z[:, Z_MQ:Z_MQ + 384], in_=pq[:, 0:384], func=AF.Copy),
                            [pq], [Z])
                    self.op("act", lambda e: e.activation(out=Z[:, Z_MKV:Z_MKV + 512], in_=pkv[:], func=AF.Copy),
                            [pkv], [Z])
            v = Z[:, 0:512].rearrange("p (h d) -> p h d", h=8)
            c = R[:, 0:256].rearrange("p (h d) -> p h d", h=8)
            s = R[:, 256:512].rearrange("p (h d) -> p h d", h=8)
            rope("pool", Z, v[:, :, 0:32], v[:, :, 32:64], c, s, 256)
            v = Z[:, Z_DQ:Z_DQ + 1536].rearrange("p (h d) -> p h d", h=24)
            c = R[:, 512:704].rearrange("p (h d) -> p h d", h=24)
            s = R[:, 704:896].rearrange("p (h d) -> p h d", h=24)
            rope("dve", Z, v[:, :, 0:8], v[:, :, 8:16], c, s, 192)
            v = Z[:, Z_MQ:Z_MQ + 384].rearrange("p (h d) -> p h d", h=4)
            c = R[:, 896:960].rearrange("p (h d) -> p h d", h=4)
            s = R[:, 976:1040].rearrange("p (h d) -> p h d", h=4)
            rope("dve", Z, v[:, :, 64:80], v[:, :, 80:96], c, s, 64)
            v = Z[:, Z_KPE:Z_KPE + 32].rearrange("p (h d) -> p h d", h=1)
            c = R[:, 960:976].rearrange("p (h d) -> p h d", h=1)
            s = R[:, 1040:1056].rearrange("p (h d) -> p h d", h=1)
            rope("dve", Z, v[:, :, 0:16], v[:, :, 16:32], c, s, 16)
            self.dma("sp", zs[r0:r0 + 128, :], Z[:], [Z], [])


        rtb = []
        NTL = len(tiles)
        load_x(0)
        if NTL > 1:
            load_x(1)
        load_rt(0)
        stage1(0)
        stage1b(0)
        for n in range(NTL):
            if n + 1 < NTL:
                load_rt(n + 1)
                stage1(n + 1)
            if n + 2 < NTL:
                load_x(n + 2)
            stage2(n, (lambda n=n: stage1b(n + 1)) if n + 1 < NTL else (lambda: None))

    def ps_bf(self, i):
        return _View(self.bank[i].h[:].bitcast(BF16).rearrange("p (a b) -> p a b", b=128), self.bank[i].res)

    def rstd_cols(self, ss, rs, tmp, a, b, n):
        self.op("act", lambda e: e.activation(out=tmp[:, a:b], in_=ss[:, a:b], func=AF.Sqrt, bias=self.epsb[:, 0:1],
                                              scale=1.0 / n), [ss, self.epsb], [tmp])
        self.op("dve", lambda e: e.reciprocal(rs[:, a:b], tmp[:, a:b]), [tmp], [rs])


    def retention(self, l, dec_f, dec_b, ret_in, retq_in, retw_in):
        zs, osx = self.zs, self.osx
        raw8 = self.sb([128, 8], F32, "r_raw8")
        self.dma("sp", raw8[:, 0:4], AP(dec_f.tensor, l * 4, [[0, 128], [1, 4]]), [], [raw8])
        self.dma("sp", raw8[:, 4:8], AP(dec_b.tensor, l * 4, [[0, 128], [1, 4]]), [], [raw8])
        rawp = self.sb([128, 2, 2], F32, "r_rawp")
        for half in range(2):
            for di, dt_ in enumerate((dec_f, dec_b)):
                self.dma("sp", rawp[half * 64:(half + 1) * 64, di, :],
                         AP(dt_.tensor, l * 4 + half, [[0, 64], [2, 2]]), [], [rawp], slow=True)
        lg8 = self.sb([128, 8], F32, "r_lg8")
        lgp = self.sb([128, 4], F32, "r_lgp")

        def logsig(dst, src):
            self.op("act", lambda e: e.activation(out=dst, in_=src, func=AF.Exp, scale=-1.0), [raw8, rawp], [lg8, lgp])
            self.op("dve", lambda e: e.tensor_scalar(dst, dst, 1.0, None, ALU.add), [lg8, lgp], [lg8, lgp])
            self.op("act", lambda e: e.activation(out=dst, in_=dst, func=AF.Ln), [lg8, lgp], [lg8, lgp])
            self.op("dve", lambda e: e.tensor_scalar(dst, dst, -1.0, None, ALU.mult), [lg8, lgp], [lg8, lgp])

        logsig(lg8[:], raw8[:])
        logsig(lgp[:], rawp[:].rearrange("p a b -> p (a b)"))
        rc = self.sb([128, 4, 128], F32, "r_rc")
        self.dma("sp", rc[:], ret_in[:, :, :], [], [rc])
        rq = self.sb([128, 2, 128], F32, "r_rq")
        self.dma("sp", rq[:], retq_in[:, :, :], [], [rq])
        rw = self.sb([128, 8], F32, "r_rw")
        self.dma("sp", rw[:], retw_in[:, :], [], [rw])
        DT = self.sb([128, 4, 128], F32, "r_DT")
        e1 = self.sb([128, 128], F32, "r_e1")
        e2 = self.sb([128, 128], F32, "r_e2")
        for h in range(4):
            self.op("act", lambda e, h=h: e.activation(out=e1[:], in_=rc[:, 0, :], func=AF.Exp, scale=lg8[:, h:h + 1]),
                    [rc, lg8], [e1])
            self.op("act", lambda e, h=h: e.activation(out=e2[:], in_=rc[:, 1, :], func=AF.Exp,
                                                      scale=lg8[:, 4 + h:5 + h]), [rc, lg8], [e2])
            self.op("dve", lambda e: e.tensor_tensor(e1[:], e1[:], rc[:, 2, :], ALU.mult), [e1, rc], [e1])
            self.op("dve", lambda e: e.tensor_tensor(e2[:], e2[:], rc[:, 3, :], ALU.mult), [e2, rc], [e2])
            self.op("dve", lambda e, h=h: e.tensor_tensor(DT[:, h, :], e1[:], e2[:], ALU.add), [e1, e2], [DT])
        QFB = self.sb([128, 2, 2, 128], F32, "r_QFB")
        for di in range(2):
            for pr in range(2):
                self.op("act", lambda e, di=di, pr=pr: e.activation(out=QFB[:, di, pr, :], in_=rq[:, di, :], func=AF.Exp,
                                                                  scale=lgp[:, di * 2 + pr:di * 2 + pr + 1]),
                        [rq, lgp], [QFB])
        WFB = self.sb([128, 8], F32, "r_WFB")
        self.op("dve", lambda e: e.tensor_tensor(WFB[:], rw[:], lg8[:], ALU.mult), [rw, lg8], [WFB])
        self.op("act", lambda e: e.activation(out=WFB[:], in_=WFB[:], func=AF.Exp), [WFB], [WFB])
        dexp = self.sb([128, 4], F32, "r_dexp")
        self.op("act", lambda e: e.activation(out=dexp[:], in_=lgp[:], func=AF.Exp, scale=128.0), [lgp], [dexp])
        DEC = self.sb([128, 4, 64], F32, "r_DEC")
        self.op("dve", lambda e: e.tensor_copy(DEC[:], bcast_last(dexp[:], 64)), [dexp], [DEC])

        if getattr(self, 'cut', 0) == 1:
            return
        NMAX = max(self.seqs) // 128
        kvall = self.sb([128, NMAX, 4, 64], F32, "r_kvall")
        stF = self.sb([128, NMAX, 2, 64], BF16, "r_stF")
        stB = self.sb([128, NMAX, 2, 64], BF16, "r_stB")
        curF = self.sb([128, 2, 64], F32, "r_curF")
        curB = self.sb([128, 2, 64], F32, "r_curB")
        kvt = [self.sb([128, 512], BF16, "r_kvt%d" % i) for i in range(2)]
        kw = [self.sb([128, 2, 256], BF16, "r_kw%d" % i) for i in range(2)]
        qk = [self.sb([128, 1024], BF16, "r_qk%d" % i) for i in range(2)]
        qkT = [self.sb([128, 4, 128], BF16, "r_qkT%d" % i) for i in range(2)]
        qfT = [self.sb([128, 2, 128], BF16, "r_qfT%d" % i) for i in range(2)]
        qbT = [self.sb([128, 2, 128], BF16, "r_qbT%d" % i) for i in range(2)]
        stt = [self.sb([128, 4, 128], BF16, "r_stt%d" % i) for i in range(2)]
        ot = [self.sb([128, 256], BF16, "r_ot%d" % i) for i in range(2)]
        o1 = self.sb([128, 256], F32, "r_o1")
        junk = self.sb([128, 64], BF16, "r_junk")
        ssr = self.sb([128, 4], F32, "r_ssr")
        sst = self.sb([128, 4], F32, "r_sst")
        rr = self.sb([128, 4], F32, "r_rr")

        for q in range(self.nseq):
            S = self.seqs[q]
            N = S // 128
            t0 = self.t0[q]
            def load1(n):
                self.dma("sp", kvt[n % 2][:], zs[t0 + n * 128:t0 + (n + 1) * 128, 256:768], [], [kvt[n % 2]])
            load1(0)
            for n in range(N):
                if n + 1 < N:
                    load1(n + 1)
                KV, KW = kvt[n % 2], kw[n % 2]
                pk = self.bank[(n % 2) * 4]
                k3 = KV[:, 0:256].rearrange("p (h d) -> p h d", h=4)
                for di in range(2):
                    self.op("dve" if di == 0 else "pool", lambda e, di=di: e.tensor_tensor(
                        KW[:, di, :].rearrange("p (h d) -> p h d", h=4), k3, bcast_last(WFB[:, di * 4:di * 4 + 4], 64),
                        ALU.mult), [KV, WFB], [KW])
                for di in range(2):
                    for pr in range(2):
                        c0 = (di * 2 + pr) * 128
                        self.op("pe", lambda e, di=di, pr=pr, c0=c0: e.matmul(
                            pk[:, c0:c0 + 128], KW[:, di, pr * 128:(pr + 1) * 128], KV[:, 256 + pr * 128:256 + (pr + 1) * 128],
                            start=True, stop=True), [KW, KV], [pk], inc=(di == 1 and pr == 1))
                pk3 = pk[:].rearrange("p (a b) -> p a b", a=4)
                self.op("act", lambda e: e.activation(out=kvall[0:64, n, :, :], in_=pk3[0:64, :, 0:64], func=AF.Copy),
                        [pk], [kvall])
                self.op("act", lambda e: e.activation(out=kvall[64:128, n, :, :], in_=pk3[64:128, :, 64:128], func=AF.Copy),
                        [pk], [kvall])
            if getattr(self, 'cut', 0) == 2:
                return
            self.op("dve", lambda e: e.memset(curF[:], 0.0), [], [curF])
            self.op("pool", lambda e: e.memset(curB[:], 0.0), [], [curB])
            for n in range(N):
                self.op("dve", lambda e, n=n: e.tensor_copy(stF[:, n, :, :], curF[:]), [curF], [stF])
                if n < N - 1:
                    self.op("dve", lambda e: e.tensor_tensor(curF[:], curF[:], DEC[:, 0:2, :], ALU.mult), [curF, DEC], [curF])
                    self.op("dve", lambda e, n=n: e.tensor_tensor(curF[:], curF[:], kvall[:, n, 0:2, :], ALU.add),
                            [curF, kvall], [curF])
            for n in range(N - 1, -1, -1):
                self.op("pool", lambda e, n=n: e.tensor_copy(stB[:, n, :, :], curB[:]), [curB], [stB])
                if n > 0:
                    self.op("pool", lambda e: e.tensor_tensor(curB[:], curB[:], DEC[:, 2:4, :], ALU.mult), [curB, DEC], [curB])
                    self.op("pool", lambda e, n=n: e.tensor_tensor(curB[:], curB[:], kvall[:, n, 2:4, :], ALU.add),
                            [curB, kvall], [curB])
            if getattr(self, 'cut', 0) == 3:
                return
            def load2(n):
                self.dma("sp", qk[n % 2][:], zs[t0 + n * 128:t0 + (n + 1) * 128, 0:1024], [], [qk[n % 2]])
            load2(0)
            for n in range(N):
                if n + 1 < N:
                    load2(n + 1)
                s = n % 2
                QK, QKT, QF, QB, STT, OT = qk[s], qkT[s], qfT[s], qbT[s], stt[s], ot[s]
                pT = self.ps_bf(4 * s)
                pstE, pstO = self.bank[4 * s + 1], self.bank[4 * s + 2]
                po = self.bank[4 * s + 3]
                self.transposes(pT, [QK[:, j * 128:(j + 1) * 128] for j in range(4)], QK)
                self.op("act", lambda e: e.activation(out=QKT[:], in_=pT[:, 0:4, :], func=AF.Copy), [pT], [QKT])
                self.op("dve", lambda e: e.tensor_tensor(QF[:], pT[:, 0:2, :], QFB[:, 0, :, :], ALU.mult), [pT, QFB], [QF])
                self.op("pool", lambda e: e.tensor_tensor(QB[:], QKT[:, 0:2, :], QFB[:, 1, :, :], ALU.mult), [QKT, QFB], [QB])
                if self.cut == 4:
                    continue
                for h in range(4):
                    pr, b0 = h // 2, (h % 2) * 64
                    pst = pstE if h % 2 == 0 else pstO
                    self.op("pe", lambda e, h=h, pr=pr, b0=b0, pst=pst: e.matmul(
                        pst[:, pr * 128:(pr + 1) * 128], QKT[b0:b0 + 64, 2 + pr, :], QKT[b0:b0 + 64, pr, :],
                        start=True, stop=True), [QKT], [pst], inc=(h >= 2))
                for par, pst in enumerate((pstE, pstO)):
                    self.op("dve", lambda e, par=par, pst=pst: e.tensor_tensor(
                        STT[:, par::2, :], pst[:, 0:256].rearrange("p (a b) -> p a b", a=2), DT[:, par::2, :], ALU.mult),
                        [pst, DT], [STT])
                if self.cut == 5:
                    continue
                for h in range(4):
                    pr, b0 = h // 2, (h % 2) * 64
                    oc = po[:, h * 64:(h + 1) * 64]
                    self.op("pe", lambda e, h=h, oc=oc: e.matmul(oc, STT[:, h, :], QK[:, 512 + h * 64:512 + (h + 1) * 64],
                                                                 start=True, stop=False), [STT, QK], [po], inc=False)
                    self.op("pe", lambda e, pr=pr, b0=b0, oc=oc: e.matmul(oc, QF[b0:b0 + 64, pr, :], stF[b0:b0 + 64, n, pr, :],
                                                                         start=False, stop=False), [QF, stF], [po], inc=False)
                    self.op("pe", lambda e, pr=pr, b0=b0, oc=oc: e.matmul(oc, QB[b0:b0 + 64, pr, :], stB[b0:b0 + 64, n, pr, :],
                                                                         start=False, stop=True), [QB, stB], [po], inc=True)
                if self.cut == 6:
                    continue
                for h in range(4):
                    self.op("act", lambda e, h=h: e.activation(out=junk[:], in_=po[:, h * 64:(h + 1) * 64], func=AF.Square,
                                                              accum_out=ssr[:, h:h + 1]), [po], [junk, ssr])
                self.rstd_cols(ssr, rr, sst, 0, 4, 64)
                self.op("dve", lambda e: e.tensor_tensor(o1[:].rearrange("p (h d) -> p h d", h=4),
                                                         po[:, 0:256].rearrange("p (h d) -> p h d", h=4),
                                                         bcast_last(rr[:], 64), ALU.mult), [po, rr], [o1])
                self.op("pool", lambda e: e.tensor_tensor(OT[:], o1[:], QK[:, 768:1024], ALU.mult), [o1, QK], [OT])
                self.dma("sp", osx[t0 + n * 128:t0 + (n + 1) * 128, 0:256], OT[:], [OT], [])

    def fourier(self, l, w_fmix, c64_in, dft_in):
        zs, osx = self.zs, self.osx
        wg = self.sb([64, 4, 64], BF16, "f_wg")
        self.dma("pool", wg[:], w_fmix[l].rearrange("g c e -> c g e"), [], [wg])
        c64 = self.sb([64, 2, 128], BF16, "f_c64")
        self.dma("pool", c64[:], c64_in[:, :, :], [], [c64])
        M12 = self.sb([128, 2, 2, 256], BF16, "f_M12")
        self.op("dve", lambda e: e.memset(M12[:], 0.0), [], [M12])
        pm = self.bank[0]
        for cs in range(2):
            for g in range(4):
                c0 = cs * 256 + g * 64
                self.op("pe", lambda e, cs=cs, g=g, c0=c0: e.matmul(pm[:, c0:c0 + 64], c64[:, cs, :], wg[:, g, :],
                                                                   start=True, stop=True), [c64, wg], [pm],
                        inc=(cs == 1 and g == 3))
        for cs in range(2):
            for g in range(4):
                cc, hf = g // 2, g % 2
                c0 = cs * 256 + g * 64
                self.op("act", lambda e, cs=cs, g=g, cc=cc, hf=hf, c0=c0: e.activation(
                    out=M12[hf * 64:(hf + 1) * 64, cs, cc, g * 64:(g + 1) * 64], in_=pm[hf * 64:(hf + 1) * 64, c0:c0 + 64],
                    func=AF.Copy), [pm], [M12])
        SMAX = max(self.seqs)
        groups = []
        for q in range(self.nseq):
            if groups and len(groups[-1]) < 2 and self.seqs[groups[-1][0]] == self.seqs[q] and self.seqs[q] <= 2048:
                groups[-1].append(q)
            else:
                groups.append([q])
        uall = [self.sb([128, SMAX // 128, 256], BF16, "f_uall%d" % i) for i in range(2)]
        PT = [self.sb([128, 2, 2, SMAX if i == 0 else min(SMAX, 2048)], BF16, "f_PT%d" % i) for i in range(2)]
        fo = self.sb([128, SMAX // 128, 256], BF16, "f_fo")
        dtile = [self.sb([128, 512], BF16, "f_dt%d" % i) for i in range(3)]
        ndt = 0
        nev = 0
        for grp in groups:
            S = self.seqs[grp[0]]
            NT = S // 128
            f = 4096 // S
            nrm = 1.0 / np.sqrt(S * 64.0)
            for gi, q in enumerate(grp):
                src = AP(zs.tensor, self.t0[q] * ZW + Z_FU, [[ZW, 128], [128 * ZW, NT], [1, 256]])
                self.dma("sp", uall[gi][:, 0:NT, :], src, [], [uall[gi]])
            it = 0
            for kb in range(S // 512):
                for cs in range(2):
                    bs = (it % 2) * 4
                    it += 1
                    for sc in range(NT):
                        dtb = dtile[ndt % 3]
                        ndt += 1
                        src = AP(dft_in.tensor, cs * 4096 * 4096 + sc * 128 * f * 4096 + kb * 512, [[f * 4096, 128], [1, 512]])
                        self.dma("sp", dtb[:], src, [], [dtb])
                        for gi, q in enumerate(grp):
                            for cc in range(2):
                                bk = self.bank[bs + gi * 2 + cc]
                                self.op("pe", lambda e, gi=gi, cc=cc, bk=bk, sc=sc, dtb=dtb: e.matmul(
                                    bk[:], uall[gi][:, sc, cc * 128:(cc + 1) * 128], dtb[:], start=(sc == 0), stop=(sc == NT - 1)),
                                    [uall[gi], dtb], [bk], inc=(sc == NT - 1 or (gi == len(grp) - 1 and cc == 1)))
                    for gi, q in enumerate(grp):
                        for cc in range(2):
                            bk = self.bank[bs + gi * 2 + cc]
                            dst = PT[gi][:, cs, cc, kb * 512:(kb + 1) * 512]
                            if nev % 2 == 0:
                                self.op("act", lambda e, bk=bk, dst=dst: e.activation(out=dst, in_=bk[:], func=AF.Copy, scale=nrm),
                                        [bk], [PT[gi]])
                            else:
                                self.op("dve", lambda e, bk=bk, dst=dst: e.tensor_scalar(dst, bk[:], nrm, None, ALU.mult),
                                        [bk], [PT[gi]])
                            nev += 1
            for gi, q in enumerate(grp):
                for kb2 in range(NT):
                    bk = self.bank[kb2 % 2]
                    i4 = 0
                    for cs in range(2):
                        for cc in range(2):
                            self.op("pe", lambda e, cs=cs, cc=cc, bk=bk, i4=i4: e.matmul(
                                bk[:, 0:256], PT[gi][:, cs, cc, kb2 * 128:(kb2 + 1) * 128], M12[:, cs, cc, :],
                                start=(i4 == 0), stop=(i4 == 3)), [PT[gi], M12], [bk], inc=(i4 == 3))
                            i4 += 1
                    if kb2 % 2 == 0:
                        self.op("act", lambda e, bk=bk: e.activation(out=fo[:, kb2, :], in_=bk[:, 0:256], func=AF.Copy), [bk], [fo])
                    else:
                        self.op("dve", lambda e, bk=bk: e.tensor_copy(fo[:, kb2, :], bk[:, 0:256]), [bk], [fo])
                dst = AP(osx.tensor, self.t0[q] * D + 256, [[D, 128], [128 * D, NT], [1, 256]])
                self.dma("sp", dst, fo[:, 0:NT, :], [fo], [])

    def dilated(self, l, dmask_in):
        zs, dsc, dstt = self.zs, self.dsc, self.dstt
        MB = self.sb([128, 4, 2, 512], BF16, "d_MB")
        self.dma("sp", MB[:], dmask_in.rearrange("v m k q -> k v m q"), [], [MB])
        NSL = 9
        W = 2
        kst = [self.sb([128, 4, 66], BF16, "d_kst%d" % i) for i in range(NSL)]
        vst = [self.sb([128, 4, 64], BF16, "d_vst%d" % i) for i in range(NSL)]
        kT = [self.sb([128, 4, 128], BF16, "d_kT%d" % i) for i in range(NSL)]
        VP = [self.sb([128, 4, 65], BF16, "d_VP%d" % i) for i in range(NSL)]
        for i in range(NSL):
            self.op("dve", lambda e, i=i: e.memset(kst[i][:], 1.0), [], [kst[i]])
            self.op("pool", lambda e, i=i: e.memset(vst[i][:], 0.0), [], [vst[i]])
        NQ = 4
        qst = [self.sb([128, 4, 66], BF16, "d_qst%d" % i) for i in range(NQ)]
        qT = [self.sb([128, 4, 128], BF16, "d_qT%d" % i) for i in range(NQ)]
        for i in range(NQ):
            self.op("pool", lambda e, i=i: e.memset(qst[i][:], 1.0), [], [qst[i]])
        sqk_ = [self.sb([128, 4, 64], F32, "d_sqk%d" % i) for i in range(NSL)]
        sqq_ = [self.sb([128, 4, 64], F32, "d_sqq%d" % i) for i in range(NQ)]
        ssk_ = [self.sb([128, 4], F32, "d_ssk%d" % i) for i in range(NSL)]
        ssq_ = [self.sb([128, 4], F32, "d_ssq%d" % i) for i in range(NQ)]
        wk_ = [self.sb([128, 4], F32, "d_wk%d" % i) for i in range(NSL)]
        PP = [[self.sb([128, 512], BF16, "d_P%d_%d" % (i, m)) for m in range(2)] for i in range(NQ)]
        stat = [self.sb([128, 8], F32, "d_stat%d" % i) for i in range(NQ)]
        dn = [self.sb([128, 256], BF16, "d_dn%d" % i) for i in range(NQ)]
        col = lambda a: a.rearrange("p (h o) -> p h o", o=1)
        pTk, pTq = self.ps_bf(6), self.ps_bf(7)

        blocks = []
        for q in range(self.nseq):
            S = self.seqs[q]
            for gi, d in enumerate(DILS):
                Lq = S // d
                NB = Lq // 128
                for r in range(d):
                    for b in range(NB):
                        blocks.append((q, gi, d, r, b, NB, Lq))
        kslot = {}
        kcount = [0]

        def row(blk, t):
            q, gi, d, r = blk[0:4]
            return self.t0[q] + r + d * t

        def key_load(blk, m):
            q, gi, d, r, b, NB, Lq = blk
            key = (q, gi, r, m)
            if key in kslot:
                return None
            sl = kcount[0] % NSL
            kcount[0] += 1
            kslot[key] = sl
            K_, V_ = kst[sl], vst[sl]
            ta, tb = max(0, 128 * m - 64), min(Lq, 128 * m + 64)
            p0 = ta - (128 * m - 64)
            p1 = p0 + (tb - ta)
            self.dma("sp", K_[p0:p1, :, 0:64], AP(zs.tensor, row(blk, ta) * ZW + Z_DK + gi * 256, [[d * ZW, tb - ta], [64, 4], [1, 64]]),
                     [], [K_])
            self.dma("sp", V_[p0:p1, :, :], AP(zs.tensor, row(blk, ta) * ZW + Z_DV + gi * 256, [[d * ZW, tb - ta], [64, 4], [1, 64]]),
                     [], [V_])
            return sl

        def key_compute(sl):
            K_, V_ = kst[sl], vst[sl]
            sqk, ssk, wk = sqk_[sl], ssk_[sl], wk_[sl]
            self.op("dve", lambda e: e.tensor_tensor(sqk[:], K_[:, :, 0:64], K_[:, :, 0:64], ALU.mult), [K_], [sqk])
            self.op("dve", lambda e: e.tensor_reduce(ssk[:], sqk[:], AX.X, ALU.add), [sqk], [ssk])
            self.op("dve", lambda e: e.tensor_scalar(K_[:, :, 65:66], col(ssk[:]), -0.5, None, ALU.mult), [ssk], [K_])
            self.op("act", lambda e: e.activation(out=col(wk[:]), in_=K_[:, :, 65:66], func=AF.Exp, scale=-DIL_SCALE), [K_], [wk])
            self.op("pool", lambda e: e.tensor_tensor(VP[sl][:, :, 0:64], V_[:], bcast_last(wk[:], 64), ALU.mult), [V_, wk], [VP[sl]])
            self.op("pool", lambda e: e.tensor_copy(VP[sl][:, :, 64:65], col(wk[:])), [wk], [VP[sl]])
            for h in range(4):
                self.op("pe", lambda e, h=h: e.transpose(pTk[0:66, h, :], K_[:, h, :], self.ident[:]), [K_, self.ident], [pTk],
                        inc=(h == 3))
            self.op("act", lambda e: e.activation(out=kT[sl][0:66, :, :], in_=pTk[0:66, 0:4, :], func=AF.Copy), [pTk], [kT[sl]])

        def gen(i):
            blk = blocks[i]
            q, gi, d, r, b, NB, Lq = blk
            s4, s2 = i % NQ, i % 2
            Q_, ST, DN, QT_ = qst[s4], stat[s4], dn[s4], qT[s4]
            sqq, ssq = sqq_[s4], ssq_[s4]
            self.dma("sp", Q_[:, :, 0:64], AP(zs.tensor, row(blk, 128 * b) * ZW + Z_DQ + gi * 256, [[d * ZW, 128], [64, 4], [1, 64]]),
                     [], [Q_])
            new = [key_load(blk, b), key_load(blk, b + 1)]
            yield
            for sl in new:
                if sl is not None:
                    key_compute(sl)
            self.op("pool", lambda e: e.tensor_tensor(sqq[:], Q_[:, :, 0:64], Q_[:, :, 0:64], ALU.mult), [Q_], [sqq])
            self.op("dve", lambda e: e.tensor_reduce(ssq[:], sqq[:], AX.X, ALU.add), [sqq], [ssq])
            self.op("dve", lambda e: e.tensor_scalar(Q_[:, :, 64:65], col(ssq[:]), -0.5, None, ALU.mult), [ssq], [Q_])
            self.op("dve", lambda e: e.tensor_scalar(col(ST[:, 0:4]), Q_[:, :, 64:65], -DIL_SCALE, None, ALU.mult), [Q_], [ST])
            for h in range(4):
                self.op("pe", lambda e, h=h: e.transpose(pTq[0:66, h, :], Q_[:, h, :], self.ident[:]), [Q_, self.ident], [pTq],
                        inc=(h == 3))
            self.op("dve", lambda e: e.tensor_copy(QT_[0:66, :, :], pTq[0:66, 0:4, :]), [pTq], [QT_])
            yield
            var = (1 if b == 0 else 0) + (2 if b == NB - 1 else 0)
            sls = [kslot[(q, gi, r, b + mm)] for mm in range(2)]
            for mm in range(2):
                sl = sls[mm]
                bk = self.bank[s2 * 2 + mm]
                self.op("pe", lambda e, bk=bk, mm=mm: e.matmul(bk[:], self.ident[:], MB[:, var, mm, :], start=True, stop=False),
                        [self.ident, MB], [bk], inc=False)
                for h in range(4):
                    self.op("pe", lambda e, bk=bk, h=h, sl=sl: e.matmul(bk[:, h * 128:(h + 1) * 128], kT[sl][0:66, h, :], QT_[0:66, h, :],
                                                                       start=False, stop=(h == 3)), [kT[sl], QT_], [bk], inc=(h == 3))
            yield
            for mm in range(2):
                bk = self.bank[s2 * 2 + mm]
                P_ = PP[s4][mm]
                self.op("act", lambda e, bk=bk, P_=P_: e.activation(out=P_[:], in_=bk[:], func=AF.Exp, scale=DIL_SCALE), [bk], [P_])
            yield
            po = self.bank[4 + s2]
            for h in range(4):
                for mm in range(2):
                    sl = sls[mm]
                    P_ = PP[s4][mm]
                    self.op("pe", lambda e, h=h, mm=mm, sl=sl, P_=P_: e.matmul(po[:, h * 65:(h + 1) * 65], P_[:, h * 128:(h + 1) * 128],
                                                                              VP[sl][:, h, :], start=(mm == 0), stop=(mm == 1)),
                            [P_, VP[sl]], [po], inc=(h == 3 and mm == 1))
            yield
            po3 = po[:, 0:260].rearrange("p (a b) -> p a b", b=65)
            self.op("act", lambda e: e.activation(out=DN[:].rearrange("p (h d) -> p h d", h=4), in_=po3[:, :, 0:64], func=AF.Copy),
                    [po], [DN])
            self.op("dve", lambda e: e.tensor_copy(col(ST[:, 4:8]), po3[:, :, 64:65]), [po], [ST])
            r0 = row(blk, 128 * b)
            self.dma("act", AP(dsc.tensor, r0 * 768 + gi * 256, [[d * 768, 128], [1, 256]]), DN[:], [DN], [])
            self.dma("pool", AP(dstt.tensor, r0 * 24 + gi * 8, [[d * 24, 128], [1, 8]]), ST[:], [ST], [])

        self.interleave((gen(i) for i in range(len(blocks))), W)

    def mla(self, l):
        zs, osx = self.zs, self.osx
        SMAX = max(self.seqs)
        KT = self.sb([128, 4, SMAX], BF16, "m_KT")
        QT = self.sb([128, 4, SMAX], BF16, "m_QT")
        VP = self.sb([128, SMAX // 128, 4, 65], BF16, "m_VP")
        OUT = self.sb([128, SMAX // 128, 256], BF16, "m_OUT")
        kst = [self.sb([128, 4, 98], BF16, "m_kst%d" % i) for i in range(2)]
        qst = [self.sb([128, 4, 98], BF16, "m_qst%d" % i) for i in range(2)]
        vst = [self.sb([128, 4, 64], BF16, "m_vst%d" % i) for i in range(2)]
        sqk = self.sb([128, 4, 96], F32, "m_sqk")
        sqq = self.sb([128, 4, 96], F32, "m_sqq")
        ssk = self.sb([128, 4], F32, "m_ssk")
        ssq = self.sb([128, 4], F32, "m_ssq")
        wk = self.sb([128, 4], F32, "m_wk")
        rden = self.sb([128, 4], F32, "m_rden")
        PT = [self.sb([128, 512], BF16, "m_PT%d" % i) for i in range(3)]
        for i in range(2):
            self.op("dve", lambda e, i=i: e.memset(kst[i][:], 1.0), [], [kst[i]])
            self.op("pool", lambda e, i=i: e.memset(qst[i][:], 1.0), [], [qst[i]])
        col = lambda a: a.rearrange("p (h o) -> p h o", o=1)
        grp = 0
        for q in range(self.nseq):
            S = self.seqs[q]
            NT = S // 128
            t0 = self.t0[q]

            def loadb(i):
                r0 = t0 + i * 128
                K_, Q_, V_ = kst[i % 2], qst[i % 2], vst[i % 2]
                self.dma("sp", K_[:, :, 0:64], AP(zs.tensor, r0 * ZW + Z_MKV, [[ZW, 128], [128, 4], [1, 64]]), [], [K_])
                self.dma("sp", K_[:, :, 64:96], AP(zs.tensor, r0 * ZW + Z_KPE, [[ZW, 128], [0, 4], [1, 32]]), [], [K_])
                self.dma("sp", V_[:], AP(zs.tensor, r0 * ZW + Z_MKV + 64, [[ZW, 128], [128, 4], [1, 64]]), [], [V_])
                self.dma("sp", Q_[:, :, 0:96], AP(zs.tensor, r0 * ZW + Z_MQ, [[ZW, 128], [96, 4], [1, 96]]), [], [Q_])

            loadb(0)
            for i in range(NT):
                if i + 1 < NT:
                    loadb(i + 1)
                K_, Q_, V_ = kst[i % 2], qst[i % 2], vst[i % 2]
                self.op("dve", lambda e: e.tensor_tensor(sqk[:], K_[:, :, 0:96], K_[:, :, 0:96], ALU.mult), [K_], [sqk])
                self.op("dve", lambda e: e.tensor_reduce(ssk[:], sqk[:], AX.X, ALU.add), [sqk], [ssk])
                self.op("dve", lambda e: e.tensor_scalar(K_[:, :, 97:98], col(ssk[:]), -0.5, None, ALU.mult), [ssk], [K_])
                self.op("act", lambda e: e.activation(out=col(wk[:]), in_=K_[:, :, 97:98], func=AF.Exp, scale=-MLA_SCALE), [K_], [wk])
                self.op("dve", lambda e: e.tensor_tensor(VP[:, i, :, 0:64], V_[:], bcast_last(wk[:], 64), ALU.mult), [V_, wk], [VP])
                self.op("dve", lambda e: e.tensor_copy(VP[:, i, :, 64:65], col(wk[:])), [wk], [VP])
                self.op("pool", lambda e: e.tensor_tensor(sqq[:], Q_[:, :, 0:96], Q_[:, :, 0:96], ALU.mult), [Q_], [sqq])
                self.op("dve", lambda e: e.tensor_reduce(ssq[:], sqq[:], AX.X, ALU.add), [sqq], [ssq])
                self.op("dve", lambda e: e.tensor_scalar(Q_[:, :, 96:97], col(ssq[:]), -0.5, None, ALU.mult), [ssq], [Q_])
                pT = self.ps_bf(6 + i % 2)
                srcs = [K_[:, h, :] for h in range(4)] + [Q_[:, h, :] for h in range(4)]
                for j, a in enumerate(srcs):
                    self.op("pe", lambda e, j=j, a=a: e.transpose(pT[0:98, j, :], a, self.ident[:]),
                            [K_, Q_, self.ident], [pT], inc=(j == 7))
                self.op("act", lambda e: e.activation(out=KT[0:98, :, i * 128:(i + 1) * 128], in_=pT[0:98, 0:4, :], func=AF.Copy),
                        [pT], [KT])
                self.op("dve", lambda e: e.tensor_copy(QT[0:98, :, i * 128:(i + 1) * 128], pT[0:98, 4:8, :]), [pT], [QT])
            NKC, NQG = S // 128, S // 512
            steps = [(h, qg, kc) for h in range(4) for qg in range(NQG) for kc in range(NKC)]

            def st_mm(idx):
                h, qg, kc = steps[idx]
                bk = self.bank[idx % 3]
                self.op("pe", lambda e: e.matmul(bk[:], KT[0:98, h, kc * 128:(kc + 1) * 128], QT[0:98, h, qg * 512:(qg + 1) * 512],
                                                 start=True, stop=True), [KT, QT], [bk])

            st_mm(0)
            for idx, (h, qg, kc) in enumerate(steps):
                if idx + 1 < len(steps):
                    st_mm(idx + 1)
                bk, P_ = self.bank[idx % 3], PT[idx % 3]
                if kc == 0:
                    po = self.bank[4 + grp % 2]
                    grp += 1
                    self.op("dve", lambda e, po=po: e.memset(po[:, 0:260], 0.0), [], [po])
                self.op("act", lambda e, bk=bk, P_=P_: e.activation(out=P_[:], in_=bk[:], func=AF.Exp, scale=MLA_SCALE), [bk], [P_])
                for qb in range(4):
                    self.op("pe", lambda e, qb=qb, P_=P_, po=po, h=h, kc=kc: e.matmul(
                        po[:, qb * 65:(qb + 1) * 65], P_[:, qb * 128:(qb + 1) * 128], VP[:, kc, h, :], start=False,
                        stop=(kc == NKC - 1), skip_group_check=True), [P_, VP], [po], inc=(qb == 3))
                if kc == NKC - 1:
                    po3 = po[:, 0:260].rearrange("p (a b) -> p a b", b=65)
                    self.op("dve", lambda e, po3=po3: e.reciprocal(col(rden[:]), po3[:, :, 64:65]), [po], [rden])
                    self.op("dve", lambda e, po3=po3, qg=qg, h=h: e.tensor_tensor(
                        OUT[:, qg * 4:(qg + 1) * 4, h * 64:(h + 1) * 64], po3[:, :, 0:64], bcast_last(rden[:], 64), ALU.mult),
                        [po, rden], [OUT])
            self.dma("sp", AP(osx.tensor, t0 * D + 768, [[D, 128], [128 * D, NT], [1, 256]]), OUT[:, 0:NT, :], [OUT], [])

    def phase_c1(self, l, xsrc, w_out, modd, x1s, h2s):
        osx, dsc, dstt = self.osx, self.dsc, self.dstt
        wout = self.sb([128, 8, D], BF16, "c_wout")
        for k in range(8):
            self.dma("pool", wout[:, k, :], w_out[l, k * 128:(k + 1) * 128, :], [], [wout])
        rows = self.sb([128, 3, D], F32, "c_rows")
        otl = [self.sb([128, D], BF16, "c_ot%d" % i) for i in range(2)]
        dnl = [self.sb([128, 3, 256], BF16, "c_dn%d" % i) for i in range(2)]
        stl = [self.sb([128, 3, 8], F32, "c_st%d" % i) for i in range(2)]
        xtl = [self.sb([128, D], F32, "c_xt%d" % i) for i in range(2)]
        M = self.sb([128, 4], F32, "c_M")
        ee = self.sb([128, 3, 4], F32, "c_ee")
        ww = self.sb([128, 3, 4], F32, "c_ww")
        wd = self.sb([128, 4], F32, "c_wd")
        od = self.sb([128, 3, 256], F32, "c_od")
        oT = self.sb([128, 8, 128], BF16, "c_oT")
        ss = self.sb([128, 4], F32, "c_ss")
        sst = self.sb([128, 4], F32, "c_sst")
        rs = self.sb([128, 4], F32, "c_rs")
        junk = self.sb([128, D], BF16, "c_junk")
        t1 = self.sb([128, D], F32, "c_t1")
        x1t = [self.sb([128, D], F32, "c_x1t%d" % i) for i in range(2)]
        hb = self.sb([128, D], BF16, "c_hb")
        h2T = [self.sb([128, 8, 128], BF16, "c_h2T%d" % i) for i in range(2)]
        tiles = [(q, i) for q in range(self.nseq) for i in range(self.seqs[q] // 128)]

        def load(n):
            q, i = tiles[n]
            r0 = self.t0[q] + i * 128
            s = n % 2
            self.dma("sp", otl[s][:], osx[r0:r0 + 128, :], [], [otl[s]])
            self.dma("sp", dnl[s][:], dsc[r0:r0 + 128, :, :], [], [dnl[s]])
            self.dma("sp", stl[s][:], dstt[r0:r0 + 128, :, :], [], [stl[s]])
            self.dma("sp", xtl[s][:], xsrc[r0:r0 + 128, :], [], [xtl[s]])

        load(0)
        curq = -1
        for n, (q, i) in enumerate(tiles):
            if n + 1 < len(tiles):
                load(n + 1)
            if q != curq:
                self.load_rows(rows, modd, l, q, (2, 3, 4))
                curq = q
            s = n % 2
            OT, DN, ST, X, X1, H2T = otl[s], dnl[s], stl[s], xtl[s], x1t[s], h2T[s]
            r0 = self.t0[q] + i * 128
            self.op("dve", lambda e: e.tensor_tensor(M[:], ST[:, 0, 0:4], ST[:, 1, 0:4], ALU.max), [ST], [M])
            self.op("dve", lambda e: e.tensor_tensor(M[:], M[:], ST[:, 2, 0:4], ALU.max), [ST, M], [M])
            self.op("dve", lambda e: e.tensor_tensor(ee[:], ST[:, :, 0:4], bcast_mid(M[:], 3), ALU.subtract), [ST, M], [ee])
            self.op("act", lambda e: e.activation(out=ee[:], in_=ee[:], func=AF.Exp), [ee], [ee])
            self.op("dve", lambda e: e.tensor_tensor(ww[:], ee[:], ST[:, :, 4:8], ALU.mult), [ee, ST], [ww])
            self.op("dve", lambda e: e.tensor_tensor(wd[:], ww[:, 0, :], ww[:, 1, :], ALU.add), [ww], [wd])
            self.op("dve", lambda e: e.tensor_tensor(wd[:], wd[:], ww[:, 2, :], ALU.add), [ww, wd], [wd])
            self.op("dve", lambda e: e.reciprocal(wd[:], wd[:]), [wd], [wd])
            self.op("dve", lambda e: e.tensor_tensor(ee[:], ee[:], bcast_mid(wd[:], 3), ALU.mult), [ee, wd], [ee])
            for g in range(3):
                self.op("pool", lambda e, g=g: e.tensor_tensor(od[:, g, :].rearrange("p (h d) -> p h d", h=4),
                                                               DN[:, g, :].rearrange("p (h d) -> p h d", h=4),
                                                               bcast_last(ee[:, g, :], 64), ALU.mult), [DN, ee], [od])
            self.op("pool", lambda e: e.tensor_tensor(od[:, 0, :], od[:, 0, :], od[:, 1, :], ALU.add), [od], [od])
            self.op("pool", lambda e: e.tensor_tensor(OT[:, 512:768], od[:, 0, :], od[:, 2, :], ALU.add), [od], [OT])
            pT = self.ps_bf(0)
            self.transposes(pT, [OT[:, k * 128:(k + 1) * 128] for k in range(8)], OT)
            self.op("act", lambda e: e.activation(out=oT[:], in_=pT[:], func=AF.Copy), [pT], [oT])
            py = (self.bank[1], self.bank[2])
            for nb in range(2):
                for k in range(8):
                    self.op("pe", lambda e, nb=nb, k=k: e.matmul(py[nb][:], oT[:, k, :], wout[:, k, nb * 512:(nb + 1) * 512],
                                                                start=(k == 0), stop=(k == 7)), [oT, wout], [py[nb]], inc=(k == 7))
            for nb in range(2):
                self.op("act", lambda e, nb=nb: e.activation(out=junk[:, nb * 512:(nb + 1) * 512], in_=py[nb][:], func=AF.Square,
                                                            accum_out=ss[:, nb:nb + 1]), [py[nb]], [junk, ss])
            self.op("dve", lambda e: e.tensor_tensor(ss[:, 2:3], ss[:, 0:1], ss[:, 1:2], ALU.add), [ss], [ss])
            self.rstd_cols(ss, rs, sst, 2, 3, D)
            for nb in range(2):
                self.op("dve", lambda e, nb=nb: e.scalar_tensor_tensor(t1[:, nb * 512:(nb + 1) * 512], py[nb][:], rs[:, 2:3],
                                                                      rows[:, 0, nb * 512:(nb + 1) * 512], ALU.mult, ALU.mult),
                        [py[nb], rs, rows], [t1])
            self.op("pool", lambda e: e.tensor_tensor(X1[:], t1[:], X[:], ALU.add), [t1, X], [X1])
            self.dma("sp", x1s[r0:r0 + 128, :], X1[:], [X1], [])
            self.op("act", lambda e: e.activation(out=junk[:], in_=X1[:], func=AF.Square, accum_out=ss[:, 3:4]), [X1], [junk, ss])
            self.rstd_cols(ss, rs, sst, 3, 4, D)
            self.op("dve", lambda e: e.scalar_tensor_tensor(t1[:], X1[:], rs[:, 3:4], rows[:, 1, :], ALU.mult, ALU.mult),
                    [X1, rs, rows], [t1])
            self.op("pool", lambda e: e.tensor_tensor(hb[:], t1[:], rows[:, 2, :], ALU.add), [t1, rows], [hb])
            pT2 = self.ps_bf(3)
            self.transposes(pT2, [hb[:, k * 128:(k + 1) * 128] for k in range(8)], hb)
            self.op("act", lambda e: e.activation(out=H2T[:], in_=pT2[:], func=AF.Copy), [pT2], [H2T])
            col0 = self.t0[q] + 2 * q + 1 + i * 128
            self.dma("sp", AP(h2s.tensor, col0, [[self.Tp, 128], [128 * self.Tp, 8], [1, 128]]), H2T[:], [H2T], [])

    def phase_c2a(self, l, w_up, conv_w, conv_b, h2s, vts):
        wup = self.sb([128, 8, 2 * DFF], BF16, "u_wup")
        for k in range(8):
            self.dma("pool", wup[:, k, :], w_up[l, k * 128:(k + 1) * 128, :], [], [wup])
        cw = self.sb([128, 44, 4], F32, "u_cw")
        for j in range(4):
            base = (l * 3 + j) * 2 * DFF if j < 3 else l * 2 * DFF
            tns = conv_w.tensor if j < 3 else conv_b.tensor
            self.dma("sp", cw[:, :, j:j + 1], AP(tns, base, [[1, 128], [128, 44], [1, 1]]), [], [cw], slow=True)
        h2w = [self.sb([128, 8, 512], BF16, "u_h2w%d" % i) for i in range(2)]
        ta = [self.sb([128, 512], F32, "u_ta%d" % i) for i in range(2)]
        tb = [self.sb([128, 512], F32, "u_tb%d" % i) for i in range(2)]
        sa = [self.sb([128, 512], F32, "u_sa%d" % i) for i in range(2)]
        vt = [self.sb([128, 22, 512], BF16, "u_vt%d" % i) for i in range(2)]
        wins = []
        for q in range(self.nseq):
            w0 = 0
            while w0 < self.seqs[q]:
                n = min(WIN, self.seqs[q] - w0)
                wins.append((q, w0, n))
                w0 += n

        def load(wi):
            q, w0, n = wins[wi]
            cb = self.t0[q] + 2 * q + w0
            self.dma("sp", h2w[wi % 2][:, :, 0:n + 2], AP(h2s.tensor, cb, [[self.Tp, 128], [128 * self.Tp, 8], [1, n + 2]]),
                     [], [h2w[wi % 2]])

        load(0)
        it = 0
        for wi, (q, w0, n) in enumerate(wins):
            if wi + 1 < len(wins):
                load(wi + 1)
            H, VT = h2w[wi % 2], vt[wi % 2]
            for j in range(22):
                s = it % 2
                it += 1
                bA, bB = self.bank[2 * s], self.bank[2 * s + 1]
                TA, TB, SA = ta[s], tb[s], sa[s]
                for (bk, ch) in ((bA, j * 128), (bB, DFF + j * 128)):
                    for k in range(8):
                        self.op("pe", lambda e, bk=bk, ch=ch, k=k: e.matmul(bk[:, 0:n + 2], wup[:, k, ch:ch + 128], H[:, k, 0:n + 2],
                                                                           start=(k == 0), stop=(k == 7)), [wup, H], [bk], inc=(k == 7))
                for (bk, T_, c) in ((bA, TA, j), (bB, TB, 22 + j)):
                    self.op("act", lambda e, bk=bk, T_=T_, c=c: e.activation(out=T_[:, 0:n], in_=bk[:, 1:n + 1], func=AF.Identity,
                                                                            bias=cw[:, c, 3:4], scale=cw[:, c, 1:2]), [bk, cw], [T_])
                    self.op("dve", lambda e, bk=bk, T_=T_, c=c: e.scalar_tensor_tensor(T_[:, 0:n], bk[:, 0:n], cw[:, c, 0:1], T_[:, 0:n],
                                                                                      ALU.mult, ALU.add), [bk, cw, T_], [T_])
                    self.op("dve", lambda e, bk=bk, T_=T_, c=c: e.scalar_tensor_tensor(T_[:, 0:n], bk[:, 2:n + 2], cw[:, c, 2:3], T_[:, 0:n],
                                                                                      ALU.mult, ALU.add), [bk, cw, T_], [T_])
                self.op("act", lambda e: e.activation(out=SA[:, 0:n], in_=TA[:, 0:n], func=AF.Silu), [TA], [SA])
                self.op("pool", lambda e, j=j: e.tensor_tensor(VT[:, j, 0:n], SA[:, 0:n], TB[:, 0:n], ALU.mult), [SA, TB], [VT])
            self.dma("sp", AP(vts.tensor, self.t0[q] + w0, [[self.T, 128], [128 * self.T, 22], [1, n]]), VT[:, :, 0:n], [VT], [])

    def phase_c2b(self, l, w_down, modd, x1s, vts, xdst):
        wd = self.sb([128, 22, D], BF16, "w_wd")
        for j in range(22):
            self.dma("pool", wd[:, j, :], w_down[l, j * 128:(j + 1) * 128, :], [], [wd])
        rows = self.sb([128, 1, D], F32, "w_rows")
        vtl = [self.sb([128, 22, 128], BF16, "w_vt%d" % i) for i in range(2)]
        x1l = [self.sb([128, D], F32, "w_x1%d" % i) for i in range(2)]
        x2l = [self.sb([128, D], F32, "w_x2%d" % i) for i in range(2)]
        t1 = self.sb([128, D], F32, "w_t1")
        junk = self.sb([128, D], BF16, "w_junk")
        ss = self.sb([128, 4], F32, "w_ss")
        sst = self.sb([128, 4], F32, "w_sst")
        rs = self.sb([128, 4], F32, "w_rs")
        tiles = [(q, i) for q in range(self.nseq) for i in range(self.seqs[q] // 128)]

        def load(n):
            q, i = tiles[n]
            r0 = self.t0[q] + i * 128
            self.dma("sp", vtl[n % 2][:], AP(vts.tensor, r0, [[self.T, 128], [128 * self.T, 22], [1, 128]]), [], [vtl[n % 2]])
            self.dma("sp", x1l[n % 2][:], x1s[r0:r0 + 128, :], [], [x1l[n % 2]])

        load(0)
        curq = -1
        for n, (q, i) in enumerate(tiles):
            if n + 1 < len(tiles):
                load(n + 1)
            if q != curq:
                self.load_rows(rows, modd, l, q, (5,))
                curq = q
            s = n % 2
            VT, X1, X2 = vtl[s], x1l[s], x2l[s]
            r0 = self.t0[q] + i * 128
            py = (self.bank[2 * s], self.bank[2 * s + 1])
            for nb in range(2):
                for j in range(22):
                    self.op("pe", lambda e, nb=nb, j=j: e.matmul(py[nb][:], VT[:, j, :], wd[:, j, nb * 512:(nb + 1) * 512],
                                                                start=(j == 0), stop=(j == 21)), [VT, wd], [py[nb]], inc=(j == 21))
            for nb in range(2):
                self.op("act", lambda e, nb=nb: e.activation(out=junk[:, nb * 512:(nb + 1) * 512], in_=py[nb][:], func=AF.Square,
                                                            accum_out=ss[:, nb:nb + 1]), [py[nb]], [junk, ss])
            self.op("dve", lambda e: e.tensor_tensor(ss[:, 2:3], ss[:, 0:1], ss[:, 1:2], ALU.add), [ss], [ss])
            self.rstd_cols(ss, rs, sst, 2, 3, D)
            for nb in range(2):
                self.op("dve", lambda e, nb=nb: e.scalar_tensor_tensor(t1[:, nb * 512:(nb + 1) * 512], py[nb][:], rs[:, 2:3],
                                                                      rows[:, 0, nb * 512:(nb + 1) * 512], ALU.mult, ALU.mult),
                        [py[nb], rs, rows], [t1])
            self.op("pool", lambda e: e.tensor_tensor(X2[:], t1[:], X1[:], ALU.add), [t1, X1], [X2])
            self.dma("sp", xdst[r0:r0 + 128, :], X2[:], [X2], [])


class _View:
    def __init__(self, ap, res):
        self.ap = ap
        self.res = res

    def __getitem__(self, k):
        return self.ap[k]


_CONST = {}


def _consts():
    if _CONST:
        return _CONST
    f32 = np.float32
    pos = np.arange(4096, dtype=f32)

    def tab(theta, rot):
        half = rot // 2
        inv = np.power(f32(theta), -np.arange(half, dtype=f32) * f32(2.0) / f32(rot)).astype(f32)
        ang = (pos[:, None] * inv[None, :]).astype(f32)
        return np.cos(ang).astype(f32), np.sin(ang).astype(f32)

    rope = np.zeros((4096, RTW), f32)
    c, s = tab(10000.0, 64)
    sc = np.array([1.0] * 4 + [0.125] * 4, f32)
    rope[:, 0:256] = (c[:, None, :] * sc[None, :, None]).reshape(4096, 256)
    rope[:, 256:512] = (s[:, None, :] * sc[None, :, None]).reshape(4096, 256)
    c, s = tab(500000.0, 16)
    rope[:, 512:704] = np.tile(c, (1, 24))
    rope[:, 704:896] = np.tile(s, (1, 24))
    c, s = tab(500000.0, 32)
    rope[:, 896:960] = np.tile(c, (1, 4))
    rope[:, 960:976] = c
    rope[:, 976:1040] = np.tile(s, (1, 4))
    rope[:, 1040:1056] = s
    _CONST["c_rope"] = rope
    _CONST["c_ident"] = np.eye(128, dtype=f32).astype(NPBF)
    a = np.arange(4096, dtype=np.int64)
    m = (a[:, None] * a[None, :]) % 4096
    ang = (2.0 * np.pi / 4096.0) * np.arange(4096, dtype=np.float64)
    ct, st = np.cos(ang).astype(f32).astype(NPBF), np.sin(ang).astype(f32).astype(NPBF)
    _CONST["c_dft"] = np.stack([ct[m], st[m]], 0)
    k = np.arange(64)
    a64 = 2.0 * np.pi * ((k[:, None] * k[None, :]) % 64) / 64.0
    c64, s64 = np.cos(a64).astype(f32), np.sin(a64).astype(f32)
    cc = np.zeros((64, 2, 128), f32)
    cc[:, 0, :] = np.concatenate([c64, c64], 1)
    cc[:, 1, :] = np.concatenate([-s64, -s64], 1)
    _CONST["c_c64"] = cc
    j = np.arange(128, dtype=f32)[:, None]
    cidx = np.arange(128, dtype=f32)[None, :]
    ret = np.zeros((128, 4, 128), f32)
    ret[:, 0, :] = np.maximum(cidx - j, 0)
    ret[:, 1, :] = np.maximum(j - cidx, 0)
    ret[:, 2, :] = (cidx >= j)
    ret[:, 3, :] = (j > cidx)
    _CONST["c_ret"] = ret
    rq = np.zeros((128, 2, 128), f32)
    rq[:, 0, :] = cidx + 1.0
    rq[:, 1, :] = 128.0 - cidx
    _CONST["c_retq"] = rq
    rw = np.zeros((128, 8), f32)
    rw[:, 0:4] = 127.0 - j
    rw[:, 4:8] = j
    _CONST["c_retw"] = rw
    kk = np.arange(128)[:, None]
    qq = np.arange(128)[None, :]
    dm = np.zeros((4, 2, 128, 512), f32)
    for v in range(4):
        for mm in range(2):
            jj = mm * 128 + kk
            ok = (jj - qq >= 0) & (jj - qq <= 128)
            if v & 1:
                ok = ok & (jj >= 64)
            if v & 2:
                ok = ok & (jj < 192)
            dm[v, mm] = np.tile(np.where(ok, 0.0, -1e30).astype(f32), (1, 4))
    dm = dm.astype(NPBF)
    _CONST["c_dmask"] = dm
    return _CONST


_WNAMES = ("w_ada", "b_ada", "norm_pre_mix", "w_in", "ret_decay_fwd", "ret_decay_bwd", "w_fmix", "mla_q_norm", "mla_w_qb",
           "mla_kv_norm", "mla_w_kvb", "w_out", "norm_post_mix", "norm_pre_ffn", "w_up", "conv_w", "conv_b", "w_down",
           "norm_post_ffn")


def run_cores(seq_lists_x, seq_lists_c, weights, seqs, n_layers=2, debug=False, trace=False, stop=1000):
    kb = KB(seqs, n_layers=n_layers, debug=debug)
    kb.stop = stop
    nc = kb.build()
    cst = _consts()
    in_maps = []
    for xs_, cs_ in zip(seq_lists_x, seq_lists_c):
        m = {"x": np.ascontiguousarray(np.concatenate(xs_, 0), dtype=np.float32),
             "cT": np.ascontiguousarray(np.stack(cs_, 1), dtype=np.float32)}
        for k in _WNAMES:
            m[k] = np.ascontiguousarray(weights[k][:n_layers], dtype=np.float32)
        m.update(cst)
        in_maps.append(m)
    res = run_bass_kernel_spmd(nc, in_maps, core_ids=list(range(len(in_maps))), trace=trace)
    return res, kb


def kernel(x_prompt, x_sample, c_prompt, c_sample, **weights):
    x_prompt = np.asarray(x_prompt, np.float32)
    x_sample = np.asarray(x_sample, np.float32)
    c_prompt = np.asarray(c_prompt, np.float32)
    c_sample = np.asarray(c_sample, np.float32)
    weights = {k: np.asarray(v, np.float32) for k, v in weights.items()}
    seqs = [4096, 2048, 2048, 2048, 2048]
    xs_, cs_ = [], []
    for i in range(8):
        xs_.append([x_prompt[i % 4]] + [x_sample[4 * i + j] for j in range(4)])
        cs_.append([c_prompt[i % 4]] + [c_sample[4 * i + j] for j in range(4)])
    res, kb = run_cores(xs_, cs_, weights, seqs)
    y_prompt = np.stack([res.results[i]["y"][0:4096] for i in range(4)], 0)
    y_sample = np.stack([res.results[i]["y"][4096 + 2048 * j:4096 + 2048 * (j + 1)] for i in range(8) for j in range(4)], 0)
    return (np.ascontiguousarray(y_prompt, dtype=np.float32), np.ascontiguousarray(y_sample, dtype=np.float32))
```

```python
import contextlib
import numpy as np
import ml_dtypes
import concourse.bass as bass
import concourse.mybir as mybir
from concourse.bass_utils import run_bass_kernel_spmd
from concourse.ap import AP

F32 = mybir.dt.float32
BF16 = mybir.dt.bfloat16
AF = mybir.ActivationFunctionType
ALU = mybir.AluOpType
AX = mybir.AxisListType
NPBF = ml_dtypes.bfloat16

D = 1024
DIN = 4000
DFF = 2816
ZW = 4512
Z_RQ, Z_RK, Z_RV, Z_RG, Z_FU, Z_DQ, Z_DK, Z_DV = 0, 256, 512, 768, 1024, 1280, 2048, 2816
Z_MQ, Z_MKV, Z_KPE = 3584, 3968, 4480
EPS = 1e-6
RTW = 512 + 384 + 160
DILS = (1, 4, 16)
MLA_SCALE = 96.0 ** -0.5
DIL_SCALE = 64.0 ** -0.5
WIN = 510


class Res:
    __slots__ = ("w", "r", "psum")

    def __init__(self):
        self.w = None
        self.r = []
        self.psum = False


class Buf:
    def __init__(self, h):
        self.h = h
        self.res = Res()

    def __getitem__(self, k):
        return self.h[k]


def bcast_last(ap, n):
    return AP(ap.tensor, ap.offset, [list(a) for a in ap.ap] + [[0, n]])


def bcast_mid(ap, n):
    l = [list(a) for a in ap.ap]
    return AP(ap.tensor, ap.offset, [l[0], [0, n]] + l[1:])


class KB:
    def __init__(self, seqs, n_layers=2, debug=False):
        self.seqs = list(seqs)
        self.nseq = len(seqs)
        self.T = sum(seqs)
        self.L = n_layers
        self.debug = debug
        self.t0 = [sum(seqs[:i]) for i in range(self.nseq)]
        self.Tp = self.T + 2 * self.nseq
        nc = bass.Bass("TRN2", target_bir_lowering=False)
        self.nc = nc
        self.eng = {"pe": nc.tensor, "act": nc.scalar, "dve": nc.vector, "pool": nc.gpsimd, "sp": nc.sync}
        self.sem = {k: nc.alloc_semaphore("s_" + k) for k in ("pe", "act", "dve", "pool")}
        self.cnt = {k: 0 for k in self.sem}
        self.waited = {k: {} for k in self.eng}
        self.ND = 24
        self.dsem = [nc.alloc_semaphore("d%d" % i) for i in range(self.ND)]
        self.dcnt = [0] * self.ND
        self.drr = 0
        self.uid = 0
        self.dram = {}
        self.stack = contextlib.ExitStack()
        self.cut = 0
        self.stop = 1000

    def _wait(self, e, deps):
        best = {}
        for key, val in deps:
            if key == "pe" and e == "pe":
                continue
            if best.get(key, 0) < val:
                best[key] = val
        w = self.waited[e]
        for key, val in best.items():
            if w.get(key, 0) >= val:
                continue
            sem = self.sem[key] if isinstance(key, str) else self.dsem[key[1]]
            self.eng[e].wait_ge(sem, val)
            w[key] = val

    def _deps(self, reads, writes, e=None):
        deps = []
        for r in reads:
            if r.res.w is not None:
                deps.append(r.res.w)
        for w in writes:
            if w.res.w is not None and w.res.w[0] != e:
                deps.append(w.res.w)
            deps.extend(t for t in w.res.r if t[0] != e)
        return deps

    def op(self, e, fn, reads=(), writes=(), inc=True):
        pr = [r for r in reads if r.res.psum]
        deps = []
        if pr:
            deps = [r.res.w for r in pr if r.res.w is not None]
            reads = [r for r in reads if not r.res.psum]
            writes = list(writes) + pr
        self._wait(e, deps + self._deps(reads, writes, e))
        ins = fn(self.eng[e])
        if inc:
            self.cnt[e] += 1
            ins.then_inc(self.sem[e], 1)
            tok = (e, self.cnt[e])
        else:
            tok = (e, self.cnt[e] + 1)
        for r in reads:
            r.res.r.append(tok)
        for w in writes:
            w.res.w = tok
            w.res.r = []
        return ins

    def dma(self, q, out, in_, reads=(), writes=(), slow=False):
        self._wait(q, self._deps(reads, writes, q))
        k = self.drr
        self.drr = (self.drr + 1) % self.ND
        if self.dcnt[k] > 0:
            self._wait(q, [(("d", k), 16 * self.dcnt[k])])
        self.dcnt[k] += 1
        self.eng[q].dma_start(out=out, in_=in_, allow_slow_non_contiguous=slow).then_inc(self.dsem[k], 16)
        tok = (("d", k), 16 * self.dcnt[k])
        for r in reads:
            r.res.r.append(tok)
        for w in writes:
            w.res.w = tok
            w.res.r = []

    def barrier(self):
        toks = [(k, self.cnt[k]) for k in self.cnt if self.cnt[k] > 0]
        toks += [(("d", k), 16 * self.dcnt[k]) for k in range(self.ND) if self.dcnt[k] > 0]
        for e in self.eng:
            self._wait(e, [t for t in toks if t[0] != e])

    def interleave(self, gens, W):
        active = []
        it = iter(gens)
        done = False
        while True:
            while not done and len(active) < W:
                g = next(it, None)
                if g is None:
                    done = True
                    break
                active.append(g)
            if not active:
                break
            for g in list(active):
                try:
                    next(g)
                except StopIteration:
                    active.remove(g)

    def sb(self, shape, dt, name=None):
        self.uid += 1
        nm = "%s_%d" % (name or "sb", self.uid)
        return Buf(self.stack.enter_context(self.nc.sbuf_tensor(nm, list(shape), dt)))

    def begin_phase(self):
        self.gstack = self.stack
        self.stack = contextlib.ExitStack()

    def end_phase(self):
        self.barrier()
        self.stack.close()
        self.stack = self.gstack

    def ps(self, shape, dt, name=None):
        self.uid += 1
        b = Buf(self.nc.alloc_psum_tensor(name or ("ps%d" % self.uid), list(shape), dt))
        b.res.psum = True
        return b

    def din(self, name, shape, dt):
        t = self.nc.dram_tensor(name, list(shape), dt, kind="ExternalInput").ap()
        self.dram[name] = t
        return t

    def dscr(self, name, shape, dt):
        kind = "ExternalOutput" if self.debug else "Internal"
        t = self.nc.dram_tensor(name, list(shape), dt, kind=kind).ap()
        self.dram[name] = t
        return t

    def rstd(self, ss, out, n, tmp):
        self.op("act", lambda e: e.activation(out=tmp[:], in_=ss[:], func=AF.Sqrt, bias=self.epsb[:, 0:1],
                                              scale=1.0 / n), [ss, self.epsb], [tmp])
        self.op("dve", lambda e: e.reciprocal(out[:], tmp[:]), [tmp], [out])

    def transposes(self, pt, src_aps, srcbuf, width=128):
        n = len(src_aps)
        for i, a in enumerate(src_aps):
            self.op("pe", lambda e, i=i, a=a: e.transpose(pt[0:width, i, :], a, self.ident[:]),
                    [srcbuf, self.ident], [pt], inc=(i == n - 1))

    def build(self):
        nc = self.nc
        L, T, NS = self.L, self.T, self.nseq
        x_in = self.din("x", [T, D], F32)
        cT_in = self.din("cT", [D, NS], F32)
        w_ada = self.din("w_ada", [L, D, 6 * D], F32)
        b_ada = self.din("b_ada", [L, 6 * D], F32)
        n_pre_mix = self.din("norm_pre_mix", [L, D], F32)
        w_in = self.din("w_in", [L, D, DIN], F32)
        dec_f = self.din("ret_decay_fwd", [L, 4], F32)
        dec_b = self.din("ret_decay_bwd", [L, 4], F32)
        w_fmix = self.din("w_fmix", [L, 4, 64, 64], F32)
        q_norm = self.din("mla_q_norm", [L, 256], F32)
        w_qb = self.din("mla_w_qb", [L, 256, 384], F32)
        kv_norm = self.din("mla_kv_norm", [L, 128], F32)
        w_kvb = self.din("mla_w_kvb", [L, 128, 512], F32)
        w_out = self.din("w_out", [L, D, D], F32)
        n_post_mix = self.din("norm_post_mix", [L, D], F32)
        n_pre_ffn = self.din("norm_pre_ffn", [L, D], F32)
        w_up = self.din("w_up", [L, D, 2 * DFF], F32)
        conv_w = self.din("conv_w", [L, 3, 2 * DFF], F32)
        conv_b = self.din("conv_b", [L, 2 * DFF], F32)
        w_down = self.din("w_down", [L, DFF, D], F32)
        n_post_ffn = self.din("norm_post_ffn", [L, D], F32)
        ident_in = self.din("c_ident", [128, 128], BF16)
        rope_in = self.din("c_rope", [4096, RTW], F32)
        dft_in = self.din("c_dft", [2, 4096, 4096], BF16)
        c64_in = self.din("c_c64", [64, 2, 128], F32)
        ret_in = self.din("c_ret", [128, 4, 128], F32)
        retq_in = self.din("c_retq", [128, 2, 128], F32)
        retw_in = self.din("c_retw", [128, 8], F32)
        dmask_in = self.din("c_dmask", [4, 2, 128, 512], BF16)
        y_out = self.nc.dram_tensor("y", [T, D], F32, kind="ExternalOutput").ap()
        zs = self.dscr("zs", [T, ZW], BF16)
        osx = self.dscr("os", [T, D], BF16)
        dsc = self.dscr("dsc", [T, 3, 256], BF16)
        dstt = self.dscr("dst", [T, 3, 8], F32)
        xs = self.dscr("xs", [T, D], F32)
        x1s = self.dscr("x1s", [T, D], F32)
        h2s = self.dscr("h2s", [D, self.Tp], BF16)
        vts = self.dscr("vts", [DFF, T], BF16)
        modd = self.dscr("modd", [L, NS, 6 * D], F32)
        self.zs, self.osx, self.dsc, self.dstt = zs, osx, dsc, dstt

        self.ident = self.sb([128, 128], BF16, "ident")
        self.dma("sp", self.ident[:], ident_in[:, :], [], [self.ident])
        self.epsb = self.sb([128, 1], F32, "epsb")
        self.op("dve", lambda e: e.memset(self.epsb[:], EPS), [], [self.epsb])
        self.zero = self.sb([128, 512], BF16, "zero")
        self.op("dve", lambda e: e.memset(self.zero[:], 0.0), [], [self.zero])
        self.bank = [self.ps([128, 512], F32, "bank%d" % i) for i in range(8)]

        self._zero_halo(h2s)

        self.nph = 0

        def run(fn, *a):
            self.nph += 1
            if self.nph > getattr(self, "stop", 1000):
                return
            self.begin_phase()
            fn(*a)
            self.end_phase()

        run(self.prephase, cT_in, w_ada, b_ada, n_pre_mix, n_post_mix, n_pre_ffn, n_post_ffn, modd)
        for l in range(L):
            xsrc = x_in if l == 0 else xs
            xdst = y_out if l == L - 1 else xs
            run(self.phase_a, l, xsrc, w_in, q_norm, w_qb, kv_norm, w_kvb, rope_in, modd)
            run(self.retention, l, dec_f, dec_b, ret_in, retq_in, retw_in)
            run(self.fourier, l, w_fmix, c64_in, dft_in)
            run(self.dilated, l, dmask_in)
            run(self.mla, l)
            run(self.phase_c1, l, xsrc, w_out, modd, x1s, h2s)
            run(self.phase_c2a, l, w_up, conv_w, conv_b, h2s, vts)
            run(self.phase_c2b, l, w_down, modd, x1s, vts, xdst)
        self.barrier()
        return nc

    def _zero_halo(self, h2s):
        for q in range(self.nseq):
            for col in (self.t0[q] + 2 * q, self.t0[q] + 2 * q + self.seqs[q] + 1):
                dst = AP(h2s.tensor, col, [[self.Tp, 128], [128 * self.Tp, 8], [1, 1]])
                src = self.zero[:, 0:8].rearrange("p (a b) -> p a b", b=1)
                self.dma("sp", dst, src, [self.zero], [], slow=True)

    def prephase(self, cT_in, w_ada, b_ada, n_pre_mix, n_post_mix, n_pre_ffn, n_post_ffn, modd):
        NS, L = self.nseq, self.L
        cT = self.sb([128, 8, NS], F32, "cT")
        self.dma("sp", cT[:], cT_in.rearrange("(k p) u -> p k u", p=128), [], [cT], slow=True)
        sg = self.sb([128, 8, NS], F32, "sg")
        cTb = self.sb([128, 8, NS], BF16, "cTb")
        self.op("act", lambda e: e.activation(out=sg[:], in_=cT[:], func=AF.Sigmoid), [cT], [sg])
        self.op("dve", lambda e: e.tensor_tensor(cTb[:], cT[:], sg[:], ALU.mult), [cT, sg], [cTb])
        wa = [self.sb([128, 8, 512], BF16, "wa%d" % i) for i in range(2)]
        mod = self.sb([NS, 6 * D], F32, "mod")
        brow = self.sb([NS, 6 * D], F32, "brow")
        nrm = self.sb([NS, 4, D], F32, "nrm")
        out6 = self.sb([NS, 6, D], F32, "out6")
        for l in range(L):
            self.dma("sp", brow[:], AP(b_ada.tensor, l * 6 * D, [[0, NS], [1, 6 * D]]), [], [brow])
            for i, nt in enumerate((n_pre_mix, n_post_mix, n_pre_ffn, n_post_ffn)):
                self.dma("sp", nrm[:, i, :], AP(nt.tensor, l * D, [[0, NS], [1, D]]), [], [nrm])
            for nb in range(12):
                w = wa[nb % 2]
                self.dma("pool", w[:], w_ada[l, :, nb * 512:(nb + 1) * 512].rearrange("(k p) n -> p k n", p=128),
                         [], [w])
                pb = self.bank[nb % 2]
                for k in range(8):
                    self.op("pe", lambda e, k=k, w=w, pb=pb: e.matmul(pb[0:NS, :], cTb[:, k, :], w[:, k, :],
                                                                     start=(k == 0), stop=(k == 7)),
                            [cTb, w], [pb], inc=(k == 7))
                self.op("dve", lambda e, pb=pb, nb=nb: e.tensor_tensor(mod[:, nb * 512:(nb + 1) * 512], pb[0:NS, :],
                                                                     brow[:, nb * 512:(nb + 1) * 512], ALU.add),
                        [pb, brow], [mod])
            m = lambda i: mod[:, i * D:(i + 1) * D]
            self.op("dve", lambda e: e.scalar_tensor_tensor(out6[:, 0, :], m(1), 1.0, nrm[:, 0, :], ALU.add, ALU.mult),
                    [mod, nrm], [out6])
            self.op("dve", lambda e: e.tensor_copy(out6[:, 1, :], m(0)), [mod], [out6])
            self.op("dve", lambda e: e.tensor_tensor(out6[:, 2, :], m(2), nrm[:, 1, :], ALU.mult), [mod, nrm], [out6])
            self.op("dve", lambda e: e.scalar_tensor_tensor(out6[:, 3, :], m(4), 1.0, nrm[:, 2, :], ALU.add, ALU.mult),
                    [mod, nrm], [out6])
            self.op("dve", lambda e: e.tensor_copy(out6[:, 4, :], m(3)), [mod], [out6])
            self.op("dve", lambda e: e.tensor_tensor(out6[:, 5, :], m(5), nrm[:, 3, :], ALU.mult), [mod, nrm], [out6])
            self.dma("sp", modd[l, :, :], out6[:].rearrange("u a d -> u (a d)"), [out6], [])

    def load_rows(self, buf, modd, l, q, idxs):
        for i, ix in enumerate(idxs):
            src = AP(modd.tensor, (l * self.nseq + q) * 6 * D + ix * D, [[0, 128], [1, D]])
            self.dma("sp", buf[:, i, :], src, [], [buf])

    def phase_a(self, l, xsrc, w_in, q_norm, w_qb, kv_norm, w_kvb, rope_in, modd):
        zs = self.zs
        win = self.sb([128, 8, DIN], BF16, "win")
        for k in range(8):
            self.dma("pool", win[:, k, :], w_in[l, k * 128:(k + 1) * 128, :], [], [win])
        wqb = self.sb([128, 2, 384], BF16, "wqb")
        self.dma("pool", wqb[:], w_qb[l].rearrange("(k p) n -> p k n", p=128), [], [wqb])
        wkvb = self.sb([128, 512], BF16, "wkvb")
        self.dma("pool", wkvb[:], w_kvb[l], [], [wkvb])
        qn = self.sb([128, 256], F32, "qn")
        self.dma("sp", qn[:], AP(q_norm.tensor, l * 256, [[0, 128], [1, 256]]), [], [qn])
        kvn = self.sb([128, 128], F32, "kvn")
        self.dma("sp", kvn[:], AP(kv_norm.tensor, l * 128, [[0, 128], [1, 128]]), [], [kvn])
        rows = self.sb([128, 2, D], F32, "arows")
        xt = [self.sb([128, D], F32, "a_xt%d" % i) for i in range(2)]
        rt = [self.sb([128, RTW], F32, "a_rt%d" % i) for i in range(2)]
        junk = self.sb([128, D], BF16, "a_junk")
        ss = self.sb([128, 4], F32, "a_ss")
        sst = self.sb([128, 4], F32, "a_sst")
        rs = self.sb([128, 4], F32, "a_rs")
        hf = self.sb([128, D], F32, "a_hf")
        hb = self.sb([128, D], BF16, "a_hb")
        hT = [self.sb([128, 8, 128], BF16, "a_hT%d" % i) for i in range(2)]
        zt = [self.sb([128, ZW], BF16, "a_zt%d" % i) for i in range(2)]
        cn = self.sb([128, 384], BF16, "a_cn")
        cnT = self.sb([128, 3, 128], BF16, "a_cnT")
        tmp = [self.sb([128, 256], F32, "a_tmp%d" % i) for i in range(4)]
        pT = self.ps_bf(0)
        pT2 = self.ps_bf(1)
        pz = [self.bank[2], self.bank[3], self.bank[4], self.bank[5]]
        pq = self.bank[6]
        pkv = self.bank[7]

        tiles = [(q, i) for q in range(self.nseq) for i in range(self.seqs[q] // 128)]

        def load_x(n):
            q, i = tiles[n]
            r0 = self.t0[q] + i * 128
            self.dma("sp", xt[n % 2][:], xsrc[r0:r0 + 128, :], [], [xt[n % 2]])

        def load_rt(n):
            q, i = tiles[n]
            self.dma("sp", rt[n % 2][:], rope_in[i * 128:(i + 1) * 128, :], [], [rt[n % 2]])

        def rope(e, buf, x1, x2, cos, sin, w):
            shp = None
            t1, t2, t3, t4 = [t[:, 0:w].rearrange("p (h d) -> p h d", h=x1.shape[1]) for t in tmp]
            self.op(e, lambda g: g.tensor_tensor(t1, x1, cos, ALU.mult), [buf] + rtb, [tmp[0]])
            self.op(e, lambda g: g.tensor_tensor(t2, x2, sin, ALU.mult), [buf] + rtb, [tmp[1]])
            self.op(e, lambda g: g.tensor_tensor(t3, x2, cos, ALU.mult), [buf] + rtb, [tmp[2]])
            self.op(e, lambda g: g.tensor_tensor(t4, x1, sin, ALU.mult), [buf] + rtb, [tmp[3]])
            self.op(e, lambda g: g.tensor_tensor(x1, t1, t2, ALU.subtract), [tmp[0], tmp[1]], [buf])
            self.op(e, lambda g: g.tensor_tensor(x2, t3, t4, ALU.add), [tmp[2], tmp[3]], [buf])

        ss1 = self.sb([128, 4], F32, "a_ss1")
        sst1 = self.sb([128, 4], F32, "a_sst1")
        rs1 = self.sb([128, 4], F32, "a_rs1")
        state = {"q": -1}

        def stage1(n):
            q, i = tiles[n]
            if q != state["q"]:
                self.load_rows(rows, modd, l, q, (0, 1))
                state["q"] = q
            X, HT = xt[n % 2], hT[n % 2]
            self.op("act", lambda e: e.activation(out=junk[:], in_=X[:], func=AF.Square, accum_out=ss1[:, 0:1]),
                    [X], [junk, ss1])
            self.rstd_cols(ss1, rs1, sst1, 0, 1, D)
            self.op("dve", lambda e: e.scalar_tensor_tensor(hf[:], X[:], rs1[:, 0:1], rows[:, 0, :], ALU.mult, ALU.mult),
                    [X, rs1, rows], [hf])
            self.op("pool", lambda e: e.tensor_tensor(hb[:], hf[:], rows[:, 1, :], ALU.add), [hf, rows], [hb])

        def stage1b(n):
            HT = hT[n % 2]
            self.transposes(pT, [hb[:, k * 128:(k + 1) * 128] for k in range(8)], hb)
            self.op("act", lambda e: e.activation(out=HT[:], in_=pT[:], func=AF.Copy), [pT], [HT])

        def stage2(n, hook):
            q, i = tiles[n]
            R, Z, HT = rt[n % 2], zt[n % 2], hT[n % 2]
            rtb.clear()
            rtb.append(R)
            r0 = self.t0[q] + i * 128
            for nb in range(8):
                c0 = nb * 512
                cw = min(512, DIN - c0)
                pb = pz[nb % 4]
                for k in range(8):
                    self.op("pe", lambda e, k=k, pb=pb, c0=c0, cw=cw: e.matmul(
                        pb[:, 0:cw], HT[:, k, :], win[:, k, c0:c0 + cw], start=(k == 0), stop=(k == 7)),
                        [HT, win], [pb], inc=(k == 7))
                if nb == 1:
                    self.op("act", lambda e, pb=pb: e.activation(out=Z[:, 512:768], in_=pb[:, 0:256], func=AF.Copy),
                            [pb], [Z])
                    self.op("act", lambda e, pb=pb: e.activation(out=Z[:, 768:1024], in_=pb[:, 256:512], func=AF.Silu),
                            [pb], [Z])
                elif nb < 7:
                    if nb in (2, 5):
                        self.op("dve", lambda e, pb=pb, c0=c0: e.tensor_copy(Z[:, c0:c0 + 512], pb[:]), [pb], [Z])
                    else:
                        self.op("act", lambda e, pb=pb, c0=c0: e.activation(out=Z[:, c0:c0 + 512], in_=pb[:], func=AF.Copy),
                                [pb], [Z])
                    if nb == 3:
                        hook()
                else:
                    self.op("act", lambda e, pb=pb: e.activation(out=junk[:, 0:256], in_=pb[:, 0:256], func=AF.Square,
                                                                accum_out=ss[:, 1:2]), [pb], [junk, ss])
                    self.op("act", lambda e, pb=pb: e.activation(out=junk[:, 256:384], in_=pb[:, 256:384],
                                                                func=AF.Square, accum_out=ss[:, 2:3]), [pb], [junk, ss])
                    self.op("act", lambda e, pb=pb: e.activation(out=Z[:, Z_KPE:Z_KPE + 32], in_=pb[:, 384:416],
                                                                func=AF.Copy), [pb], [Z])
                    self.rstd_cols(ss, rs, sst, 1, 2, 256)
                    self.rstd_cols(ss, rs, sst, 2, 3, 128)
                    self.op("dve", lambda e, pb=pb: e.scalar_tensor_tensor(cn[:, 0:256], pb[:, 0:256], rs[:, 1:2], qn[:],
                                                                          ALU.mult, ALU.mult), [pb, rs, qn], [cn])
                    self.op("dve", lambda e, pb=pb: e.scalar_tensor_tensor(cn[:, 256:384], pb[:, 256:384], rs[:, 2:3],
                                                                          kvn[:], ALU.mult, ALU.mult), [pb, rs, kvn], [cn])
                    self.transposes(pT2, [cn[:, k * 128:(k + 1) * 128] for k in range(3)], cn)
                    self.op("act", lambda e: e.activation(out=cnT[:], in_=pT2[:, 0:3, :], func=AF.Copy), [pT2], [cnT])
                    for k in range(2):
                        self.op("pe", lambda e, k=k: e.matmul(pq[:, 0:384], cnT[:, k, :], wqb[:, k, :], start=(k == 0),
                                                             stop=(k == 1)), [cnT, wqb], [pq], inc=(k == 1))
                    self.op("pe", lambda e: e.matmul(pkv[:, 0:512], cnT[:, 2, :], wkvb[:], start=True, stop=True),
                            [cnT, wkvb], [pkv])
                    self.op("act", lambda e: e.activation(out=Z[:, Z_MQ:Z_MQ + 384], in_=pq[:, 0:384], func=AF.Copy),
                            [pq], [Z])
                    self.op("act", lambda e: e.activation(out=Z[:, Z_MKV:Z_MKV + 512], in_=pkv[:], func=AF.Copy),
                            [pkv], [Z])
            v = Z[:, 0:512].rearrange("p (h d) -> p h d", h=8)
            c = R[:, 0:256].rearrange("p (h d) -> p h d", h=8)
            s = R[:, 256:512].rearrange("p (h d) -> p h d", h=8)
            rope("pool", Z, v[:, :, 0:32], v[:, :, 32:64], c, s, 256)
            v = Z[:, Z_DQ:Z_DQ + 1536].rearrange("p (h d) -> p h d", h=24)
            c = R[:, 512:704].rearrange("p (h d) -> p h d", h=24)
            s = R[:, 704:896].rearrange("p (h d) -> p h d", h=24)
            rope("dve", Z, v[:, :, 0:8], v[:, :, 8:16], c, s, 192)
            v = Z[:, Z_MQ:Z_MQ + 384].rearrange("p (h d) -> p h d", h=4)
            c = R[:, 896:960].rearrange("p (h d) -> p h d", h=4)
            s = R[:, 976:1040].rearrange("p (h d) -> p h d", h=4)
            rope("dve", Z, v[:, :, 64:80], v[:, :, 80:96], c, s, 64)
            v = Z[:, Z_KPE:Z_KPE + 32].rearrange("p (h d) -> p h d", h=1)
            c = R[:, 960:976].rearrange("p (h d) -> p h d", h=1)
            s = R[:, 1040:1056].rearrange("p (h d) -> p h d", h=1)
            rope("dve", Z, v[:, :, 0:16], v[:, :, 16:32], c, s, 16)
            self.dma("sp", zs[r0:r0 + 128, :], Z[:], [Z], [])


        rtb = []
        NTL = len(tiles)
        load_x(0)
        if NTL > 1:
            load_x(1)
        load_rt(0)
        stage1(0)
        stage1b(0)
        for n in range(NTL):
            if n + 1 < NTL:
                load_rt(n + 1)
                stage1(n + 1)
            if n + 2 < NTL:
                load_x(n + 2)
            stage2(n, (lambda n=n: stage1b(n + 1)) if n + 1 < NTL else (lambda: None))

    def ps_bf(self, i):
        return _View(self.bank[i].h[:].bitcast(BF16).rearrange("p (a b) -> p a b", b=128), self.bank[i].res)

    def rstd_cols(self, ss, rs, tmp, a, b, n):
        self.op("act", lambda e: e.activation(out=tmp[:, a:b], in_=ss[:, a:b], func=AF.Sqrt, bias=self.epsb[:, 0:1],
                                              scale=1.0 / n), [ss, self.epsb], [tmp])
        self.op("dve", lambda e: e.reciprocal(rs[:, a:b], tmp[:, a:b]), [tmp], [rs])


    def retention(self, l, dec_f, dec_b, ret_in, retq_in, retw_in):
        zs, osx = self.zs, self.osx
        raw8 = self.sb([128, 8], F32, "r_raw8")
        self.dma("sp", raw8[:, 0:4], AP(dec_f.tensor, l * 4, [[0, 128], [1, 4]]), [], [raw8])
        self.dma("sp", raw8[:, 4:8], AP(dec_b.tensor, l * 4, [[0, 128], [1, 4]]), [], [raw8])
        rawp = self.sb([128, 2, 2], F32, "r_rawp")
        for half in range(2):
            for di, dt_ in enumerate((dec_f, dec_b)):
                self.dma("sp", rawp[half * 64:(half + 1) * 64, di, :],
                         AP(dt_.tensor, l * 4 + half, [[0, 64], [2, 2]]), [], [rawp], slow=True)
        lg8 = self.sb([128, 8], F32, "r_lg8")
        lgp = self.sb([128, 4], F32, "r_lgp")

        def logsig(dst, src):
            self.op("act", lambda e: e.activation(out=dst, in_=src, func=AF.Exp, scale=-1.0), [raw8, rawp], [lg8, lgp])
            self.op("dve", lambda e: e.tensor_scalar(dst, dst, 1.0, None, ALU.add), [lg8, lgp], [lg8, lgp])
            self.op("act", lambda e: e.activation(out=dst, in_=dst, func=AF.Ln), [lg8, lgp], [lg8, lgp])
            self.op("dve", lambda e: e.tensor_scalar(dst, dst, -1.0, None, ALU.mult), [lg8, lgp], [lg8, lgp])

        logsig(lg8[:], raw8[:])
        logsig(lgp[:], rawp[:].rearrange("p a b -> p (a b)"))
        rc = self.sb([128, 4, 128], F32, "r_rc")
        self.dma("sp", rc[:], ret_in[:, :, :], [], [rc])
        rq = self.sb([128, 2, 128], F32, "r_rq")
        self.dma("sp", rq[:], retq_in[:, :, :], [], [rq])
        rw = self.sb([128, 8], F32, "r_rw")
        self.dma("sp", rw[:], retw_in[:, :], [], [rw])
        DT = self.sb([128, 4, 128], F32, "r_DT")
        e1 = self.sb([128, 128], F32, "r_e1")
        e2 = self.sb([128, 128], F32, "r_e2")
        for h in range(4):
            self.op("act", lambda e, h=h: e.activation(out=e1[:], in_=rc[:, 0, :], func=AF.Exp, scale=lg8[:, h:h + 1]),
                    [rc, lg8], [e1])
            self.op("act", lambda e, h=h: e.activation(out=e2[:], in_=rc[:, 1, :], func=AF.Exp,
                                                      scale=lg8[:, 4 + h:5 + h]), [rc, lg8], [e2])
            self.op("dve", lambda e: e.tensor_tensor(e1[:], e1[:], rc[:, 2, :], ALU.mult), [e1, rc], [e1])
            self.op("dve", lambda e: e.tensor_tensor(e2[:], e2[:], rc[:, 3, :], ALU.mult), [e2, rc], [e2])
            self.op("dve", lambda e, h=h: e.tensor_tensor(DT[:, h, :], e1[:], e2[:], ALU.add), [e1, e2], [DT])
        QFB = self.sb([128, 2, 2, 128], F32, "r_QFB")
        for di in range(2):
            for pr in range(2):
                self.op("act", lambda e, di=di, pr=pr: e.activation(out=QFB[:, di, pr, :], in_=rq[:, di, :], func=AF.Exp,
                                                                  scale=lgp[:, di * 2 + pr:di * 2 + pr + 1]),
                        [rq, lgp], [QFB])
        WFB = self.sb([128, 8], F32, "r_WFB")
        self.op("dve", lambda e: e.tensor_tensor(WFB[:], rw[:], lg8[:], ALU.mult), [rw, lg8], [WFB])
        self.op("act", lambda e: e.activation(out=WFB[:], in_=WFB[:], func=AF.Exp), [WFB], [WFB])
        dexp = self.sb([128, 4], F32, "r_dexp")
        self.op("act", lambda e: e.activation(out=dexp[:], in_=lgp[:], func=AF.Exp, scale=128.0), [lgp], [dexp])
        DEC = self.sb([128, 4, 64], F32, "r_DEC")
        self.op("dve", lambda e: e.tensor_copy(DEC[:], bcast_last(dexp[:], 64)), [dexp], [DEC])

        if getattr(self, 'cut', 0) == 1:
            return
        NMAX = max(self.seqs) // 128
        kvall = self.sb([128, NMAX, 4, 64], F32, "r_kvall")
        stF = self.sb([128, NMAX, 2, 64], BF16, "r_stF")
        stB = self.sb([128, NMAX, 2, 64], BF16, "r_stB")
        curF = self.sb([128, 2, 64], F32, "r_curF")
        curB = self.sb([128, 2, 64], F32, "r_curB")
        kvt = [self.sb([128, 512], BF16, "r_kvt%d" % i) for i in range(2)]
        kw = [self.sb([128, 2, 256], BF16, "r_kw%d" % i) for i in range(2)]
        qk = [self.sb([128, 1024], BF16, "r_qk%d" % i) for i in range(2)]
        qkT = [self.sb([128, 4, 128], BF16, "r_qkT%d" % i) for i in range(2)]
        qfT = [self.sb([128, 2, 128], BF16, "r_qfT%d" % i) for i in range(2)]
        qbT = [self.sb([128, 2, 128], BF16, "r_qbT%d" % i) for i in range(2)]
        stt = [self.sb([128, 4, 128], BF16, "r_stt%d" % i) for i in range(2)]
        ot = [self.sb([128, 256], BF16, "r_ot%d" % i) for i in range(2)]
        o1 = self.sb([128, 256], F32, "r_o1")
        junk = self.sb([128, 64], BF16, "r_junk")
        ssr = self.sb([128, 4], F32, "r_ssr")
        sst = self.sb([128, 4], F32, "r_sst")
        rr = self.sb([128, 4], F32, "r_rr")

        for q in range(self.nseq):
            S = self.seqs[q]
            N = S // 128
            t0 = self.t0[q]
            def load1(n):
                self.dma("sp", kvt[n % 2][:], zs[t0 + n * 128:t0 + (n + 1) * 128, 256:768], [], [kvt[n % 2]])
            load1(0)
            for n in range(N):
                if n + 1 < N:
                    load1(n + 1)
                KV, KW = kvt[n % 2], kw[n % 2]
                pk = self.bank[(n % 2) * 4]
                k3 = KV[:, 0:256].rearrange("p (h d) -> p h d", h=4)
                for di in range(2):
                    self.op("dve" if di == 0 else "pool", lambda e, di=di: e.tensor_tensor(
                        KW[:, di, :].rearrange("p (h d) -> p h d", h=4), k3, bcast_last(WFB[:, di * 4:di * 4 + 4], 64),
                        ALU.mult), [KV, WFB], [KW])
                for di in range(2):
                    for pr in range(2):
                        c0 = (di * 2 + pr) * 128
                        self.op("pe", lambda e, di=di, pr=pr, c0=c0: e.matmul(
                            pk[:, c0:c0 + 128], KW[:, di, pr * 128:(pr + 1) * 128], KV[:, 256 + pr * 128:256 + (pr + 1) * 128],
                            start=True, stop=True), [KW, KV], [pk], inc=(di == 1 and pr == 1))
                pk3 = pk[:].rearrange("p (a b) -> p a b", a=4)
                self.op("act", lambda e: e.activation(out=kvall[0:64, n, :, :], in_=pk3[0:64, :, 0:64], func=AF.Copy),
                        [pk], [kvall])
                self.op("act", lambda e: e.activation(out=kvall[64:128, n, :, :], in_=pk3[64:128, :, 64:128], func=AF.Copy),
                        [pk], [kvall])
            if getattr(self, 'cut', 0) == 2:
                return
            self.op("dve", lambda e: e.memset(curF[:], 0.0), [], [curF])
            self.op("pool", lambda e: e.memset(curB[:], 0.0), [], [curB])
            for n in range(N):
                self.op("dve", lambda e, n=n: e.tensor_copy(stF[:, n, :, :], curF[:]), [curF], [stF])
                if n < N - 1:
                    self.op("dve", lambda e: e.tensor_tensor(curF[:], curF[:], DEC[:, 0:2, :], ALU.mult), [curF, DEC], [curF])
                    self.op("dve", lambda e, n=n: e.tensor_tensor(curF[:], curF[:], kvall[:, n, 0:2, :], ALU.add),
                            [curF, kvall], [curF])
            for n in range(N - 1, -1, -1):
                self.op("pool", lambda e, n=n: e.tensor_copy(stB[:, n, :, :], curB[:]), [curB], [stB])
                if n > 0:
                    self.op("pool", lambda e: e.tensor_tensor(curB[:], curB[:], DEC[:, 2:4, :], ALU.mult), [curB, DEC], [curB])
                    self.op("pool", lambda e, n=n: e.tensor_tensor(curB[:], curB[:], kvall[:, n, 2:4, :], ALU.add),
                            [curB, kvall], [curB])
            if getattr(self, 'cut', 0) == 3:
                return
            def load2(n):
                self.dma("sp", qk[n % 2][:], zs[t0 + n * 128:t0 + (n + 1) * 128, 0:1024], [], [qk[n % 2]])
            load2(0)
            for n in range(N):
                if n + 1 < N:
                    load2(n + 1)
                s = n % 2
                QK, QKT, QF, QB, STT, OT = qk[s], qkT[s], qfT[s], qbT[s], stt[s], ot[s]
                pT = self.ps_bf(4 * s)
                pstE, pstO = self.bank[4 * s + 1], self.bank[4 * s + 2]
                po = self.bank[4 * s + 3]
                self.transposes(pT, [QK[:, j * 128:(j + 1) * 128] for j in range(4)], QK)
                self.op("act", lambda e: e.activation(out=QKT[:], in_=pT[:, 0:4, :], func=AF.Copy), [pT], [QKT])
                self.op("dve", lambda e: e.tensor_tensor(QF[:], pT[:, 0:2, :], QFB[:, 0, :, :], ALU.mult), [pT, QFB], [QF])
                self.op("pool", lambda e: e.tensor_tensor(QB[:], QKT[:, 0:2, :], QFB[:, 1, :, :], ALU.mult), [QKT, QFB], [QB])
                if self.cut == 4:
                    continue
                for h in range(4):
                    pr, b0 = h // 2, (h % 2) * 64
                    pst = pstE if h % 2 == 0 else pstO
                    self.op("pe", lambda e, h=h, pr=pr, b0=b0, pst=pst: e.matmul(
                        pst[:, pr * 128:(pr + 1) * 128], QKT[b0:b0 + 64, 2 + pr, :], QKT[b0:b0 + 64, pr, :],
                        start=True, stop=True), [QKT], [pst], inc=(h >= 2))
                for par, pst in enumerate((pstE, pstO)):
                    self.op("dve", lambda e, par=par, pst=pst: e.tensor_tensor(
                        STT[:, par::2, :], pst[:, 0:256].rearrange("p (a b) -> p a b", a=2), DT[:, par::2, :], ALU.mult),
                        [pst, DT], [STT])
                if self.cut == 5:
                    continue
                for h in range(4):
                    pr, b0 = h // 2, (h % 2) * 64
                    oc = po[:, h * 64:(h + 1) * 64]
                    self.op("pe", lambda e, h=h, oc=oc: e.matmul(oc, STT[:, h, :], QK[:, 512 + h * 64:512 + (h + 1) * 64],
                                                                 start=True, stop=False), [STT, QK], [po], inc=False)
                    self.op("pe", lambda e, pr=pr, b0=b0, oc=oc: e.matmul(oc, QF[b0:b0 + 64, pr, :], stF[b0:b0 + 64, n, pr, :],
                                                                         start=False, stop=False), [QF, stF], [po], inc=False)
                    self.op("pe", lambda e, pr=pr, b0=b0, oc=oc: e.matmul(oc, QB[b0:b0 + 64, pr, :], stB[b0:b0 + 64, n, pr, :],
                                                                         start=False, stop=True), [QB, stB], [po], inc=True)
                if self.cut == 6:
                    continue
                for h in range(4):
                    self.op("act", lambda e, h=h: e.activation(out=junk[:], in_=po[:, h * 64:(h + 1) * 64], func=AF.Square,
                                                              accum_out=ssr[:, h:h + 1]), [po], [junk, ssr])
                self.rstd_cols(ssr, rr, sst, 0, 4, 64)
                self.op("dve", lambda e: e.tensor_tensor(o1[:].rearrange("p (h d) -> p h d", h=4),
                                                         po[:, 0:256].rearrange("p (h d) -> p h d", h=4),
                                                         bcast_last(rr[:], 64), ALU.mult), [po, rr], [o1])
                self.op("pool", lambda e: e.tensor_tensor(OT[:], o1[:], QK[:, 768:1024], ALU.mult), [o1, QK], [OT])
                self.dma("sp", osx[t0 + n * 128:t0 + (n + 1) * 128, 0:256], OT[:], [OT], [])

    def fourier(self, l, w_fmix, c64_in, dft_in):
        zs, osx = self.zs, self.osx
        wg = self.sb([64, 4, 64], BF16, "f_wg")
        self.dma("pool", wg[:], w_fmix[l].rearrange("g c e -> c g e"), [], [wg])
        c64 = self.sb([64, 2, 128], BF16, "f_c64")
        self.dma("pool", c64[:], c64_in[:, :, :], [], [c64])
        M12 = self.sb([128, 2, 2, 256], BF16, "f_M12")
        self.op("dve", lambda e: e.memset(M12[:], 0.0), [], [M12])
        pm = self.bank[0]
        for cs in range(2):
            for g in range(4):
                c0 = cs * 256 + g * 64
                self.op("pe", lambda e, cs=cs, g=g, c0=c0: e.matmul(pm[:, c0:c0 + 64], c64[:, cs, :], wg[:, g, :],
                                                                   start=True, stop=True), [c64, wg], [pm],
                        inc=(cs == 1 and g == 3))
        for cs in range(2):
            for g in range(4):
                cc, hf = g // 2, g % 2
                c0 = cs * 256 + g * 64
                self.op("act", lambda e, cs=cs, g=g, cc=cc, hf=hf, c0=c0: e.activation(
                    out=M12[hf * 64:(hf + 1) * 64, cs, cc, g * 64:(g + 1) * 64], in_=pm[hf * 64:(hf + 1) * 64, c0:c0 + 64],
                    func=AF.Copy), [pm], [M12])
        SMAX = max(self.seqs)
        groups = []
        for q in range(self.nseq):
            if groups and len(groups[-1]) < 2 and self.seqs[groups[-1][0]] == self.seqs[q] and self.seqs[q] <= 2048:
                groups[-1].append(q)
            else:
                groups.append([q])
        uall = [self.sb([128, SMAX // 128, 256], BF16, "f_uall%d" % i) for i in range(2)]
        PT = [self.sb([128, 2, 2, SMAX if i == 0 else min(SMAX, 2048)], BF16, "f_PT%d" % i) for i in range(2)]
        fo = self.sb([128, SMAX // 128, 256], BF16, "f_fo")
        dtile = [self.sb([128, 512], BF16, "f_dt%d" % i) for i in range(3)]
        ndt = 0
        nev = 0
        for grp in groups:
            S = self.seqs[grp[0]]
            NT = S // 128
            f = 4096 // S
            nrm = 1.0 / np.sqrt(S * 64.0)
            for gi, q in enumerate(grp):
                src = AP(zs.tensor, self.t0[q] * ZW + Z_FU, [[ZW, 128], [128 * ZW, NT], [1, 256]])
                self.dma("sp", uall[gi][:, 0:NT, :], src, [], [uall[gi]])
            it = 0
            for kb in range(S // 512):
                for cs in range(2):
                    bs = (it % 2) * 4
                    it += 1
                    for sc in range(NT):
                        dtb = dtile[ndt % 3]
                        ndt += 1
                        src = AP(dft_in.tensor, cs * 4096 * 4096 + sc * 128 * f * 4096 + kb * 512, [[f * 4096, 128], [1, 512]])
                        self.dma("sp", dtb[:], src, [], [dtb])
                        for gi, q in enumerate(grp):
                            for cc in range(2):
                                bk = self.bank[bs + gi * 2 + cc]
                                self.op("pe", lambda e, gi=gi, cc=cc, bk=bk, sc=sc, dtb=dtb: e.matmul(
                                    bk[:], uall[gi][:, sc, cc * 128:(cc + 1) * 128], dtb[:], start=(sc == 0), stop=(sc == NT - 1)),
                                    [uall[gi], dtb], [bk], inc=(sc == NT - 1 or (gi == len(grp) - 1 and cc == 1)))
                    for gi, q in enumerate(grp):
                        for cc in range(2):
                            bk = self.bank[bs + gi * 2 + cc]
                            dst = PT[gi][:, cs, cc, kb * 512:(kb + 1) * 512]
                            if nev % 2 == 0:
                                self.op("act", lambda e, bk=bk, dst=dst: e.activation(out=dst, in_=bk[:], func=AF.Copy, scale=nrm),
                                        [bk], [PT[gi]])
                            else:
                                self.op("dve", lambda e, bk=bk, dst=dst: e.tensor_scalar(dst, bk[:], nrm, None, ALU.mult),
                                        [bk], [PT[gi]])
                            nev += 1
            for gi, q in enumerate(grp):
                for kb2 in range(NT):
                    bk = self.bank[kb2 % 2]
                    i4 = 0
                    for cs in range(2):
                        for cc in range(2):
                            self.op("pe", lambda e, cs=cs, cc=cc, bk=bk, i4=i4: e.matmul(
                                bk[:, 0:256], PT[gi][:, cs, cc, kb2 * 128:(kb2 + 1) * 128], M12[:, cs, cc, :],
                                start=(i4 == 0), stop=(i4 == 3)), [PT[gi], M12], [bk], inc=(i4 == 3))
                            i4 += 1
                    if kb2 % 2 == 0:
                        self.op("act", lambda e, bk=bk: e.activation(out=fo[:, kb2, :], in_=bk[:, 0:256], func=AF.Copy), [bk], [fo])
                    else:
                        self.op("dve", lambda e, bk=bk: e.tensor_copy(fo[:, kb2, :], bk[:, 0:256]), [bk], [fo])
                dst = AP(osx.tensor, self.t0[q] * D + 256, [[D, 128], [128 * D, NT], [1, 256]])
                self.dma("sp", dst, fo[:, 0:NT, :], [fo], [])

    def dilated(self, l, dmask_in):
        zs, dsc, dstt = self.zs, self.dsc, self.dstt
        MB = self.sb([128, 4, 2, 512], BF16, "d_MB")
        self.dma("sp", MB[:], dmask_in.rearrange("v m k q -> k v m q"), [], [MB])
        NSL = 9
        W = 2
        kst = [self.sb([128, 4, 66], BF16, "d_kst%d" % i) for i in range(NSL)]
        vst = [self.sb([128, 4, 64], BF16, "d_vst%d" % i) for i in range(NSL)]
        kT = [self.sb([128, 4, 128], BF16, "d_kT%d" % i) for i in range(NSL)]
        VP = [self.sb([128, 4, 65], BF16, "d_VP%d" % i) for i in range(NSL)]
        for i in range(NSL):
            self.op("dve", lambda e, i=i: e.memset(kst[i][:], 1.0), [], [kst[i]])
            self.op("pool", lambda e, i=i: e.memset(vst[i][:], 0.0), [], [vst[i]])
        NQ = 4
        qst = [self.sb([128, 4, 66], BF16, "d_qst%d" % i) for i in range(NQ)]
        qT = [self.sb([128, 4, 128], BF16, "d_qT%d" % i) for i in range(NQ)]
        for i in range(NQ):
            self.op("pool", lambda e, i=i: e.memset(qst[i][:], 1.0), [], [qst[i]])
        sqk_ = [self.sb([128, 4, 64], F32, "d_sqk%d" % i) for i in range(NSL)]
        sqq_ = [self.sb([128, 4, 64], F32, "d_sqq%d" % i) for i in range(NQ)]
        ssk_ = [self.sb([128, 4], F32, "d_ssk%d" % i) for i in range(NSL)]
        ssq_ = [self.sb([128, 4], F32, "d_ssq%d" % i) for i in range(NQ)]
        wk_ = [self.sb([128, 4], F32, "d_wk%d" % i) for i in range(NSL)]
        PP = [[self.sb([128, 512], BF16, "d_P%d_%d" % (i, m)) for m in range(2)] for i in range(NQ)]
        stat = [self.sb([128, 8], F32, "d_stat%d" % i) for i in range(NQ)]
        dn = [self.sb([128, 256], BF16, "d_dn%d" % i) for i in range(NQ)]
        col = lambda a: a.rearrange("p (h o) -> p h o", o=1)
        pTk, pTq = self.ps_bf(6), self.ps_bf(7)

        blocks = []
        for q in range(self.nseq):
            S = self.seqs[q]
            for gi, d in enumerate(DILS):
                Lq = S // d
                NB = Lq // 128
                for r in range(d):
                    for b in range(NB):
                        blocks.append((q, gi, d, r, b, NB, Lq))
        kslot = {}
        kcount = [0]

        def row(blk, t):
            q, gi, d, r = blk[0:4]
            return self.t0[q] + r + d * t

        def key_load(blk, m):
            q, gi, d, r, b, NB, Lq = blk
            key = (q, gi, r, m)
            if key in kslot:
                return None
            sl = kcount[0] % NSL
            kcount[0] += 1
            kslot[key] = sl
            K_, V_ = kst[sl], vst[sl]
            ta, tb = max(0, 128 * m - 64), min(Lq, 128 * m + 64)
            p0 = ta - (128 * m - 64)
            p1 = p0 + (tb - ta)
            self.dma("sp", K_[p0:p1, :, 0:64], AP(zs.tensor, row(blk, ta) * ZW + Z_DK + gi * 256, [[d * ZW, tb - ta], [64, 4], [1, 64]]),
                     [], [K_])
            self.dma("sp", V_[p0:p1, :, :], AP(zs.tensor, row(blk, ta) * ZW + Z_DV + gi * 256, [[d * ZW, tb - ta], [64, 4], [1, 64]]),
                     [], [V_])
            return sl

        def key_compute(sl):
            K_, V_ = kst[sl], vst[sl]
            sqk, ssk, wk = sqk_[sl], ssk_[sl], wk_[sl]
            self.op("dve", lambda e: e.tensor_tensor(sqk[:], K_[:, :, 0:64], K_[:, :, 0:64], ALU.mult), [K_], [sqk])
            self.op("dve", lambda e: e.tensor_reduce(ssk[:], sqk[:], AX.X, ALU.add), [sqk], [ssk])
            self.op("dve", lambda e: e.tensor_scalar(K_[:, :, 65:66], col(ssk[:]), -0.5, None, ALU.mult), [ssk], [K_])
            self.op("act", lambda e: e.activation(out=col(wk[:]), in_=K_[:, :, 65:66], func=AF.Exp, scale=-DIL_SCALE), [K_], [wk])
            self.op("pool", lambda e: e.tensor_tensor(VP[sl][:, :, 0:64], V_[:], bcast_last(wk[:], 64), ALU.mult), [V_, wk], [VP[sl]])
            self.op("pool", lambda e: e.tensor_copy(VP[sl][:, :, 64:65], col(wk[:])), [wk], [VP[sl]])
            for h in range(4):
                self.op("pe", lambda e, h=h: e.transpose(pTk[0:66, h, :], K_[:, h, :], self.ident[:]), [K_, self.ident], [pTk],
                        inc=(h == 3))
            self.op("act", lambda e: e.activation(out=kT[sl][0:66, :, :], in_=pTk[0:66, 0:4, :], func=AF.Copy), [pTk], [kT[sl]])

        def gen(i):
            blk = blocks[i]
            q, gi, d, r, b, NB, Lq = blk
            s4, s2 = i % NQ, i % 2
            Q_, ST, DN, QT_ = qst[s4], stat[s4], dn[s4], qT[s4]
            sqq, ssq = sqq_[s4], ssq_[s4]
            self.dma("sp", Q_[:, :, 0:64], AP(zs.tensor, row(blk, 128 * b) * ZW + Z_DQ + gi * 256, [[d * ZW, 128], [64, 4], [1, 64]]),
                     [], [Q_])
            new = [key_load(blk, b), key_load(blk, b + 1)]
            yield
            for sl in new:
                if sl is not None:
                    key_compute(sl)
            self.op("pool", lambda e: e.tensor_tensor(sqq[:], Q_[:, :, 0:64], Q_[:, :, 0:64], ALU.mult), [Q_], [sqq])
            self.op("dve", lambda e: e.tensor_reduce(ssq[:], sqq[:], AX.X, ALU.add), [sqq], [ssq])
            self.op("dve", lambda e: e.tensor_scalar(Q_[:, :, 64:65], col(ssq[:]), -0.5, None, ALU.mult), [ssq], [Q_])
            self.op("dve", lambda e: e.tensor_scalar(col(ST[:, 0:4]), Q_[:, :, 64:65], -DIL_SCALE, None, ALU.mult), [Q_], [ST])
            for h in range(4):
                self.op("pe", lambda e, h=h: e.transpose(pTq[0:66, h, :], Q_[:, h, :], self.ident[:]), [Q_, self.ident], [pTq],
                        inc=(h == 3))
            self.op("dve", lambda e: e.tensor_copy(QT_[0:66, :, :], pTq[0:66, 0:4, :]), [pTq], [QT_])
            yield
            var = (1 if b == 0 else 0) + (2 if b == NB - 1 else 0)
            sls = [kslot[(q, gi, r, b + mm)] for mm in range(2)]
            for mm in range(2):
                sl = sls[mm]
                bk = self.bank[s2 * 2 + mm]
                self.op("pe", lambda e, bk=bk, mm=mm: e.matmul(bk[:], self.ident[:], MB[:, var, mm, :], start=True, stop=False),
                        [self.ident, MB], [bk], inc=False)
                for h in range(4):
                    self.op("pe", lambda e, bk=bk, h=h, sl=sl: e.matmul(bk[:, h * 128:(h + 1) * 128], kT[sl][0:66, h, :], QT_[0:66, h, :],
                                                                       start=False, stop=(h == 3)), [kT[sl], QT_], [bk], inc=(h == 3))
            yield
            for mm in range(2):
                bk = self.bank[s2 * 2 + mm]
                P_ = PP[s4][mm]
                self.op("act", lambda e, bk=bk, P_=P_: e.activation(out=P_[:], in_=bk[:], func=AF.Exp, scale=DIL_SCALE), [bk], [P_])
            yield
            po = self.bank[4 + s2]
            for h in range(4):
                for mm in range(2):
                    sl = sls[mm]
                    P_ = PP[s4][mm]
                    self.op("pe", lambda e, h=h, mm=mm, sl=sl, P_=P_: e.matmul(po[:, h * 65:(h + 1) * 65], P_[:, h * 128:(h + 1) * 128],
                                                                              VP[sl][:, h, :], start=(mm == 0), stop=(mm == 1)),
                            [P_, VP[sl]], [po], inc=(h == 3 and mm == 1))
            yield
            po3 = po[:, 0:260].rearrange("p (a b) -> p a b", b=65)
            self.op("act", lambda e: e.activation(out=DN[:].rearrange("p (h d) -> p h d", h=4), in_=po3[:, :, 0:64], func=AF.Copy),
                    [po], [DN])
            self.op("dve", lambda e: e.tensor_copy(col(ST[:, 4:8]), po3[:, :, 64:65]), [po], [ST])
            r0 = row(blk, 128 * b)
            self.dma("act", AP(dsc.tensor, r0 * 768 + gi * 256, [[d * 768, 128], [1, 256]]), DN[:], [DN], [])
            self.dma("pool", AP(dstt.tensor, r0 * 24 + gi * 8, [[d * 24, 128], [1, 8]]), ST[:], [ST], [])

        self.interleave((gen(i) for i in range(len(blocks))), W)

    def mla(self, l):
        zs, osx = self.zs, self.osx
        SMAX = max(self.seqs)
        KT = self.sb([128, 4, SMAX], BF16, "m_KT")
        QT = self.sb([128, 4, SMAX], BF16, "m_QT")
        VP = self.sb([128, SMAX // 128, 4, 65], BF16, "m_VP")
        OUT = self.sb([128, SMAX // 128, 256], BF16, "m_OUT")
        kst = [self.sb([128, 4, 98], BF16, "m_kst%d" % i) for i in range(2)]
        qst = [self.sb([128, 4, 98], BF16, "m_qst%d" % i) for i in range(2)]
        vst = [self.sb([128, 4, 64], BF16, "m_vst%d" % i) for i in range(2)]
        sqk = self.sb([128, 4, 96], F32, "m_sqk")
        sqq = self.sb([128, 4, 96], F32, "m_sqq")
        ssk = self.sb([128, 4], F32, "m_ssk")
        ssq = self.sb([128, 4], F32, "m_ssq")
        wk = self.sb([128, 4], F32, "m_wk")
        rden = self.sb([128, 4], F32, "m_rden")
        PT = [self.sb([128, 512], BF16, "m_PT%d" % i) for i in range(3)]
        for i in range(2):
            self.op("dve", lambda e, i=i: e.memset(kst[i][:], 1.0), [], [kst[i]])
            self.op("pool", lambda e, i=i: e.memset(qst[i][:], 1.0), [], [qst[i]])
        col = lambda a: a.rearrange("p (h o) -> p h o", o=1)
        grp = 0
        for q in range(self.nseq):
            S = self.seqs[q]
            NT = S // 128
            t0 = self.t0[q]

            def loadb(i):
                r0 = t0 + i * 128
                K_, Q_, V_ = kst[i % 2], qst[i % 2], vst[i % 2]
                self.dma("sp", K_[:, :, 0:64], AP(zs.tensor, r0 * ZW + Z_MKV, [[ZW, 128], [128, 4], [1, 64]]), [], [K_])
                self.dma("sp", K_[:, :, 64:96], AP(zs.tensor, r0 * ZW + Z_KPE, [[ZW, 128], [0, 4], [1, 32]]), [], [K_])
                self.dma("sp", V_[:], AP(zs.tensor, r0 * ZW + Z_MKV + 64, [[ZW, 128], [128, 4], [1, 64]]), [], [V_])
                self.dma("sp", Q_[:, :, 0:96], AP(zs.tensor, r0 * ZW + Z_MQ, [[ZW, 128], [96, 4], [1, 96]]), [], [Q_])

            loadb(0)
            for i in range(NT):
                if i + 1 < NT:
                    loadb(i + 1)
                K_, Q_, V_ = kst[i % 2], qst[i % 2], vst[i % 2]
                self.op("dve", lambda e: e.tensor_tensor(sqk[:], K_[:, :, 0:96], K_[:, :, 0:96], ALU.mult), [K_], [sqk])
                self.op("dve", lambda e: e.tensor_reduce(ssk[:], sqk[:], AX.X, ALU.add), [sqk], [ssk])
                self.op("dve", lambda e: e.tensor_scalar(K_[:, :, 97:98], col(ssk[:]), -0.5, None, ALU.mult), [ssk], [K_])
                self.op("act", lambda e: e.activation(out=col(wk[:]), in_=K_[:, :, 97:98], func=AF.Exp, scale=-MLA_SCALE), [K_], [wk])
                self.op("dve", lambda e: e.tensor_tensor(VP[:, i, :, 0:64], V_[:], bcast_last(wk[:], 64), ALU.mult), [V_, wk], [VP])
                self.op("dve", lambda e: e.tensor_copy(VP[:, i, :, 64:65], col(wk[:])), [wk], [VP])
                self.op("pool", lambda e: e.tensor_tensor(sqq[:], Q_[:, :, 0:96], Q_[:, :, 0:96], ALU.mult), [Q_], [sqq])
                self.op("dve", lambda e: e.tensor_reduce(ssq[:], sqq[:], AX.X, ALU.add), [sqq], [ssq])
                self.op("dve", lambda e: e.tensor_scalar(Q_[:, :, 96:97], col(ssq[:]), -0.5, None, ALU.mult), [ssq], [Q_])
                pT = self.ps_bf(6 + i % 2)
                srcs = [K_[:, h, :] for h in range(4)] + [Q_[:, h, :] for h in range(4)]
                for j, a in enumerate(srcs):
                    self.op("pe", lambda e, j=j, a=a: e.transpose(pT[0:98, j, :], a, self.ident[:]),
                            [K_, Q_, self.ident], [pT], inc=(j == 7))
                self.op("act", lambda e: e.activation(out=KT[0:98, :, i * 128:(i + 1) * 128], in_=pT[0:98, 0:4, :], func=AF.Copy),
                        [pT], [KT])
                self.op("dve", lambda e: e.tensor_copy(QT[0:98, :, i * 128:(i + 1) * 128], pT[0:98, 4:8, :]), [pT], [QT])
            NKC, NQG = S // 128, S // 512
            steps = [(h, qg, kc) for h in range(4) for qg in range(NQG) for kc in range(NKC)]

            def st_mm(idx):
                h, qg, kc = steps[idx]
                bk = self.bank[idx % 3]
                self.op("pe", lambda e: e.matmul(bk[:], KT[0:98, h, kc * 128:(kc + 1) * 128], QT[0:98, h, qg * 512:(qg + 1) * 512],
                                                 start=True, stop=True), [KT, QT], [bk])

            st_mm(0)
            for idx, (h, qg, kc) in enumerate(steps):
                if idx + 1 < len(steps):
                    st_mm(idx + 1)
                bk, P_ = self.bank[idx % 3], PT[idx % 3]
                if kc == 0:
                    po = self.bank[4 + grp % 2]
                    grp += 1
                    self.op("dve", lambda e, po=po: e.memset(po[:, 0:260], 0.0), [], [po])
                self.op("act", lambda e, bk=bk, P_=P_: e.activation(out=P_[:], in_=bk[:], func=AF.Exp, scale=MLA_SCALE), [bk], [P_])
                for qb in range(4):
                    self.op("pe", lambda e, qb=qb, P_=P_, po=po, h=h, kc=kc: e.matmul(
                        po[:, qb * 65:(qb + 1) * 65], P_[:, qb * 128:(qb + 1) * 128], VP[:, kc, h, :], start=False,
                        stop=(kc == NKC - 1), skip_group_check=True), [P_, VP], [po], inc=(qb == 3))
                if kc == NKC - 1:
                    po3 = po[:, 0:260].rearrange("p (a b) -> p a b", b=65)
                    self.op("dve", lambda e, po3=po3: e.reciprocal(col(rden[:]), po3[:, :, 64:65]), [po], [rden])
                    self.op("dve", lambda e, po3=po3, qg=qg, h=h: e.tensor_tensor(
                        OUT[:, qg * 4:(qg + 1) * 4, h * 64:(h + 1) * 64], po3[:, :, 0:64], bcast_last(rden[:], 64), ALU.mult),
                        [po, rden], [OUT])
            self.dma("sp", AP(osx.tensor, t0 * D + 768, [[D, 128], [128 * D, NT], [1, 256]]), OUT[:, 0:NT, :], [OUT], [])

    def phase_c1(self, l, xsrc, w_out, modd, x1s, h2s):
        osx, dsc, dstt = self.osx, self.dsc, self.dstt
        wout = self.sb([128, 8, D], BF16, "c_wout")
        for k in range(8):
            self.dma("pool", wout[:, k, :], w_out[l, k * 128:(k + 1) * 128, :], [], [wout])
        rows = self.sb([128, 3, D], F32, "c_rows")
        otl = [self.sb([128, D], BF16, "c_ot%d" % i) for i in range(2)]
        dnl = [self.sb([128, 3, 256], BF16, "c_dn%d" % i) for i in range(2)]
        stl = [self.sb([128, 3, 8], F32, "c_st%d" % i) for i in range(2)]
        xtl = [self.sb([128, D], F32, "c_xt%d" % i) for i in range(2)]
        M_ = [self.sb([128, 4], F32, "c_M%d" % i) for i in range(2)]
        ee_ = [self.sb([128, 3, 4], F32, "c_ee%d" % i) for i in range(2)]
        ww_ = [self.sb([128, 3, 4], F32, "c_ww%d" % i) for i in range(2)]
        wd_ = [self.sb([128, 4], F32, "c_wd%d" % i) for i in range(2)]
        od_ = [self.sb([128, 3, 256], F32, "c_od%d" % i) for i in range(2)]
        oT_ = [self.sb([128, 8, 128], BF16, "c_oT%d" % i) for i in range(2)]
        ss_ = [self.sb([128, 4], F32, "c_ss%d" % i) for i in range(2)]
        sst_ = [self.sb([128, 4], F32, "c_sst%d" % i) for i in range(2)]
        rs_ = [self.sb([128, 4], F32, "c_rs%d" % i) for i in range(2)]
        junk_ = [self.sb([128, D], BF16, "c_junk%d" % i) for i in range(2)]
        t1_ = [self.sb([128, D], F32, "c_t1%d" % i) for i in range(2)]
        x1t = [self.sb([128, D], F32, "c_x1t%d" % i) for i in range(2)]
        hb_ = [self.sb([128, D], BF16, "c_hb%d" % i) for i in range(2)]
        h2T = [self.sb([128, 8, 128], BF16, "c_h2T%d" % i) for i in range(2)]
        tiles = [(q, i) for q in range(self.nseq) for i in range(self.seqs[q] // 128)]

        def load(n):
            q, i = tiles[n]
            r0 = self.t0[q] + i * 128
            s = n % 2
            self.dma("sp", otl[s][:], osx[r0:r0 + 128, :], [], [otl[s]])
            self.dma("sp", dnl[s][:], dsc[r0:r0 + 128, :, :], [], [dnl[s]])
            self.dma("sp", stl[s][:], dstt[r0:r0 + 128, :, :], [], [stl[s]])
            self.dma("sp", xtl[s][:], xsrc[r0:r0 + 128, :], [], [xtl[s]])

        rows_ = [rows, self.sb([128, 3, D], F32, "c_rows1")]
        rowq = [-1, -1]

        def gen(n):
            q, i = tiles[n]
            s = n % 2
            load(n)
            rw = rows_[s]
            if rowq[s] != q:
                self.load_rows(rw, modd, l, q, (2, 3, 4))
                rowq[s] = q
            OT, DN, ST, X, X1, H2T = otl[s], dnl[s], stl[s], xtl[s], x1t[s], h2T[s]
            M, ee, ww, wd, od, oT, ss, sst, rs, junk, t1, hb = (M_[s], ee_[s], ww_[s], wd_[s], od_[s], oT_[s], ss_[s], sst_[s],
                                                                rs_[s], junk_[s], t1_[s], hb_[s])
            r0 = self.t0[q] + i * 128
            yield
            self.op("dve", lambda e: e.tensor_tensor(M[:], ST[:, 0, 0:4], ST[:, 1, 0:4], ALU.max), [ST], [M])
            self.op("dve", lambda e: e.tensor_tensor(M[:], M[:], ST[:, 2, 0:4], ALU.max), [ST, M], [M])
            self.op("dve", lambda e: e.tensor_tensor(ee[:], ST[:, :, 0:4], bcast_mid(M[:], 3), ALU.subtract), [ST, M], [ee])
            self.op("act", lambda e: e.activation(out=ee[:], in_=ee[:], func=AF.Exp), [ee], [ee])
            self.op("dve", lambda e: e.tensor_tensor(ww[:], ee[:], ST[:, :, 4:8], ALU.mult), [ee, ST], [ww])
            self.op("dve", lambda e: e.tensor_tensor(wd[:], ww[:, 0, :], ww[:, 1, :], ALU.add), [ww], [wd])
            self.op("dve", lambda e: e.tensor_tensor(wd[:], wd[:], ww[:, 2, :], ALU.add), [ww, wd], [wd])
            self.op("dve", lambda e: e.reciprocal(wd[:], wd[:]), [wd], [wd])
            self.op("dve", lambda e: e.tensor_tensor(ee[:], ee[:], bcast_mid(wd[:], 3), ALU.mult), [ee, wd], [ee])
            for g in range(3):
                self.op("pool", lambda e, g=g: e.tensor_tensor(od[:, g, :].rearrange("p (h d) -> p h d", h=4),
                                                               DN[:, g, :].rearrange("p (h d) -> p h d", h=4),
                                                               bcast_last(ee[:, g, :], 64), ALU.mult), [DN, ee], [od])
            self.op("pool", lambda e: e.tensor_tensor(od[:, 0, :], od[:, 0, :], od[:, 1, :], ALU.add), [od], [od])
            self.op("pool", lambda e: e.tensor_tensor(OT[:, 512:768], od[:, 0, :], od[:, 2, :], ALU.add), [od], [OT])
            yield
            pT = self.ps_bf(4 * s)
            self.transposes(pT, [OT[:, k * 128:(k + 1) * 128] for k in range(8)], OT)
            self.op("act", lambda e: e.activation(out=oT[:], in_=pT[:], func=AF.Copy), [pT], [oT])
            py = (self.bank[4 * s + 1], self.bank[4 * s + 2])
            for nb in range(2):
                for k in range(8):
                    self.op("pe", lambda e, nb=nb, k=k: e.matmul(py[nb][:], oT[:, k, :], wout[:, k, nb * 512:(nb + 1) * 512],
                                                                start=(k == 0), stop=(k == 7)), [oT, wout], [py[nb]], inc=(k == 7))
            yield
            for nb in range(2):
                self.op("act", lambda e, nb=nb: e.activation(out=junk[:, nb * 512:(nb + 1) * 512], in_=py[nb][:], func=AF.Square,
                                                            accum_out=ss[:, nb:nb + 1]), [py[nb]], [junk, ss])
            self.op("dve", lambda e: e.tensor_tensor(ss[:, 2:3], ss[:, 0:1], ss[:, 1:2], ALU.add), [ss], [ss])
            self.rstd_cols(ss, rs, sst, 2, 3, D)
            for nb in range(2):
                self.op("dve", lambda e, nb=nb: e.scalar_tensor_tensor(t1[:, nb * 512:(nb + 1) * 512], py[nb][:], rs[:, 2:3],
                                                                      rw[:, 0, nb * 512:(nb + 1) * 512], ALU.mult, ALU.mult),
                        [py[nb], rs, rw], [t1])
            self.op("pool", lambda e: e.tensor_tensor(X1[:], t1[:], X[:], ALU.add), [t1, X], [X1])
            self.dma("sp", x1s[r0:r0 + 128, :], X1[:], [X1], [])
            yield
            self.op("act", lambda e: e.activation(out=junk[:], in_=X1[:], func=AF.Square, accum_out=ss[:, 3:4]), [X1], [junk, ss])
            self.rstd_cols(ss, rs, sst, 3, 4, D)
            self.op("dve", lambda e: e.scalar_tensor_tensor(t1[:], X1[:], rs[:, 3:4], rw[:, 1, :], ALU.mult, ALU.mult),
                    [X1, rs, rw], [t1])
            self.op("pool", lambda e: e.tensor_tensor(hb[:], t1[:], rw[:, 2, :], ALU.add), [t1, rw], [hb])
            yield
            pT2 = self.ps_bf(4 * s + 3)
            self.transposes(pT2, [hb[:, k * 128:(k + 1) * 128] for k in range(8)], hb)
            self.op("act", lambda e: e.activation(out=H2T[:], in_=pT2[:], func=AF.Copy), [pT2], [H2T])
            col0 = self.t0[q] + 2 * q + 1 + i * 128
            self.dma("sp", AP(h2s.tensor, col0, [[self.Tp, 128], [128 * self.Tp, 8], [1, 128]]), H2T[:], [H2T], [])


        self.interleave((gen(n) for n in range(len(tiles))), 2)

    def phase_c2a(self, l, w_up, conv_w, conv_b, h2s, vts):
        wup = self.sb([128, 8, 2 * DFF], BF16, "u_wup")
        for k in range(8):
            self.dma("pool", wup[:, k, :], w_up[l, k * 128:(k + 1) * 128, :], [], [wup])
        cw = self.sb([128, 44, 4], F32, "u_cw")
        for j in range(4):
            base = (l * 3 + j) * 2 * DFF if j < 3 else l * 2 * DFF
            tns = conv_w.tensor if j < 3 else conv_b.tensor
            self.dma("sp", cw[:, :, j:j + 1], AP(tns, base, [[1, 128], [128, 44], [1, 1]]), [], [cw], slow=True)
        h2w = [self.sb([128, 8, 512], BF16, "u_h2w%d" % i) for i in range(2)]
        ta = [self.sb([128, 512], F32, "u_ta%d" % i) for i in range(2)]
        tb = [self.sb([128, 512], F32, "u_tb%d" % i) for i in range(2)]
        sa = [self.sb([128, 512], F32, "u_sa%d" % i) for i in range(2)]
        vt = [self.sb([128, 22, 512], BF16, "u_vt%d" % i) for i in range(2)]
        wins = []
        for q in range(self.nseq):
            w0 = 0
            while w0 < self.seqs[q]:
                n = min(WIN, self.seqs[q] - w0)
                wins.append((q, w0, n))
                w0 += n

        def load(wi):
            q, w0, n = wins[wi]
            cb = self.t0[q] + 2 * q + w0
            self.dma("sp", h2w[wi % 2][:, :, 0:n + 2], AP(h2s.tensor, cb, [[self.Tp, 128], [128 * self.Tp, 8], [1, n + 2]]),
                     [], [h2w[wi % 2]])

        load(0)
        it = 0
        for wi, (q, w0, n) in enumerate(wins):
            if wi + 1 < len(wins):
                load(wi + 1)
            H, VT = h2w[wi % 2], vt[wi % 2]
            for j in range(22):
                s = it % 2
                it += 1
                bA, bB = self.bank[2 * s], self.bank[2 * s + 1]
                TA, TB, SA = ta[s], tb[s], sa[s]
                for (bk, ch) in ((bA, j * 128), (bB, DFF + j * 128)):
                    for k in range(8):
                        self.op("pe", lambda e, bk=bk, ch=ch, k=k: e.matmul(bk[:, 0:n + 2], wup[:, k, ch:ch + 128], H[:, k, 0:n + 2],
                                                                           start=(k == 0), stop=(k == 7)), [wup, H], [bk], inc=(k == 7))
                for (bk, T_, c) in ((bA, TA, j), (bB, TB, 22 + j)):
                    self.op("act", lambda e, bk=bk, T_=T_, c=c: e.activation(out=T_[:, 0:n], in_=bk[:, 1:n + 1], func=AF.Identity,
                                                                            bias=cw[:, c, 3:4], scale=cw[:, c, 1:2]), [bk, cw], [T_])
                    self.op("dve", lambda e, bk=bk, T_=T_, c=c: e.scalar_tensor_tensor(T_[:, 0:n], bk[:, 0:n], cw[:, c, 0:1], T_[:, 0:n],
                                                                                      ALU.mult, ALU.add), [bk, cw, T_], [T_])
                    self.op("dve", lambda e, bk=bk, T_=T_, c=c: e.scalar_tensor_tensor(T_[:, 0:n], bk[:, 2:n + 2], cw[:, c, 2:3], T_[:, 0:n],
                                                                                      ALU.mult, ALU.add), [bk, cw, T_], [T_])
                self.op("act", lambda e: e.activation(out=SA[:, 0:n], in_=TA[:, 0:n], func=AF.Silu), [TA], [SA])
                self.op("pool", lambda e, j=j: e.tensor_tensor(VT[:, j, 0:n], SA[:, 0:n], TB[:, 0:n], ALU.mult), [SA, TB], [VT])
            self.dma("sp", AP(vts.tensor, self.t0[q] + w0, [[self.T, 128], [128 * self.T, 22], [1, n]]), VT[:, :, 0:n], [VT], [])

    def phase_c2b(self, l, w_down, modd, x1s, vts, xdst):
        wd = self.sb([128, 22, D], BF16, "w_wd")
        for j in range(22):
            self.dma("pool", wd[:, j, :], w_down[l, j * 128:(j + 1) * 128, :], [], [wd])
        rows_ = [self.sb([128, 1, D], F32, "w_rows%d" % i) for i in range(2)]
        rowq = [-1, -1]
        vwl = [self.sb([128, 22, 512], BF16, "w_vw%d" % i) for i in range(2)]
        x1l = [self.sb([128, D], F32, "w_x1%d" % i) for i in range(2)]
        x2l = [self.sb([128, D], F32, "w_x2%d" % i) for i in range(2)]
        t1_ = [self.sb([128, D], F32, "w_t1%d" % i) for i in range(2)]
        junk = self.sb([128, D], BF16, "w_junk")
        ss_ = [self.sb([128, 4], F32, "w_ss%d" % i) for i in range(2)]
        sst_ = [self.sb([128, 4], F32, "w_sst%d" % i) for i in range(2)]
        rs_ = [self.sb([128, 4], F32, "w_rs%d" % i) for i in range(2)]
        subs = []
        wi = 0
        for q in range(self.nseq):
            w0 = 0
            while w0 < self.seqs[q]:
                n = min(WIN, self.seqs[q] - w0)
                for a in range(0, n, 128):
                    subs.append((q, wi, w0, n, a, min(128, n - a)))
                w0 += n
                wi += 1
        loaded = set()

        def gen(g):
            q, wi, w0, n, a, cnt = subs[g]
            s = g % 2
            VW = vwl[wi % 2]
            if wi not in loaded:
                loaded.add(wi)
                self.dma("sp", VW[:, :, 0:n], AP(vts.tensor, self.t0[q] + w0, [[self.T, 128], [128 * self.T, 22], [1, n]]), [], [VW])
            r0 = self.t0[q] + w0 + a
            X1, X2, t1, ss, sst, rs, rw = x1l[s], x2l[s], t1_[s], ss_[s], sst_[s], rs_[s], rows_[s]
            self.dma("sp", X1[0:cnt, :], x1s[r0:r0 + cnt, :], [], [X1])
            if rowq[s] != q:
                self.load_rows(rw, modd, l, q, (5,))
                rowq[s] = q
            yield
            py = (self.bank[2 * s], self.bank[2 * s + 1])
            for nb in range(2):
                for j in range(22):
                    self.op("pe", lambda e, nb=nb, j=j: e.matmul(py[nb][0:cnt, :], VW[:, j, a:a + cnt], wd[:, j, nb * 512:(nb + 1) * 512],
                                                                start=(j == 0), stop=(j == 21)), [VW, wd], [py[nb]], inc=(j == 21))
            yield
            for nb in range(2):
                self.op("act", lambda e, nb=nb: e.activation(out=junk[0:cnt, nb * 512:(nb + 1) * 512], in_=py[nb][0:cnt, :], func=AF.Square,
                                                            accum_out=ss[0:cnt, nb:nb + 1]), [py[nb]], [ss])
            self.op("dve", lambda e: e.tensor_tensor(ss[0:cnt, 2:3], ss[0:cnt, 0:1], ss[0:cnt, 1:2], ALU.add), [ss], [ss])
            self.op("act", lambda e: e.activation(out=sst[0:cnt, 2:3], in_=ss[0:cnt, 2:3], func=AF.Sqrt, bias=self.epsb[0:cnt, 0:1],
                                                  scale=1.0 / D), [ss, self.epsb], [sst])
            self.op("dve", lambda e: e.reciprocal(rs[0:cnt, 2:3], sst[0:cnt, 2:3]), [sst], [rs])
            for nb in range(2):
                self.op("dve", lambda e, nb=nb: e.scalar_tensor_tensor(t1[0:cnt, nb * 512:(nb + 1) * 512], py[nb][0:cnt, :], rs[0:cnt, 2:3],
                                                                      rw[0:cnt, 0, nb * 512:(nb + 1) * 512], ALU.mult, ALU.mult),
                        [py[nb], rs, rw], [t1])
            self.op("pool", lambda e: e.tensor_tensor(X2[0:cnt, :], t1[0:cnt, :], X1[0:cnt, :], ALU.add), [t1, X1], [X2])
            self.dma("act", xdst[r0:r0 + cnt, :], X2[0:cnt, :], [X2], [])

        self.interleave((gen(g) for g in range(len(subs))), 2)


class _View:
    def __init__(self, ap, res):
        self.ap = ap
        self.res = res

    def __getitem__(self, k):
        return self.ap[k]


_CONST = {}


def _consts():
    if _CONST:
        return _CONST
    f32 = np.float32
    pos = np.arange(4096, dtype=f32)

    def tab(theta, rot):
        half = rot // 2
        inv = np.power(f32(theta), -np.arange(half, dtype=f32) * f32(2.0) / f32(rot)).astype(f32)
        ang = (pos[:, None] * inv[None, :]).astype(f32)
        return np.cos(ang).astype(f32), np.sin(ang).astype(f32)

    rope = np.zeros((4096, RTW), f32)
    c, s = tab(10000.0, 64)
    sc = np.array([1.0] * 4 + [0.125] * 4, f32)
    rope[:, 0:256] = (c[:, None, :] * sc[None, :, None]).reshape(4096, 256)
    rope[:, 256:512] = (s[:, None, :] * sc[None, :, None]).reshape(4096, 256)
    c, s = tab(500000.0, 16)
    rope[:, 512:704] = np.tile(c, (1, 24))
    rope[:, 704:896] = np.tile(s, (1, 24))
    c, s = tab(500000.0, 32)
    rope[:, 896:960] = np.tile(c, (1, 4))
    rope[:, 960:976] = c
    rope[:, 976:1040] = np.tile(s, (1, 4))
    rope[:, 1040:1056] = s
    _CONST["c_rope"] = rope
    _CONST["c_ident"] = np.eye(128, dtype=f32).astype(NPBF)
    a = np.arange(4096, dtype=np.int64)
    m = (a[:, None] * a[None, :]) % 4096
    ang = (2.0 * np.pi / 4096.0) * np.arange(4096, dtype=np.float64)
    ct, st = np.cos(ang).astype(f32).astype(NPBF), np.sin(ang).astype(f32).astype(NPBF)
    _CONST["c_dft"] = np.stack([ct[m], st[m]], 0)
    k = np.arange(64)
    a64 = 2.0 * np.pi * ((k[:, None] * k[None, :]) % 64) / 64.0
    c64, s64 = np.cos(a64).astype(f32), np.sin(a64).astype(f32)
    cc = np.zeros((64, 2, 128), f32)
    cc[:, 0, :] = np.concatenate([c64, c64], 1)
    cc[:, 1, :] = np.concatenate([-s64, -s64], 1)
    _CONST["c_c64"] = cc
    j = np.arange(128, dtype=f32)[:, None]
    cidx = np.arange(128, dtype=f32)[None, :]
    ret = np.zeros((128, 4, 128), f32)
    ret[:, 0, :] = np.maximum(cidx - j, 0)
    ret[:, 1, :] = np.maximum(j - cidx, 0)
    ret[:, 2, :] = (cidx >= j)
    ret[:, 3, :] = (j > cidx)
    _CONST["c_ret"] = ret
    rq = np.zeros((128, 2, 128), f32)
    rq[:, 0, :] = cidx + 1.0
    rq[:, 1, :] = 128.0 - cidx
    _CONST["c_retq"] = rq
    rw = np.zeros((128, 8), f32)
    rw[:, 0:4] = 127.0 - j
    rw[:, 4:8] = j
    _CONST["c_retw"] = rw
    kk = np.arange(128)[:, None]
    qq = np.arange(128)[None, :]
    dm = np.zeros((4, 2, 128, 512), f32)
    for v in range(4):
        for mm in range(2):
            jj = mm * 128 + kk
            ok = (jj - qq >= 0) & (jj - qq <= 128)
            if v & 1:
                ok = ok & (jj >= 64)
            if v & 2:
                ok = ok & (jj < 192)
            dm[v, mm] = np.tile(np.where(ok, 0.0, -1e30).astype(f32), (1, 4))
    dm = dm.astype(NPBF)
    _CONST["c_dmask"] = dm
    return _CONST


_WNAMES = ("w_ada", "b_ada", "norm_pre_mix", "w_in", "ret_decay_fwd", "ret_decay_bwd", "w_fmix", "mla_q_norm", "mla_w_qb",
           "mla_kv_norm", "mla_w_kvb", "w_out", "norm_post_mix", "norm_pre_ffn", "w_up", "conv_w", "conv_b", "w_down",
           "norm_post_ffn")


def run_cores(seq_lists_x, seq_lists_c, weights, seqs, n_layers=2, debug=False, trace=False, stop=1000):
    kb = KB(seqs, n_layers=n_layers, debug=debug)
    kb.stop = stop
    nc = kb.build()
    cst = _consts()
    in_maps = []
    for xs_, cs_ in zip(seq_lists_x, seq_lists_c):
        m = {"x": np.ascontiguousarray(np.concatenate(xs_, 0), dtype=np.float32),
             "cT": np.ascontiguousarray(np.stack(cs_, 1), dtype=np.float32)}
        for k in _WNAMES:
            m[k] = np.ascontiguousarray(weights[k][:n_layers], dtype=np.float32)
        m.update(cst)
        in_maps.append(m)
    res = run_bass_kernel_spmd(nc, in_maps, core_ids=list(range(len(in_maps))), trace=trace)
    return res, kb


def kernel(x_prompt, x_sample, c_prompt, c_sample, **weights):
    x_prompt = np.asarray(x_prompt, np.float32)
    x_sample = np.asarray(x_sample, np.float32)
    c_prompt = np.asarray(c_prompt, np.float32)
    c_sample = np.asarray(c_sample, np.float32)
    weights = {k: np.asarray(v, np.float32) for k, v in weights.items()}
    seqs = [4096, 2048, 2048, 2048, 2048]
    xs_, cs_ = [], []
    for i in range(8):
        xs_.append([x_prompt[i % 4]] + [x_sample[4 * i + j] for j in range(4)])
        cs_.append([c_prompt[i % 4]] + [c_sample[4 * i + j] for j in range(4)])
    res, kb = run_cores(xs_, cs_, weights, seqs)
    y_prompt = np.stack([res.results[i]["y"][0:4096] for i in range(4)], 0)
    y_sample = np.stack([res.results[i]["y"][4096 + 2048 * j:4096 + 2048 * (j + 1)] for i in range(8) for j in range(4)], 0)
    return (np.ascontiguousarray(y_prompt, dtype=np.float32), np.ascontiguousarray(y_sample, dtype=np.float32))
```

```python
import contextlib
import numpy as np
import ml_dtypes
import concourse.bass as bass
import concourse.mybir as mybir
from concourse.bass_utils import run_bass_kernel_spmd
from concourse.ap import AP

F32 = mybir.dt.float32
BF16 = mybir.dt.bfloat16
AF = mybir.ActivationFunctionType
ALU = mybir.AluOpType
AX = mybir.AxisListType
NPBF = ml_dtypes.bfloat16

D = 1024
DIN = 4000
DFF = 2816
ZW = 4512
Z_RQ, Z_RK, Z_RV, Z_RG, Z_FU, Z_DQ, Z_DK, Z_DV = 0, 256, 512, 768, 1024, 1280, 2048, 2816
Z_MQ, Z_MKV, Z_KPE = 3584, 3968, 4480
EPS = 1e-6
RTW = 512 + 384 + 160
DILS = (1, 4, 16)
MLA_SCALE = 96.0 ** -0.5
DIL_SCALE = 64.0 ** -0.5
WIN = 510


class Res:
    __slots__ = ("w", "r", "psum")

    def __init__(self):
        self.w = None
        self.r = []
        self.psum = False


class Buf:
    def __init__(self, h):
        self.h = h
        self.res = Res()

    def __getitem__(self, k):
        return self.h[k]


def bcast_last(ap, n):
    return AP(ap.tensor, ap.offset, [list(a) for a in ap.ap] + [[0, n]])


def bcast_mid(ap, n):
    l = [list(a) for a in ap.ap]
    return AP(ap.tensor, ap.offset, [l[0], [0, n]] + l[1:])


class KB:
    def __init__(self, seqs, n_layers=2, debug=False):
        self.seqs = list(seqs)
        self.nseq = len(seqs)
        self.T = sum(seqs)
        self.L = n_layers
        self.debug = debug
        self.t0 = [sum(seqs[:i]) for i in range(self.nseq)]
        self.Tp = self.T + 2 * self.nseq
        nc = bass.Bass("TRN2", target_bir_lowering=False)
        self.nc = nc
        self.eng = {"pe": nc.tensor, "act": nc.scalar, "dve": nc.vector, "pool": nc.gpsimd, "sp": nc.sync}
        self.sem = {k: nc.alloc_semaphore("s_" + k) for k in ("pe", "act", "dve", "pool")}
        self.cnt = {k: 0 for k in self.sem}
        self.waited = {k: {} for k in self.eng}
        self.ND = 24
        self.dsem = [nc.alloc_semaphore("d%d" % i) for i in range(self.ND)]
        self.dcnt = [0] * self.ND
        self.drr = 0
        self.uid = 0
        self.dram = {}
        self.stack = contextlib.ExitStack()
        self.cut = 0
        self.stop = 1000

    def _wait(self, e, deps):
        best = {}
        for key, val in deps:
            if key == "pe" and e == "pe":
                continue
            if best.get(key, 0) < val:
                best[key] = val
        w = self.waited[e]
        for key, val in best.items():
            if w.get(key, 0) >= val:
                continue
            sem = self.sem[key] if isinstance(key, str) else self.dsem[key[1]]
            self.eng[e].wait_ge(sem, val)
            w[key] = val

    def _deps(self, reads, writes, e=None):
        deps = []
        for r in reads:
            if r.res.w is not None:
                deps.append(r.res.w)
        for w in writes:
            if w.res.w is not None and w.res.w[0] != e:
                deps.append(w.res.w)
            deps.extend(t for t in w.res.r if t[0] != e)
        return deps

    def op(self, e, fn, reads=(), writes=(), inc=True):
        pr = [r for r in reads if r.res.psum]
        deps = []
        if pr:
            deps = [r.res.w for r in pr if r.res.w is not None]
            reads = [r for r in reads if not r.res.psum]
            writes = list(writes) + pr
        self._wait(e, deps + self._deps(reads, writes, e))
        ins = fn(self.eng[e])
        if inc:
            self.cnt[e] += 1
            ins.then_inc(self.sem[e], 1)
            tok = (e, self.cnt[e])
        else:
            tok = (e, self.cnt[e] + 1)
        for r in reads:
            r.res.r.append(tok)
        for w in writes:
            w.res.w = tok
            w.res.r = []
        return ins

    def dma(self, q, out, in_, reads=(), writes=(), slow=False):
        self._wait(q, self._deps(reads, writes, q))
        k = self.drr
        self.drr = (self.drr + 1) % self.ND
        if self.dcnt[k] > 0:
            self._wait(q, [(("d", k), 16 * self.dcnt[k])])
        self.dcnt[k] += 1
        self.eng[q].dma_start(out=out, in_=in_, allow_slow_non_contiguous=slow).then_inc(self.dsem[k], 16)
        tok = (("d", k), 16 * self.dcnt[k])
        for r in reads:
            r.res.r.append(tok)
        for w in writes:
            w.res.w = tok
            w.res.r = []

    def barrier(self):
        toks = [(k, self.cnt[k]) for k in self.cnt if self.cnt[k] > 0]
        toks += [(("d", k), 16 * self.dcnt[k]) for k in range(self.ND) if self.dcnt[k] > 0]
        for e in self.eng:
            self._wait(e, [t for t in toks if t[0] != e])

    def split_add(self, out, a, b, reads, rows=slice(None)):
        c = 320
        self.op("pool", lambda e: e.tensor_tensor(out[rows, 0:c], a[rows, 0:c], b[rows, 0:c], ALU.add), reads, [out])
        self.op("dve", lambda e: e.tensor_tensor(out[rows, c:D], a[rows, c:D], b[rows, c:D], ALU.add), reads, [out])

    def interleave(self, gens, W):
        active = []
        it = iter(gens)
        done = False
        while True:
            while not done and len(active) < W:
                g = next(it, None)
                if g is None:
                    done = True
                    break
                active.append(g)
            if not active:
                break
            for g in list(active):
                try:
                    next(g)
                except StopIteration:
                    active.remove(g)

    def sb(self, shape, dt, name=None):
        self.uid += 1
        nm = "%s_%d" % (name or "sb", self.uid)
        return Buf(self.stack.enter_context(self.nc.sbuf_tensor(nm, list(shape), dt)))

    def begin_phase(self):
        self.gstack = self.stack
        self.stack = contextlib.ExitStack()

    def end_phase(self):
        self.barrier()
        self.stack.close()
        self.stack = self.gstack

    def ps(self, shape, dt, name=None):
        self.uid += 1
        b = Buf(self.nc.alloc_psum_tensor(name or ("ps%d" % self.uid), list(shape), dt))
        b.res.psum = True
        return b

    def din(self, name, shape, dt):
        t = self.nc.dram_tensor(name, list(shape), dt, kind="ExternalInput").ap()
        self.dram[name] = t
        return t

    def dscr(self, name, shape, dt):
        kind = "ExternalOutput" if self.debug else "Internal"
        t = self.nc.dram_tensor(name, list(shape), dt, kind=kind).ap()
        self.dram[name] = t
        return t

    def rstd(self, ss, out, n, tmp):
        self.op("act", lambda e: e.activation(out=tmp[:], in_=ss[:], func=AF.Sqrt, bias=self.epsb[:, 0:1],
                                              scale=1.0 / n), [ss, self.epsb], [tmp])
        self.op("dve", lambda e: e.reciprocal(out[:], tmp[:]), [tmp], [out])

    def transposes(self, pt, src_aps, srcbuf, width=128):
        n = len(src_aps)
        for i, a in enumerate(src_aps):
            self.op("pe", lambda e, i=i, a=a: e.transpose(pt[0:width, i, :], a, self.ident[:]),
                    [srcbuf, self.ident], [pt], inc=(i == n - 1))

    def build(self):
        nc = self.nc
        L, T, NS = self.L, self.T, self.nseq
        x_in = self.din("x", [T, D], F32)
        cT_in = self.din("cT", [D, NS], F32)
        w_ada = self.din("w_ada", [L, D, 6 * D], F32)
        b_ada = self.din("b_ada", [L, 6 * D], F32)
        n_pre_mix = self.din("norm_pre_mix", [L, D], F32)
        w_in = self.din("w_in", [L, D, DIN], F32)
        dec_f = self.din("ret_decay_fwd", [L, 4], F32)
        dec_b = self.din("ret_decay_bwd", [L, 4], F32)
        w_fmix = self.din("w_fmix", [L, 4, 64, 64], F32)
        q_norm = self.din("mla_q_norm", [L, 256], F32)
        w_qb = self.din("mla_w_qb", [L, 256, 384], F32)
        kv_norm = self.din("mla_kv_norm", [L, 128], F32)
        w_kvb = self.din("mla_w_kvb", [L, 128, 512], F32)
        w_out = self.din("w_out", [L, D, D], F32)
        n_post_mix = self.din("norm_post_mix", [L, D], F32)
        n_pre_ffn = self.din("norm_pre_ffn", [L, D], F32)
        w_up = self.din("w_up", [L, D, 2 * DFF], F32)
        conv_w = self.din("conv_w", [L, 3, 2 * DFF], F32)
        conv_b = self.din("conv_b", [L, 2 * DFF], F32)
        w_down = self.din("w_down", [L, DFF, D], F32)
        n_post_ffn = self.din("norm_post_ffn", [L, D], F32)
        ident_in = self.din("c_ident", [128, 128], BF16)
        rope_in = self.din("c_rope", [4096, RTW], F32)
        dft_in = self.din("c_dft", [2, 4096, 4096], BF16)
        c64_in = self.din("c_c64", [64, 2, 128], F32)
        ret_in = self.din("c_ret", [128, 4, 128], F32)
        retq_in = self.din("c_retq", [128, 2, 128], F32)
        retw_in = self.din("c_retw", [128, 8], F32)
        dmask_in = self.din("c_dmask", [4, 2, 128, 512], BF16)
        y_out = self.nc.dram_tensor("y", [T, D], F32, kind="ExternalOutput").ap()
        zs = self.dscr("zs", [T, ZW], BF16)
        osx = self.dscr("os", [T, D], BF16)
        dsc = self.dscr("dsc", [T, 3, 256], BF16)
        dstt = self.dscr("dst", [T, 3, 8], F32)
        xs = self.dscr("xs", [T, D], F32)
        x1s = self.dscr("x1s", [T, D], F32)
        h2s = self.dscr("h2s", [D, self.Tp], BF16)
        vts = self.dscr("vts", [DFF, T], BF16)
        modd = self.dscr("modd", [L, NS, 6 * D], F32)
        self.zs, self.osx, self.dsc, self.dstt = zs, osx, dsc, dstt

        self.ident = self.sb([128, 128], BF16, "ident")
        self.dma("sp", self.ident[:], ident_in[:, :], [], [self.ident])
        self.epsb = self.sb([128, 1], F32, "epsb")
        self.op("dve", lambda e: e.memset(self.epsb[:], EPS), [], [self.epsb])
        self.zero = self.sb([128, 512], BF16, "zero")
        self.op("dve", lambda e: e.memset(self.zero[:], 0.0), [], [self.zero])
        self.bank = [self.ps([128, 512], F32, "bank%d" % i) for i in range(8)]

        self._zero_halo(h2s)

        self.nph = 0

        def run(fn, *a):
            self.nph += 1
            if self.nph > getattr(self, "stop", 1000):
                return
            self.begin_phase()
            fn(*a)
            self.end_phase()

        run(self.prephase, cT_in, w_ada, b_ada, n_pre_mix, n_post_mix, n_pre_ffn, n_post_ffn, modd)
        for l in range(L):
            xsrc = x_in if l == 0 else xs
            xdst = y_out if l == L - 1 else xs
            run(self.phase_a, l, xsrc, w_in, q_norm, w_qb, kv_norm, w_kvb, rope_in, modd)
            run(self.retention, l, dec_f, dec_b, ret_in, retq_in, retw_in)
            run(self.fourier, l, w_fmix, c64_in, dft_in)
            run(self.dilated, l, dmask_in)
            run(self.mla, l)
            run(self.phase_c1, l, xsrc, w_out, modd, x1s, h2s)
            run(self.phase_c2a, l, w_up, conv_w, conv_b, h2s, vts)
            run(self.phase_c2b, l, w_down, modd, x1s, vts, xdst)
        self.barrier()
        return nc

    def _zero_halo(self, h2s):
        for q in range(self.nseq):
            for col in (self.t0[q] + 2 * q, self.t0[q] + 2 * q + self.seqs[q] + 1):
                dst = AP(h2s.tensor, col, [[self.Tp, 128], [128 * self.Tp, 8], [1, 1]])
                src = self.zero[:, 0:8].rearrange("p (a b) -> p a b", b=1)
                self.dma("sp", dst, src, [self.zero], [], slow=True)

    def prephase(self, cT_in, w_ada, b_ada, n_pre_mix, n_post_mix, n_pre_ffn, n_post_ffn, modd):
        NS, L = self.nseq, self.L
        cT = self.sb([128, 8, NS], F32, "cT")
        self.dma("sp", cT[:], cT_in.rearrange("(k p) u -> p k u", p=128), [], [cT], slow=True)
        sg = self.sb([128, 8, NS], F32, "sg")
        cTb = self.sb([128, 8, NS], BF16, "cTb")
        self.op("act", lambda e: e.activation(out=sg[:], in_=cT[:], func=AF.Sigmoid), [cT], [sg])
        self.op("dve", lambda e: e.tensor_tensor(cTb[:], cT[:], sg[:], ALU.mult), [cT, sg], [cTb])
        wa = [self.sb([128, 8, 512], BF16, "wa%d" % i) for i in range(2)]
        mod = self.sb([NS, 6 * D], F32, "mod")
        brow = self.sb([NS, 6 * D], F32, "brow")
        nrm = self.sb([NS, 4, D], F32, "nrm")
        out6 = self.sb([NS, 6, D], F32, "out6")
        for l in range(L):
            self.dma("sp", brow[:], AP(b_ada.tensor, l * 6 * D, [[0, NS], [1, 6 * D]]), [], [brow])
            for i, nt in enumerate((n_pre_mix, n_post_mix, n_pre_ffn, n_post_ffn)):
                self.dma("sp", nrm[:, i, :], AP(nt.tensor, l * D, [[0, NS], [1, D]]), [], [nrm])
            for nb in range(12):
                w = wa[nb % 2]
                self.dma("pool", w[:], w_ada[l, :, nb * 512:(nb + 1) * 512].rearrange("(k p) n -> p k n", p=128),
                         [], [w])
                pb = self.bank[nb % 2]
                for k in range(8):
                    self.op("pe", lambda e, k=k, w=w, pb=pb: e.matmul(pb[0:NS, :], cTb[:, k, :], w[:, k, :],
                                                                     start=(k == 0), stop=(k == 7)),
                            [cTb, w], [pb], inc=(k == 7))
                self.op("dve", lambda e, pb=pb, nb=nb: e.tensor_tensor(mod[:, nb * 512:(nb + 1) * 512], pb[0:NS, :],
                                                                     brow[:, nb * 512:(nb + 1) * 512], ALU.add),
                        [pb, brow], [mod])
            m = lambda i: mod[:, i * D:(i + 1) * D]
            self.op("dve", lambda e: e.scalar_tensor_tensor(out6[:, 0, :], m(1), 1.0, nrm[:, 0, :], ALU.add, ALU.mult),
                    [mod, nrm], [out6])
            self.op("dve", lambda e: e.tensor_copy(out6[:, 1, :], m(0)), [mod], [out6])
            self.op("dve", lambda e: e.tensor_tensor(out6[:, 2, :], m(2), nrm[:, 1, :], ALU.mult), [mod, nrm], [out6])
            self.op("dve", lambda e: e.scalar_tensor_tensor(out6[:, 3, :], m(4), 1.0, nrm[:, 2, :], ALU.add, ALU.mult),
                    [mod, nrm], [out6])
            self.op("dve", lambda e: e.tensor_copy(out6[:, 4, :], m(3)), [mod], [out6])
            self.op("dve", lambda e: e.tensor_tensor(out6[:, 5, :], m(5), nrm[:, 3, :], ALU.mult), [mod, nrm], [out6])
            self.dma("sp", modd[l, :, :], out6[:].rearrange("u a d -> u (a d)"), [out6], [])

    def load_rows(self, buf, modd, l, q, idxs):
        for i, ix in enumerate(idxs):
            src = AP(modd.tensor, (l * self.nseq + q) * 6 * D + ix * D, [[0, 128], [1, D]])
            self.dma("sp", buf[:, i, :], src, [], [buf])

    def phase_a(self, l, xsrc, w_in, q_norm, w_qb, kv_norm, w_kvb, rope_in, modd):
        zs = self.zs
        win = self.sb([128, 8, DIN], BF16, "win")
        for k in range(8):
            self.dma("pool", win[:, k, :], w_in[l, k * 128:(k + 1) * 128, :], [], [win])
        wqb = self.sb([128, 2, 384], BF16, "wqb")
        self.dma("pool", wqb[:], w_qb[l].rearrange("(k p) n -> p k n", p=128), [], [wqb])
        wkvb = self.sb([128, 512], BF16, "wkvb")
        self.dma("pool", wkvb[:], w_kvb[l], [], [wkvb])
        qn = self.sb([128, 256], F32, "qn")
        self.dma("sp", qn[:], AP(q_norm.tensor, l * 256, [[0, 128], [1, 256]]), [], [qn])
        kvn = self.sb([128, 128], F32, "kvn")
        self.dma("sp", kvn[:], AP(kv_norm.tensor, l * 128, [[0, 128], [1, 128]]), [], [kvn])
        rows = self.sb([128, 2, D], F32, "arows")
        xt = [self.sb([128, D], F32, "a_xt%d" % i) for i in range(2)]
        rt = [self.sb([128, RTW], F32, "a_rt%d" % i) for i in range(3)]
        junk = self.sb([128, D], BF16, "a_junk")
        ss = self.sb([128, 4], F32, "a_ss")
        sst = self.sb([128, 4], F32, "a_sst")
        rs = self.sb([128, 4], F32, "a_rs")
        hf = self.sb([128, D], F32, "a_hf")
        hb = self.sb([128, D], BF16, "a_hb")
        hT = [self.sb([128, 8, 128], BF16, "a_hT%d" % i) for i in range(2)]
        zt = [self.sb([128, ZW], BF16, "a_zt%d" % i) for i in range(2)]
        cn = self.sb([128, 384], BF16, "a_cn")
        cnT = self.sb([128, 3, 128], BF16, "a_cnT")
        tmp = [self.sb([128, 256], F32, "a_tmp%d" % i) for i in range(4)]
        pT = self.ps_bf(0)
        pT2 = self.ps_bf(1)
        pz = [self.bank[2], self.bank[3], self.bank[4], self.bank[5]]
        pq = self.bank[6]
        pkv = self.bank[7]

        tiles = [(q, i) for q in range(self.nseq) for i in range(self.seqs[q] // 128)]

        def load_x(n):
            q, i = tiles[n]
            r0 = self.t0[q] + i * 128
            self.dma("sp", xt[n % 2][:], xsrc[r0:r0 + 128, :], [], [xt[n % 2]])

        def load_rt(n):
            q, i = tiles[n]
            self.dma("sp", rt[n % 3][:], rope_in[i * 128:(i + 1) * 128, :], [], [rt[n % 3]])

        def rope(e, buf, x1, x2, cos, sin, w):
            shp = None
            t1, t2, t3, t4 = [t[:, 0:w].rearrange("p (h d) -> p h d", h=x1.shape[1]) for t in tmp]
            self.op(e, lambda g: g.tensor_tensor(t1, x1, cos, ALU.mult), [buf] + rtb, [tmp[0]])
            self.op(e, lambda g: g.tensor_tensor(t2, x2, sin, ALU.mult), [buf] + rtb, [tmp[1]])
            self.op(e, lambda g: g.tensor_tensor(t3, x2, cos, ALU.mult), [buf] + rtb, [tmp[2]])
            self.op(e, lambda g: g.tensor_tensor(t4, x1, sin, ALU.mult), [buf] + rtb, [tmp[3]])
            self.op(e, lambda g: g.tensor_tensor(x1, t1, t2, ALU.subtract), [tmp[0], tmp[1]], [buf])
            self.op(e, lambda g: g.tensor_tensor(x2, t3, t4, ALU.add), [tmp[2], tmp[3]], [buf])

        ss1 = self.sb([128, 4], F32, "a_ss1")
        sst1 = self.sb([128, 4], F32, "a_sst1")
        rs1 = self.sb([128, 4], F32, "a_rs1")
        state = {"q": -1}

        def stage1(n):
            q, i = tiles[n]
            if q != state["q"]:
                self.load_rows(rows, modd, l, q, (0, 1))
                state["q"] = q
            X, HT = xt[n % 2], hT[n % 2]
            self.op("act", lambda e: e.activation(out=junk[:], in_=X[:], func=AF.Square, accum_out=ss1[:, 0:1]),
                    [X], [junk, ss1])
            self.rstd_cols(ss1, rs1, sst1, 0, 1, D)
            self.op("dve", lambda e: e.scalar_tensor_tensor(hf[:], X[:], rs1[:, 0:1], rows[:, 0, :], ALU.mult, ALU.mult),
                    [X, rs1, rows], [hf])
            self.split_add(hb, hf, _Row(rows, 1), [hf, rows])

        def stage1b(n):
            HT = hT[n % 2]
            self.transposes(pT, [hb[:, k * 128:(k + 1) * 128] for k in range(8)], hb)
            self.op("act", lambda e: e.activation(out=HT[:], in_=pT[:], func=AF.Copy), [pT], [HT])

        def mla_pe(n):
            q, i = tiles[n]
            R, Z = rt[n % 3], zt[n % 2]
            rtb.clear()
            rtb.append(R)
            r0 = self.t0[q] + i * 128
            self.transposes(pT2, [cn[:, k * 128:(k + 1) * 128] for k in range(3)], cn)
            self.op("act", lambda e: e.activation(out=cnT[:], in_=pT2[:, 0:3, :], func=AF.Copy), [pT2], [cnT])

        def mla_pe2(n):
            q, i = tiles[n]
            R, Z = rt[n % 3], zt[n % 2]
            rtb.clear()
            rtb.append(R)
            r0 = self.t0[q] + i * 128
            for k in range(2):
                self.op("pe", lambda e, k=k: e.matmul(pq[:, 0:384], cnT[:, k, :], wqb[:, k, :], start=(k == 0),
                                                     stop=(k == 1)), [cnT, wqb], [pq], inc=(k == 1))
            self.op("pe", lambda e: e.matmul(pkv[:, 0:512], cnT[:, 2, :], wkvb[:], start=True, stop=True),
                    [cnT, wkvb], [pkv])
            self.op("act", lambda e: e.activation(out=Z[:, Z_MQ:Z_MQ + 384], in_=pq[:, 0:384], func=AF.Copy),
                    [pq], [Z])
            self.op("act", lambda e: e.activation(out=Z[:, Z_MKV:Z_MKV + 512], in_=pkv[:], func=AF.Copy),
                    [pkv], [Z])
            v = Z[:, Z_MQ:Z_MQ + 384].rearrange("p (h d) -> p h d", h=4)
            c = R[:, 896:960].rearrange("p (h d) -> p h d", h=4)
            s = R[:, 976:1040].rearrange("p (h d) -> p h d", h=4)
            rope("dve", Z, v[:, :, 64:80], v[:, :, 80:96], c, s, 64)
            v = Z[:, Z_KPE:Z_KPE + 32].rearrange("p (h d) -> p h d", h=1)
            c = R[:, 960:976].rearrange("p (h d) -> p h d", h=1)
            s = R[:, 1040:1056].rearrange("p (h d) -> p h d", h=1)
            rope("dve", Z, v[:, :, 0:16], v[:, :, 16:32], c, s, 16)
            self.dma("sp", zs[r0:r0 + 128, :], Z[:], [Z], [])


        def stage2(n, hook, hook1, hook2, hook3):
            q, i = tiles[n]
            R, Z, HT = rt[n % 3], zt[n % 2], hT[n % 2]
            rtb.clear()
            rtb.append(R)
            r0 = self.t0[q] + i * 128
            for nb in range(8):
                c0 = nb * 512
                cw = min(512, DIN - c0)
                pb = pz[nb % 4]
                for k in range(8):
                    self.op("pe", lambda e, k=k, pb=pb, c0=c0, cw=cw: e.matmul(
                        pb[:, 0:cw], HT[:, k, :], win[:, k, c0:c0 + cw], start=(k == 0), stop=(k == 7)),
                        [HT, win], [pb], inc=(k == 7))
                if nb == 1:
                    self.op("act", lambda e, pb=pb: e.activation(out=Z[:, 512:768], in_=pb[:, 0:256], func=AF.Copy),
                            [pb], [Z])
                    self.op("act", lambda e, pb=pb: e.activation(out=Z[:, 768:1024], in_=pb[:, 256:512], func=AF.Silu),
                            [pb], [Z])
                elif nb < 7:
                    if nb in (2, 5):
                        self.op("dve", lambda e, pb=pb, c0=c0: e.tensor_copy(Z[:, c0:c0 + 512], pb[:]), [pb], [Z])
                    else:
                        self.op("act", lambda e, pb=pb, c0=c0: e.activation(out=Z[:, c0:c0 + 512], in_=pb[:], func=AF.Copy),
                                [pb], [Z])
                    if nb == 2:
                        hook1()
                    if nb == 3:
                        hook()
                    if nb == 4:
                        hook3()
                        rtb.clear()
                        rtb.append(R)
                else:
                    self.op("act", lambda e, pb=pb: e.activation(out=junk[:, 0:256], in_=pb[:, 0:256], func=AF.Square,
                                                                accum_out=ss[:, 1:2]), [pb], [junk, ss])
                    self.op("act", lambda e, pb=pb: e.activation(out=junk[:, 256:384], in_=pb[:, 256:384],
                                                                func=AF.Square, accum_out=ss[:, 2:3]), [pb], [junk, ss])
                    self.op("act", lambda e, pb=pb: e.activation(out=Z[:, Z_KPE:Z_KPE + 32], in_=pb[:, 384:416],
                                                                func=AF.Copy), [pb], [Z])
                    self.rstd_cols(ss, rs, sst, 1, 2, 256)
                    self.rstd_cols(ss, rs, sst, 2, 3, 128)
                    self.op("dve", lambda e, pb=pb: e.scalar_tensor_tensor(cn[:, 0:256], pb[:, 0:256], rs[:, 1:2], qn[:],
                                                                          ALU.mult, ALU.mult), [pb, rs, qn], [cn])
                    self.op("dve", lambda e, pb=pb: e.scalar_tensor_tensor(cn[:, 256:384], pb[:, 256:384], rs[:, 2:3],
                                                                          kvn[:], ALU.mult, ALU.mult), [pb, rs, kvn], [cn])
            hook2()
            v = Z[:, 0:512].rearrange("p (h d) -> p h d", h=8)
            c = R[:, 0:256].rearrange("p (h d) -> p h d", h=8)
            s = R[:, 256:512].rearrange("p (h d) -> p h d", h=8)
            rope("pool", Z, v[:, :, 0:32], v[:, :, 32:64], c, s, 256)
            v = Z[:, Z_DQ:Z_DQ + 1536].rearrange("p (h d) -> p h d", h=24)
            c = R[:, 512:704].rearrange("p (h d) -> p h d", h=24)
            s = R[:, 704:896].rearrange("p (h d) -> p h d", h=24)
            rope("dve", Z, v[:, :, 0:8], v[:, :, 8:16], c, s, 192)
        rtb = []
        NTL = len(tiles)
        load_x(0)
        if NTL > 1:
            load_x(1)
        load_rt(0)
        stage1(0)
        stage1b(0)
        if NTL > 1:
            stage1(1)
        nop = lambda: None
        for n in range(NTL):
            if n + 1 < NTL:
                load_rt(n + 1)
            if n + 2 < NTL:
                load_x(n + 2)
            stage2(n,
                   (lambda n=n: stage1b(n + 1)) if n + 1 < NTL else nop,
                   (lambda n=n: mla_pe(n - 1)) if n > 0 else nop,
                   (lambda n=n: stage1(n + 2)) if n + 2 < NTL else nop,
                   (lambda n=n: mla_pe2(n - 1)) if n > 0 else nop)
        mla_pe(NTL - 1)
        mla_pe2(NTL - 1)

    def ps_bf(self, i):
        return _View(self.bank[i].h[:].bitcast(BF16).rearrange("p (a b) -> p a b", b=128), self.bank[i].res)

    def rstd_cols(self, ss, rs, tmp, a, b, n):
        self.op("act", lambda e: e.activation(out=tmp[:, a:b], in_=ss[:, a:b], func=AF.Sqrt, bias=self.epsb[:, 0:1],
                                              scale=1.0 / n), [ss, self.epsb], [tmp])
        self.op("dve", lambda e: e.reciprocal(rs[:, a:b], tmp[:, a:b]), [tmp], [rs])


    def retention(self, l, dec_f, dec_b, ret_in, retq_in, retw_in):
        zs, osx = self.zs, self.osx
        raw8 = self.sb([128, 8], F32, "r_raw8")
        self.dma("sp", raw8[:, 0:4], AP(dec_f.tensor, l * 4, [[0, 128], [1, 4]]), [], [raw8])
        self.dma("sp", raw8[:, 4:8], AP(dec_b.tensor, l * 4, [[0, 128], [1, 4]]), [], [raw8])
        rawp = self.sb([128, 2, 2], F32, "r_rawp")
        for half in range(2):
            for di, dt_ in enumerate((dec_f, dec_b)):
                self.dma("sp", rawp[half * 64:(half + 1) * 64, di, :],
                         AP(dt_.tensor, l * 4 + half, [[0, 64], [2, 2]]), [], [rawp], slow=True)
        lg8 = self.sb([128, 8], F32, "r_lg8")
        lgp = self.sb([128, 4], F32, "r_lgp")

        def logsig(dst, src):
            self.op("act", lambda e: e.activation(out=dst, in_=src, func=AF.Exp, scale=-1.0), [raw8, rawp], [lg8, lgp])
            self.op("dve", lambda e: e.tensor_scalar(dst, dst, 1.0, None, ALU.add), [lg8, lgp], [lg8, lgp])
            self.op("act", lambda e: e.activation(out=dst, in_=dst, func=AF.Ln), [lg8, lgp], [lg8, lgp])
            self.op("dve", lambda e: e.tensor_scalar(dst, dst, -1.0, None, ALU.mult), [lg8, lgp], [lg8, lgp])

        logsig(lg8[:], raw8[:])
        logsig(lgp[:], rawp[:].rearrange("p a b -> p (a b)"))
        rc = self.sb([128, 4, 128], F32, "r_rc")
        self.dma("sp", rc[:], ret_in[:, :, :], [], [rc])
        rq = self.sb([128, 2, 128], F32, "r_rq")
        self.dma("sp", rq[:], retq_in[:, :, :], [], [rq])
        rw = self.sb([128, 8], F32, "r_rw")
        self.dma("sp", rw[:], retw_in[:, :], [], [rw])
        DT = self.sb([128, 4, 128], F32, "r_DT")
        e1 = self.sb([128, 128], F32, "r_e1")
        e2 = self.sb([128, 128], F32, "r_e2")
        for h in range(4):
            self.op("act", lambda e, h=h: e.activation(out=e1[:], in_=rc[:, 0, :], func=AF.Exp, scale=lg8[:, h:h + 1]),
                    [rc, lg8], [e1])
            self.op("act", lambda e, h=h: e.activation(out=e2[:], in_=rc[:, 1, :], func=AF.Exp,
                                                      scale=lg8[:, 4 + h:5 + h]), [rc, lg8], [e2])
            self.op("dve", lambda e: e.tensor_tensor(e1[:], e1[:], rc[:, 2, :], ALU.mult), [e1, rc], [e1])
            self.op("dve", lambda e: e.tensor_tensor(e2[:], e2[:], rc[:, 3, :], ALU.mult), [e2, rc], [e2])
            self.op("dve", lambda e, h=h: e.tensor_tensor(DT[:, h, :], e1[:], e2[:], ALU.add), [e1, e2], [DT])
        QFB = self.sb([128, 2, 2, 128], F32, "r_QFB")
        for di in range(2):
            for pr in range(2):
                self.op("act", lambda e, di=di, pr=pr: e.activation(out=QFB[:, di, pr, :], in_=rq[:, di, :], func=AF.Exp,
                                                                  scale=lgp[:, di * 2 + pr:di * 2 + pr + 1]),
                        [rq, lgp], [QFB])
        WFB = self.sb([128, 8], F32, "r_WFB")
        self.op("dve", lambda e: e.tensor_tensor(WFB[:], rw[:], lg8[:], ALU.mult), [rw, lg8], [WFB])
        self.op("act", lambda e: e.activation(out=WFB[:], in_=WFB[:], func=AF.Exp), [WFB], [WFB])
        dexp = self.sb([128, 4], F32, "r_dexp")
        self.op("act", lambda e: e.activation(out=dexp[:], in_=lgp[:], func=AF.Exp, scale=128.0), [lgp], [dexp])
        DEC = self.sb([128, 4, 64], F32, "r_DEC")
        self.op("dve", lambda e: e.tensor_copy(DEC[:], bcast_last(dexp[:], 64)), [dexp], [DEC])

        if getattr(self, 'cut', 0) == 1:
            return
        NMAX = max(self.seqs) // 128
        kvall = self.sb([128, NMAX, 4, 64], F32, "r_kvall")
        stF = self.sb([128, NMAX, 2, 64], BF16, "r_stF")
        stB = self.sb([128, NMAX, 2, 64], BF16, "r_stB")
        curF = self.sb([128, 2, 64], F32, "r_curF")
        curB = self.sb([128, 2, 64], F32, "r_curB")
        kvt = [self.sb([128, 512], BF16, "r_kvt%d" % i) for i in range(2)]
        kw = [self.sb([128, 2, 256], BF16, "r_kw%d" % i) for i in range(2)]
        qk = [self.sb([128, 1024], BF16, "r_qk%d" % i) for i in range(2)]
        qkT = [self.sb([128, 4, 128], BF16, "r_qkT%d" % i) for i in range(2)]
        qfT = [self.sb([128, 2, 128], BF16, "r_qfT%d" % i) for i in range(2)]
        qbT = [self.sb([128, 2, 128], BF16, "r_qbT%d" % i) for i in range(2)]
        stt = [self.sb([128, 4, 128], BF16, "r_stt%d" % i) for i in range(2)]
        ot = [self.sb([128, 256], BF16, "r_ot%d" % i) for i in range(2)]
        o1 = self.sb([128, 256], F32, "r_o1")
        junk = self.sb([128, 64], BF16, "r_junk")
        ssr = self.sb([128, 4], F32, "r_ssr")
        sst = self.sb([128, 4], F32, "r_sst")
        rr = self.sb([128, 4], F32, "r_rr")

        for q in range(self.nseq):
            S = self.seqs[q]
            N = S // 128
            t0 = self.t0[q]
            def load1(n):
                self.dma("sp", kvt[n % 2][:], zs[t0 + n * 128:t0 + (n + 1) * 128, 256:768], [], [kvt[n % 2]])
            load1(0)
            for n in range(N):
                if n + 1 < N:
                    load1(n + 1)
                KV, KW = kvt[n % 2], kw[n % 2]
                pk = self.bank[(n % 2) * 4]
                k3 = KV[:, 0:256].rearrange("p (h d) -> p h d", h=4)
                for di in range(2):
                    self.op("dve" if di == 0 else "pool", lambda e, di=di: e.tensor_tensor(
                        KW[:, di, :].rearrange("p (h d) -> p h d", h=4), k3, bcast_last(WFB[:, di * 4:di * 4 + 4], 64),
                        ALU.mult), [KV, WFB], [KW])
                for di in range(2):
                    for pr in range(2):
                        c0 = (di * 2 + pr) * 128
                        self.op("pe", lambda e, di=di, pr=pr, c0=c0: e.matmul(
                            pk[:, c0:c0 + 128], KW[:, di, pr * 128:(pr + 1) * 128], KV[:, 256 + pr * 128:256 + (pr + 1) * 128],
                            start=True, stop=True), [KW, KV], [pk], inc=(di == 1 and pr == 1))
                pk3 = pk[:].rearrange("p (a b) -> p a b", a=4)
                self.op("act", lambda e: e.activation(out=kvall[0:64, n, :, :], in_=pk3[0:64, :, 0:64], func=AF.Copy),
                        [pk], [kvall])
                self.op("act", lambda e: e.activation(out=kvall[64:128, n, :, :], in_=pk3[64:128, :, 64:128], func=AF.Copy),
                        [pk], [kvall])
            if getattr(self, 'cut', 0) == 2:
                return
            self.op("dve", lambda e: e.memset(curF[:], 0.0), [], [curF])
            self.op("pool", lambda e: e.memset(curB[:], 0.0), [], [curB])
            for n in range(N):
                self.op("dve", lambda e, n=n: e.tensor_copy(stF[:, n, :, :], curF[:]), [curF], [stF])
                if n < N - 1:
                    self.op("dve", lambda e: e.tensor_tensor(curF[:], curF[:], DEC[:, 0:2, :], ALU.mult), [curF, DEC], [curF])
                    self.op("dve", lambda e, n=n: e.tensor_tensor(curF[:], curF[:], kvall[:, n, 0:2, :], ALU.add),
                            [curF, kvall], [curF])
            for n in range(N - 1, -1, -1):
                self.op("pool", lambda e, n=n: e.tensor_copy(stB[:, n, :, :], curB[:]), [curB], [stB])
                if n > 0:
                    self.op("pool", lambda e: e.tensor_tensor(curB[:], curB[:], DEC[:, 2:4, :], ALU.mult), [curB, DEC], [curB])
                    self.op("pool", lambda e, n=n: e.tensor_tensor(curB[:], curB[:], kvall[:, n, 2:4, :], ALU.add),
                            [curB, kvall], [curB])
            if getattr(self, 'cut', 0) == 3:
                return
            def load2(n):
                self.dma("sp", qk[n % 2][:], zs[t0 + n * 128:t0 + (n + 1) * 128, 0:1024], [], [qk[n % 2]])
            load2(0)
            for n in range(N):
                if n + 1 < N:
                    load2(n + 1)
                s = n % 2
                QK, QKT, QF, QB, STT, OT = qk[s], qkT[s], qfT[s], qbT[s], stt[s], ot[s]
                pT = self.ps_bf(4 * s)
                pstE, pstO = self.bank[4 * s + 1], self.bank[4 * s + 2]
                po = self.bank[4 * s + 3]
                self.transposes(pT, [QK[:, j * 128:(j + 1) * 128] for j in range(4)], QK)
                self.op("act", lambda e: e.activation(out=QKT[:], in_=pT[:, 0:4, :], func=AF.Copy), [pT], [QKT])
                self.op("dve", lambda e: e.tensor_tensor(QF[:], pT[:, 0:2, :], QFB[:, 0, :, :], ALU.mult), [pT, QFB], [QF])
                self.op("pool", lambda e: e.tensor_tensor(QB[:], QKT[:, 0:2, :], QFB[:, 1, :, :], ALU.mult), [QKT, QFB], [QB])
                if self.cut == 4:
                    continue
                for h in range(4):
                    pr, b0 = h // 2, (h % 2) * 64
                    pst = pstE if h % 2 == 0 else pstO
                    self.op("pe", lambda e, h=h, pr=pr, b0=b0, pst=pst: e.matmul(
                        pst[:, pr * 128:(pr + 1) * 128], QKT[b0:b0 + 64, 2 + pr, :], QKT[b0:b0 + 64, pr, :],
                        start=True, stop=True), [QKT], [pst], inc=(h >= 2))
                for par, pst in enumerate((pstE, pstO)):
                    self.op("dve", lambda e, par=par, pst=pst: e.tensor_tensor(
                        STT[:, par::2, :], pst[:, 0:256].rearrange("p (a b) -> p a b", a=2), DT[:, par::2, :], ALU.mult),
                        [pst, DT], [STT])
                if self.cut == 5:
                    continue
                for h in range(4):
                    pr, b0 = h // 2, (h % 2) * 64
                    oc = po[:, h * 64:(h + 1) * 64]
                    self.op("pe", lambda e, h=h, oc=oc: e.matmul(oc, STT[:, h, :], QK[:, 512 + h * 64:512 + (h + 1) * 64],
                                                                 start=True, stop=False), [STT, QK], [po], inc=False)
                    self.op("pe", lambda e, pr=pr, b0=b0, oc=oc: e.matmul(oc, QF[b0:b0 + 64, pr, :], stF[b0:b0 + 64, n, pr, :],
                                                                         start=False, stop=False), [QF, stF], [po], inc=False)
                    self.op("pe", lambda e, pr=pr, b0=b0, oc=oc: e.matmul(oc, QB[b0:b0 + 64, pr, :], stB[b0:b0 + 64, n, pr, :],
                                                                         start=False, stop=True), [QB, stB], [po], inc=True)
                if self.cut == 6:
                    continue
                for h in range(4):
                    self.op("act", lambda e, h=h: e.activation(out=junk[:], in_=po[:, h * 64:(h + 1) * 64], func=AF.Square,
                                                              accum_out=ssr[:, h:h + 1]), [po], [junk, ssr])
                self.rstd_cols(ssr, rr, sst, 0, 4, 64)
                self.op("dve", lambda e: e.tensor_tensor(o1[:].rearrange("p (h d) -> p h d", h=4),
                                                         po[:, 0:256].rearrange("p (h d) -> p h d", h=4),
                                                         bcast_last(rr[:], 64), ALU.mult), [po, rr], [o1])
                self.op("pool", lambda e: e.tensor_tensor(OT[:], o1[:], QK[:, 768:1024], ALU.mult), [o1, QK], [OT])
                self.dma("sp", osx[t0 + n * 128:t0 + (n + 1) * 128, 0:256], OT[:], [OT], [])

    def fourier(self, l, w_fmix, c64_in, dft_in):
        zs, osx = self.zs, self.osx
        wg = self.sb([64, 4, 64], BF16, "f_wg")
        self.dma("pool", wg[:], w_fmix[l].rearrange("g c e -> c g e"), [], [wg])
        c64 = self.sb([64, 2, 128], BF16, "f_c64")
        self.dma("pool", c64[:], c64_in[:, :, :], [], [c64])
        M12 = self.sb([128, 2, 2, 256], BF16, "f_M12")
        self.op("dve", lambda e: e.memset(M12[:], 0.0), [], [M12])
        pm = self.bank[0]
        for cs in range(2):
            for g in range(4):
                c0 = cs * 256 + g * 64
                self.op("pe", lambda e, cs=cs, g=g, c0=c0: e.matmul(pm[:, c0:c0 + 64], c64[:, cs, :], wg[:, g, :],
                                                                   start=True, stop=True), [c64, wg], [pm],
                        inc=(cs == 1 and g == 3))
        for cs in range(2):
            for g in range(4):
                cc, hf = g // 2, g % 2
                c0 = cs * 256 + g * 64
                self.op("act", lambda e, cs=cs, g=g, cc=cc, hf=hf, c0=c0: e.activation(
                    out=M12[hf * 64:(hf + 1) * 64, cs, cc, g * 64:(g + 1) * 64], in_=pm[hf * 64:(hf + 1) * 64, c0:c0 + 64],
                    func=AF.Copy), [pm], [M12])
        SMAX = max(self.seqs)
        groups = []
        for q in range(self.nseq):
            if groups and len(groups[-1]) < 2 and self.seqs[groups[-1][0]] == self.seqs[q] and self.seqs[q] <= 2048:
                groups[-1].append(q)
            else:
                groups.append([q])
        uall = [self.sb([128, SMAX // 128, 256], BF16, "f_uall%d" % i) for i in range(2)]
        PT = [self.sb([128, 2, 2, SMAX if i == 0 else min(SMAX, 2048)], BF16, "f_PT%d" % i) for i in range(2)]
        fo = self.sb([128, SMAX // 128, 256], BF16, "f_fo")
        dtile = [self.sb([128, 512], BF16, "f_dt%d" % i) for i in range(3)]
        ndt = 0
        nev = 0
        for grp in groups:
            S = self.seqs[grp[0]]
            NT = S // 128
            f = 4096 // S
            nrm = 1.0 / np.sqrt(S * 64.0)
            for gi, q in enumerate(grp):
                src = AP(zs.tensor, self.t0[q] * ZW + Z_FU, [[ZW, 128], [128 * ZW, NT], [1, 256]])
                self.dma("sp", uall[gi][:, 0:NT, :], src, [], [uall[gi]])
            it = 0
            for kb in range(S // 512):
                for cs in range(2):
                    bs = (it % 2) * 4
                    it += 1
                    for sc in range(NT):
                        dtb = dtile[ndt % 3]
                        ndt += 1
                        src = AP(dft_in.tensor, cs * 4096 * 4096 + sc * 128 * f * 4096 + kb * 512, [[f * 4096, 128], [1, 512]])
                        self.dma("sp", dtb[:], src, [], [dtb])
                        for gi, q in enumerate(grp):
                            for cc in range(2):
                                bk = self.bank[bs + gi * 2 + cc]
                                self.op("pe", lambda e, gi=gi, cc=cc, bk=bk, sc=sc, dtb=dtb: e.matmul(
                                    bk[:], uall[gi][:, sc, cc * 128:(cc + 1) * 128], dtb[:], start=(sc == 0), stop=(sc == NT - 1)),
                                    [uall[gi], dtb], [bk], inc=(sc == NT - 1 or (gi == len(grp) - 1 and cc == 1)))
                    for gi, q in enumerate(grp):
                        for cc in range(2):
                            bk = self.bank[bs + gi * 2 + cc]
                            dst = PT[gi][:, cs, cc, kb * 512:(kb + 1) * 512]
                            if nev % 2 == 0:
                                self.op("act", lambda e, bk=bk, dst=dst: e.activation(out=dst, in_=bk[:], func=AF.Copy, scale=nrm),
                                        [bk], [PT[gi]])
                            else:
                                self.op("dve", lambda e, bk=bk, dst=dst: e.tensor_scalar(dst, bk[:], nrm, None, ALU.mult),
                                        [bk], [PT[gi]])
                            nev += 1
            for gi, q in enumerate(grp):
                for kb2 in range(NT):
                    bk = self.bank[kb2 % 2]
                    i4 = 0
                    for cs in range(2):
                        for cc in range(2):
                            self.op("pe", lambda e, cs=cs, cc=cc, bk=bk, i4=i4: e.matmul(
                                bk[:, 0:256], PT[gi][:, cs, cc, kb2 * 128:(kb2 + 1) * 128], M12[:, cs, cc, :],
                                start=(i4 == 0), stop=(i4 == 3)), [PT[gi], M12], [bk], inc=(i4 == 3))
                            i4 += 1
                    if kb2 % 2 == 0:
                        self.op("act", lambda e, bk=bk: e.activation(out=fo[:, kb2, :], in_=bk[:, 0:256], func=AF.Copy), [bk], [fo])
                    else:
                        self.op("dve", lambda e, bk=bk: e.tensor_copy(fo[:, kb2, :], bk[:, 0:256]), [bk], [fo])
                dst = AP(osx.tensor, self.t0[q] * D + 256, [[D, 128], [128 * D, NT], [1, 256]])
                self.dma("sp", dst, fo[:, 0:NT, :], [fo], [])

    def dilated(self, l, dmask_in):
        zs, dsc, dstt = self.zs, self.dsc, self.dstt
        MB = self.sb([128, 4, 2, 512], BF16, "d_MB")
        self.dma("sp", MB[:], dmask_in.rearrange("v m k q -> k v m q"), [], [MB])
        NSL = 9
        W = 2
        kst = [self.sb([128, 4, 66], BF16, "d_kst%d" % i) for i in range(NSL)]
        vst = [self.sb([128, 4, 64], BF16, "d_vst%d" % i) for i in range(NSL)]
        kT = [self.sb([128, 4, 128], BF16, "d_kT%d" % i) for i in range(NSL)]
        VP = [self.sb([128, 4, 65], BF16, "d_VP%d" % i) for i in range(NSL)]
        for i in range(NSL):
            self.op("dve", lambda e, i=i: e.memset(kst[i][:], 1.0), [], [kst[i]])
            self.op("pool", lambda e, i=i: e.memset(vst[i][:], 0.0), [], [vst[i]])
        NQ = 4
        qst = [self.sb([128, 4, 66], BF16, "d_qst%d" % i) for i in range(NQ)]
        qT = [self.sb([128, 4, 128], BF16, "d_qT%d" % i) for i in range(NQ)]
        for i in range(NQ):
            self.op("pool", lambda e, i=i: e.memset(qst[i][:], 1.0), [], [qst[i]])
        sqk_ = [self.sb([128, 4, 64], F32, "d_sqk%d" % i) for i in range(NSL)]
        sqq_ = [self.sb([128, 4, 64], F32, "d_sqq%d" % i) for i in range(NQ)]
        ssk_ = [self.sb([128, 4], F32, "d_ssk%d" % i) for i in range(NSL)]
        ssq_ = [self.sb([128, 4], F32, "d_ssq%d" % i) for i in range(NQ)]
        wk_ = [self.sb([128, 4], F32, "d_wk%d" % i) for i in range(NSL)]
        PP = [[self.sb([128, 512], BF16, "d_P%d_%d" % (i, m)) for m in range(2)] for i in range(NQ)]
        stat = [self.sb([128, 8], F32, "d_stat%d" % i) for i in range(NQ)]
        dn = [self.sb([128, 256], BF16, "d_dn%d" % i) for i in range(NQ)]
        col = lambda a: a.rearrange("p (h o) -> p h o", o=1)
        pTk, pTq = self.ps_bf(6), self.ps_bf(7)

        blocks = []
        for q in range(self.nseq):
            S = self.seqs[q]
            for gi, d in enumerate(DILS):
                Lq = S // d
                NB = Lq // 128
                for r in range(d):
                    for b in range(NB):
                        blocks.append((q, gi, d, r, b, NB, Lq))
        kslot = {}
        kcount = [0]

        def row(blk, t):
            q, gi, d, r = blk[0:4]
            return self.t0[q] + r + d * t

        def key_load(blk, m):
            q, gi, d, r, b, NB, Lq = blk
            key = (q, gi, r, m)
            if key in kslot:
                return None
            sl = kcount[0] % NSL
            kcount[0] += 1
            kslot[key] = sl
            K_, V_ = kst[sl], vst[sl]
            ta, tb = max(0, 128 * m - 64), min(Lq, 128 * m + 64)
            p0 = ta - (128 * m - 64)
            p1 = p0 + (tb - ta)
            self.dma("sp", K_[p0:p1, :, 0:64], AP(zs.tensor, row(blk, ta) * ZW + Z_DK + gi * 256, [[d * ZW, tb - ta], [64, 4], [1, 64]]),
                     [], [K_])
            self.dma("sp", V_[p0:p1, :, :], AP(zs.tensor, row(blk, ta) * ZW + Z_DV + gi * 256, [[d * ZW, tb - ta], [64, 4], [1, 64]]),
                     [], [V_])
            return sl

        def key_compute(sl):
            K_, V_ = kst[sl], vst[sl]
            sqk, ssk, wk = sqk_[sl], ssk_[sl], wk_[sl]
            self.op("dve", lambda e: e.tensor_tensor(sqk[:], K_[:, :, 0:64], K_[:, :, 0:64], ALU.mult), [K_], [sqk])
            self.op("dve", lambda e: e.tensor_reduce(ssk[:], sqk[:], AX.X, ALU.add), [sqk], [ssk])
            self.op("dve", lambda e: e.tensor_scalar(K_[:, :, 65:66], col(ssk[:]), -0.5, None, ALU.mult), [ssk], [K_])
            self.op("act", lambda e: e.activation(out=col(wk[:]), in_=K_[:, :, 65:66], func=AF.Exp, scale=-DIL_SCALE), [K_], [wk])
            self.op("pool", lambda e: e.tensor_tensor(VP[sl][:, :, 0:64], V_[:], bcast_last(wk[:], 64), ALU.mult), [V_, wk], [VP[sl]])
            self.op("pool", lambda e: e.tensor_copy(VP[sl][:, :, 64:65], col(wk[:])), [wk], [VP[sl]])
            for h in range(4):
                self.op("pe", lambda e, h=h: e.transpose(pTk[0:66, h, :], K_[:, h, :], self.ident[:]), [K_, self.ident], [pTk],
                        inc=(h == 3))
            self.op("act", lambda e: e.activation(out=kT[sl][0:66, :, :], in_=pTk[0:66, 0:4, :], func=AF.Copy), [pTk], [kT[sl]])

        def gen(i):
            blk = blocks[i]
            q, gi, d, r, b, NB, Lq = blk
            s4, s2 = i % NQ, i % 2
            Q_, ST, DN, QT_ = qst[s4], stat[s4], dn[s4], qT[s4]
            sqq, ssq = sqq_[s4], ssq_[s4]
            self.dma("sp", Q_[:, :, 0:64], AP(zs.tensor, row(blk, 128 * b) * ZW + Z_DQ + gi * 256, [[d * ZW, 128], [64, 4], [1, 64]]),
                     [], [Q_])
            new = [key_load(blk, b), key_load(blk, b + 1)]
            yield
            for sl in new:
                if sl is not None:
                    key_compute(sl)
            self.op("pool", lambda e: e.tensor_tensor(sqq[:], Q_[:, :, 0:64], Q_[:, :, 0:64], ALU.mult), [Q_], [sqq])
            self.op("dve", lambda e: e.tensor_reduce(ssq[:], sqq[:], AX.X, ALU.add), [sqq], [ssq])
            self.op("dve", lambda e: e.tensor_scalar(Q_[:, :, 64:65], col(ssq[:]), -0.5, None, ALU.mult), [ssq], [Q_])
            self.op("dve", lambda e: e.tensor_scalar(col(ST[:, 0:4]), Q_[:, :, 64:65], -DIL_SCALE, None, ALU.mult), [Q_], [ST])
            for h in range(4):
                self.op("pe", lambda e, h=h: e.transpose(pTq[0:66, h, :], Q_[:, h, :], self.ident[:]), [Q_, self.ident], [pTq],
                        inc=(h == 3))
            self.op("dve", lambda e: e.tensor_copy(QT_[0:66, :, :], pTq[0:66, 0:4, :]), [pTq], [QT_])
            yield
            var = (1 if b == 0 else 0) + (2 if b == NB - 1 else 0)
            sls = [kslot[(q, gi, r, b + mm)] for mm in range(2)]
            for mm in range(2):
                sl = sls[mm]
                bk = self.bank[s2 * 2 + mm]
                self.op("pe", lambda e, bk=bk, mm=mm: e.matmul(bk[:], self.ident[:], MB[:, var, mm, :], start=True, stop=False),
                        [self.ident, MB], [bk], inc=False)
                for h in range(4):
                    self.op("pe", lambda e, bk=bk, h=h, sl=sl: e.matmul(bk[:, h * 128:(h + 1) * 128], kT[sl][0:66, h, :], QT_[0:66, h, :],
                                                                       start=False, stop=(h == 3)), [kT[sl], QT_], [bk], inc=(h == 3))
            yield
            for mm in range(2):
                bk = self.bank[s2 * 2 + mm]
                P_ = PP[s4][mm]
                self.op("act", lambda e, bk=bk, P_=P_: e.activation(out=P_[:], in_=bk[:], func=AF.Exp, scale=DIL_SCALE), [bk], [P_])
            yield
            po = self.bank[4 + s2]
            for h in range(4):
                for mm in range(2):
                    sl = sls[mm]
                    P_ = PP[s4][mm]
                    self.op("pe", lambda e, h=h, mm=mm, sl=sl, P_=P_: e.matmul(po[:, h * 65:(h + 1) * 65], P_[:, h * 128:(h + 1) * 128],
                                                                              VP[sl][:, h, :], start=(mm == 0), stop=(mm == 1)),
                            [P_, VP[sl]], [po], inc=(h == 3 and mm == 1))
            yield
            po3 = po[:, 0:260].rearrange("p (a b) -> p a b", b=65)
            self.op("act", lambda e: e.activation(out=DN[:].rearrange("p (h d) -> p h d", h=4), in_=po3[:, :, 0:64], func=AF.Copy),
                    [po], [DN])
            self.op("dve", lambda e: e.tensor_copy(col(ST[:, 4:8]), po3[:, :, 64:65]), [po], [ST])
            r0 = row(blk, 128 * b)
            self.dma("act", AP(dsc.tensor, r0 * 768 + gi * 256, [[d * 768, 128], [1, 256]]), DN[:], [DN], [])
            self.dma("pool", AP(dstt.tensor, r0 * 24 + gi * 8, [[d * 24, 128], [1, 8]]), ST[:], [ST], [])

        self.interleave((gen(i) for i in range(len(blocks))), W)

    def mla(self, l):
        zs, osx = self.zs, self.osx
        SMAX = max(self.seqs)
        KT = self.sb([128, 4, SMAX], BF16, "m_KT")
        QT = self.sb([128, 4, SMAX], BF16, "m_QT")
        VP = self.sb([128, SMAX // 128, 4, 65], BF16, "m_VP")
        OUT = self.sb([128, SMAX // 128, 256], BF16, "m_OUT")
        kst = [self.sb([128, 4, 98], BF16, "m_kst%d" % i) for i in range(2)]
        qst = [self.sb([128, 4, 98], BF16, "m_qst%d" % i) for i in range(2)]
        vst = [self.sb([128, 4, 64], BF16, "m_vst%d" % i) for i in range(2)]
        sqk_ = [self.sb([128, 4, 96], F32, "m_sqk%d" % i) for i in range(2)]
        sqq_ = [self.sb([128, 4, 96], F32, "m_sqq%d" % i) for i in range(2)]
        ssk_ = [self.sb([128, 4], F32, "m_ssk%d" % i) for i in range(2)]
        ssq_ = [self.sb([128, 4], F32, "m_ssq%d" % i) for i in range(2)]
        wk_ = [self.sb([128, 4], F32, "m_wk%d" % i) for i in range(2)]
        rden = self.sb([128, 4], F32, "m_rden")
        PT = [self.sb([128, 512], BF16, "m_PT%d" % i) for i in range(3)]
        for i in range(2):
            self.op("dve", lambda e, i=i: e.memset(kst[i][:], 1.0), [], [kst[i]])
            self.op("pool", lambda e, i=i: e.memset(qst[i][:], 1.0), [], [qst[i]])
        col = lambda a: a.rearrange("p (h o) -> p h o", o=1)
        grp = 0
        for q in range(self.nseq):
            S = self.seqs[q]
            NT = S // 128
            t0 = self.t0[q]

            def loadb(i):
                r0 = t0 + i * 128
                K_, Q_, V_ = kst[i % 2], qst[i % 2], vst[i % 2]
                self.dma("sp", K_[:, :, 0:64], AP(zs.tensor, r0 * ZW + Z_MKV, [[ZW, 128], [128, 4], [1, 64]]), [], [K_])
                self.dma("sp", K_[:, :, 64:96], AP(zs.tensor, r0 * ZW + Z_KPE, [[ZW, 128], [0, 4], [1, 32]]), [], [K_])
                self.dma("sp", V_[:], AP(zs.tensor, r0 * ZW + Z_MKV + 64, [[ZW, 128], [128, 4], [1, 64]]), [], [V_])
                self.dma("sp", Q_[:, :, 0:96], AP(zs.tensor, r0 * ZW + Z_MQ, [[ZW, 128], [96, 4], [1, 96]]), [], [Q_])

            def bgen(i):
                loadb(i)
                K_, Q_, V_ = kst[i % 2], qst[i % 2], vst[i % 2]
                sqk, sqq, ssk, ssq, wk = sqk_[i % 2], sqq_[i % 2], ssk_[i % 2], ssq_[i % 2], wk_[i % 2]
                yield
                self.op("dve", lambda e: e.tensor_tensor(sqk[:], K_[:, :, 0:96], K_[:, :, 0:96], ALU.mult), [K_], [sqk])
                self.op("dve", lambda e: e.tensor_reduce(ssk[:], sqk[:], AX.X, ALU.add), [sqk], [ssk])
                self.op("dve", lambda e: e.tensor_scalar(K_[:, :, 97:98], col(ssk[:]), -0.5, None, ALU.mult), [ssk], [K_])
                self.op("act", lambda e: e.activation(out=col(wk[:]), in_=K_[:, :, 97:98], func=AF.Exp, scale=-MLA_SCALE), [K_], [wk])
                self.op("dve", lambda e: e.tensor_tensor(VP[:, i, :, 0:64], V_[:], bcast_last(wk[:], 64), ALU.mult), [V_, wk], [VP])
                self.op("dve", lambda e: e.tensor_copy(VP[:, i, :, 64:65], col(wk[:])), [wk], [VP])
                self.op("pool", lambda e: e.tensor_tensor(sqq[:], Q_[:, :, 0:96], Q_[:, :, 0:96], ALU.mult), [Q_], [sqq])
                self.op("dve", lambda e: e.tensor_reduce(ssq[:], sqq[:], AX.X, ALU.add), [sqq], [ssq])
                self.op("dve", lambda e: e.tensor_scalar(Q_[:, :, 96:97], col(ssq[:]), -0.5, None, ALU.mult), [ssq], [Q_])
                yield
                pT = self.ps_bf(6 + i % 2)
                srcs = [K_[:, h, :] for h in range(4)] + [Q_[:, h, :] for h in range(4)]
                for j, a in enumerate(srcs):
                    self.op("pe", lambda e, j=j, a=a: e.transpose(pT[0:98, j, :], a, self.ident[:]),
                            [K_, Q_, self.ident], [pT], inc=(j == 7))
                self.op("act", lambda e: e.activation(out=KT[0:98, :, i * 128:(i + 1) * 128], in_=pT[0:98, 0:4, :], func=AF.Copy),
                        [pT], [KT])
                self.op("dve", lambda e: e.tensor_copy(QT[0:98, :, i * 128:(i + 1) * 128], pT[0:98, 4:8, :]), [pT], [QT])

            self.interleave((bgen(i) for i in range(NT)), 2)
            NKC, NQG = S // 128, S // 512
            steps = [(h, qg, kc) for h in range(4) for qg in range(NQG) for kc in range(NKC)]

            def st_mm(idx):
                h, qg, kc = steps[idx]
                bk = self.bank[idx % 3]
                self.op("pe", lambda e: e.matmul(bk[:], KT[0:98, h, kc * 128:(kc + 1) * 128], QT[0:98, h, qg * 512:(qg + 1) * 512],
                                                 start=True, stop=True), [KT, QT], [bk])

            st_mm(0)
            for idx, (h, qg, kc) in enumerate(steps):
                if idx + 1 < len(steps):
                    st_mm(idx + 1)
                bk, P_ = self.bank[idx % 3], PT[idx % 3]
                if kc == 0:
                    po = self.bank[4 + grp % 2]
                    grp += 1
                    self.op("dve", lambda e, po=po: e.memset(po[:, 0:260], 0.0), [], [po])
                self.op("act", lambda e, bk=bk, P_=P_: e.activation(out=P_[:], in_=bk[:], func=AF.Exp, scale=MLA_SCALE), [bk], [P_])
                for qb in range(4):
                    self.op("pe", lambda e, qb=qb, P_=P_, po=po, h=h, kc=kc: e.matmul(
                        po[:, qb * 65:(qb + 1) * 65], P_[:, qb * 128:(qb + 1) * 128], VP[:, kc, h, :], start=False,
                        stop=(kc == NKC - 1), skip_group_check=True), [P_, VP], [po], inc=(qb == 3))
                if kc == NKC - 1:
                    po3 = po[:, 0:260].rearrange("p (a b) -> p a b", b=65)
                    self.op("dve", lambda e, po3=po3: e.reciprocal(col(rden[:]), po3[:, :, 64:65]), [po], [rden])
                    self.op("dve", lambda e, po3=po3, qg=qg, h=h: e.tensor_tensor(
                        OUT[:, qg * 4:(qg + 1) * 4, h * 64:(h + 1) * 64], po3[:, :, 0:64], bcast_last(rden[:], 64), ALU.mult),
                        [po, rden], [OUT])
            self.dma("sp", AP(osx.tensor, t0 * D + 768, [[D, 128], [128 * D, NT], [1, 256]]), OUT[:, 0:NT, :], [OUT], [])

    def phase_c1(self, l, xsrc, w_out, modd, x1s, h2s):
        osx, dsc, dstt = self.osx, self.dsc, self.dstt
        wout = self.sb([128, 8, D], BF16, "c_wout")
        for k in range(8):
            self.dma("pool", wout[:, k, :], w_out[l, k * 128:(k + 1) * 128, :], [], [wout])
        rows = self.sb([128, 3, D], F32, "c_rows")
        otl = [self.sb([128, D], BF16, "c_ot%d" % i) for i in range(2)]
        dnl = [self.sb([128, 3, 256], BF16, "c_dn%d" % i) for i in range(2)]
        stl = [self.sb([128, 3, 8], F32, "c_st%d" % i) for i in range(2)]
        xtl = [self.sb([128, D], F32, "c_xt%d" % i) for i in range(2)]
        M_ = [self.sb([128, 4], F32, "c_M%d" % i) for i in range(2)]
        ee_ = [self.sb([128, 3, 4], F32, "c_ee%d" % i) for i in range(2)]
        ww_ = [self.sb([128, 3, 4], F32, "c_ww%d" % i) for i in range(2)]
        wd_ = [self.sb([128, 4], F32, "c_wd%d" % i) for i in range(2)]
        od_ = [self.sb([128, 3, 256], F32, "c_od%d" % i) for i in range(2)]
        oT_ = [self.sb([128, 8, 128], BF16, "c_oT%d" % i) for i in range(2)]
        ss_ = [self.sb([128, 4], F32, "c_ss%d" % i) for i in range(2)]
        sst_ = [self.sb([128, 4], F32, "c_sst%d" % i) for i in range(2)]
        rs_ = [self.sb([128, 4], F32, "c_rs%d" % i) for i in range(2)]
        junk_ = [self.sb([128, D], BF16, "c_junk%d" % i) for i in range(2)]
        t1_ = [self.sb([128, D], F32, "c_t1%d" % i) for i in range(2)]
        x1t = [self.sb([128, D], F32, "c_x1t%d" % i) for i in range(2)]
        hb_ = [self.sb([128, D], BF16, "c_hb%d" % i) for i in range(2)]
        h2T = [self.sb([128, 8, 128], BF16, "c_h2T%d" % i) for i in range(2)]
        tiles = [(q, i) for q in range(self.nseq) for i in range(self.seqs[q] // 128)]

        def load(n):
            q, i = tiles[n]
            r0 = self.t0[q] + i * 128
            s = n % 2
            self.dma("sp", otl[s][:], osx[r0:r0 + 128, :], [], [otl[s]])
            self.dma("sp", dnl[s][:], dsc[r0:r0 + 128, :, :], [], [dnl[s]])
            self.dma("sp", stl[s][:], dstt[r0:r0 + 128, :, :], [], [stl[s]])
            self.dma("sp", xtl[s][:], xsrc[r0:r0 + 128, :], [], [xtl[s]])

        rows_ = [rows, self.sb([128, 3, D], F32, "c_rows1")]
        rowq = [-1, -1]

        def gen(n):
            q, i = tiles[n]
            s = n % 2
            load(n)
            rw = rows_[s]
            if rowq[s] != q:
                self.load_rows(rw, modd, l, q, (2, 3, 4))
                rowq[s] = q
            OT, DN, ST, X, X1, H2T = otl[s], dnl[s], stl[s], xtl[s], x1t[s], h2T[s]
            M, ee, ww, wd, od, oT, ss, sst, rs, junk, t1, hb = (M_[s], ee_[s], ww_[s], wd_[s], od_[s], oT_[s], ss_[s], sst_[s],
                                                                rs_[s], junk_[s], t1_[s], hb_[s])
            r0 = self.t0[q] + i * 128
            yield
            self.op("dve", lambda e: e.tensor_tensor(M[:], ST[:, 0, 0:4], ST[:, 1, 0:4], ALU.max), [ST], [M])
            self.op("dve", lambda e: e.tensor_tensor(M[:], M[:], ST[:, 2, 0:4], ALU.max), [ST, M], [M])
            self.op("dve", lambda e: e.tensor_tensor(ee[:], ST[:, :, 0:4], bcast_mid(M[:], 3), ALU.subtract), [ST, M], [ee])
            self.op("act", lambda e: e.activation(out=ee[:], in_=ee[:], func=AF.Exp), [ee], [ee])
            self.op("dve", lambda e: e.tensor_tensor(ww[:], ee[:], ST[:, :, 4:8], ALU.mult), [ee, ST], [ww])
            self.op("dve", lambda e: e.tensor_tensor(wd[:], ww[:, 0, :], ww[:, 1, :], ALU.add), [ww], [wd])
            self.op("dve", lambda e: e.tensor_tensor(wd[:], wd[:], ww[:, 2, :], ALU.add), [ww, wd], [wd])
            self.op("dve", lambda e: e.reciprocal(wd[:], wd[:]), [wd], [wd])
            self.op("dve", lambda e: e.tensor_tensor(ee[:], ee[:], bcast_mid(wd[:], 3), ALU.mult), [ee, wd], [ee])
            for g in range(3):
                self.op("pool", lambda e, g=g: e.tensor_tensor(od[:, g, :].rearrange("p (h d) -> p h d", h=4),
                                                               DN[:, g, :].rearrange("p (h d) -> p h d", h=4),
                                                               bcast_last(ee[:, g, :], 64), ALU.mult), [DN, ee], [od])
            self.op("pool", lambda e: e.tensor_tensor(od[:, 0, :], od[:, 0, :], od[:, 1, :], ALU.add), [od], [od])
            self.op("pool", lambda e: e.tensor_tensor(OT[:, 512:768], od[:, 0, :], od[:, 2, :], ALU.add), [od], [OT])
            yield
            pT = self.ps_bf(4 * s)
            self.transposes(pT, [OT[:, k * 128:(k + 1) * 128] for k in range(8)], OT)
            self.op("act", lambda e: e.activation(out=oT[:], in_=pT[:], func=AF.Copy), [pT], [oT])
            py = (self.bank[4 * s + 1], self.bank[4 * s + 2])
            for nb in range(2):
                for k in range(8):
                    self.op("pe", lambda e, nb=nb, k=k: e.matmul(py[nb][:], oT[:, k, :], wout[:, k, nb * 512:(nb + 1) * 512],
                                                                start=(k == 0), stop=(k == 7)), [oT, wout], [py[nb]], inc=(k == 7))
            yield
            for nb in range(2):
                self.op("act", lambda e, nb=nb: e.activation(out=junk[:, nb * 512:(nb + 1) * 512], in_=py[nb][:], func=AF.Square,
                                                            accum_out=ss[:, nb:nb + 1]), [py[nb]], [junk, ss])
            self.op("dve", lambda e: e.tensor_tensor(ss[:, 2:3], ss[:, 0:1], ss[:, 1:2], ALU.add), [ss], [ss])
            self.rstd_cols(ss, rs, sst, 2, 3, D)
            for nb in range(2):
                self.op("dve", lambda e, nb=nb: e.scalar_tensor_tensor(t1[:, nb * 512:(nb + 1) * 512], py[nb][:], rs[:, 2:3],
                                                                      rw[:, 0, nb * 512:(nb + 1) * 512], ALU.mult, ALU.mult),
                        [py[nb], rs, rw], [t1])
            self.split_add(X1, t1, X, [t1, X])
            self.dma("sp", x1s[r0:r0 + 128, :], X1[:], [X1], [])
            yield
            self.op("act", lambda e: e.activation(out=junk[:], in_=X1[:], func=AF.Square, accum_out=ss[:, 3:4]), [X1], [junk, ss])
            self.rstd_cols(ss, rs, sst, 3, 4, D)
            self.op("dve", lambda e: e.scalar_tensor_tensor(t1[:], X1[:], rs[:, 3:4], rw[:, 1, :], ALU.mult, ALU.mult),
                    [X1, rs, rw], [t1])
            self.split_add(hb, t1, _Row(rw, 2), [t1, rw])
            yield
            pT2 = self.ps_bf(4 * s + 3)
            self.transposes(pT2, [hb[:, k * 128:(k + 1) * 128] for k in range(8)], hb)
            self.op("act", lambda e: e.activation(out=H2T[:], in_=pT2[:], func=AF.Copy), [pT2], [H2T])
            col0 = self.t0[q] + 2 * q + 1 + i * 128
            self.dma("sp", AP(h2s.tensor, col0, [[self.Tp, 128], [128 * self.Tp, 8], [1, 128]]), H2T[:], [H2T], [])


        self.interleave((gen(n) for n in range(len(tiles))), 2)

    def phase_c2a(self, l, w_up, conv_w, conv_b, h2s, vts):
        wup = self.sb([128, 8, 2 * DFF], BF16, "u_wup")
        for k in range(8):
            self.dma("pool", wup[:, k, :], w_up[l, k * 128:(k + 1) * 128, :], [], [wup])
        cw = self.sb([128, 44, 4], F32, "u_cw")
        for j in range(4):
            base = (l * 3 + j) * 2 * DFF if j < 3 else l * 2 * DFF
            tns = conv_w.tensor if j < 3 else conv_b.tensor
            self.dma("sp", cw[:, :, j:j + 1], AP(tns, base, [[1, 128], [128, 44], [1, 1]]), [], [cw], slow=True)
        h2w = [self.sb([128, 8, 512], BF16, "u_h2w%d" % i) for i in range(2)]
        ta = [self.sb([128, 512], F32, "u_ta%d" % i) for i in range(2)]
        tb = [self.sb([128, 512], F32, "u_tb%d" % i) for i in range(2)]
        sa = [self.sb([128, 512], F32, "u_sa%d" % i) for i in range(2)]
        vt = [self.sb([128, 22, 512], BF16, "u_vt%d" % i) for i in range(2)]
        wins = []
        for q in range(self.nseq):
            w0 = 0
            while w0 < self.seqs[q]:
                n = min(WIN, self.seqs[q] - w0)
                wins.append((q, w0, n))
                w0 += n

        def load(wi):
            q, w0, n = wins[wi]
            cb = self.t0[q] + 2 * q + w0
            self.dma("sp", h2w[wi % 2][:, :, 0:n + 2], AP(h2s.tensor, cb, [[self.Tp, 128], [128 * self.Tp, 8], [1, n + 2]]),
                     [], [h2w[wi % 2]])

        load(0)
        it = 0
        for wi, (q, w0, n) in enumerate(wins):
            if wi + 1 < len(wins):
                load(wi + 1)
            H, VT = h2w[wi % 2], vt[wi % 2]
            for j in range(22):
                s = it % 2
                it += 1
                bA, bB = self.bank[2 * s], self.bank[2 * s + 1]
                TA, TB, SA = ta[s], tb[s], sa[s]
                for (bk, ch) in ((bA, j * 128), (bB, DFF + j * 128)):
                    for k in range(8):
                        self.op("pe", lambda e, bk=bk, ch=ch, k=k: e.matmul(bk[:, 0:n + 2], wup[:, k, ch:ch + 128], H[:, k, 0:n + 2],
                                                                           start=(k == 0), stop=(k == 7)), [wup, H], [bk], inc=(k == 7))
                for (bk, T_, c) in ((bA, TA, j), (bB, TB, 22 + j)):
                    self.op("act", lambda e, bk=bk, T_=T_, c=c: e.activation(out=T_[:, 0:n], in_=bk[:, 1:n + 1], func=AF.Identity,
                                                                            bias=cw[:, c, 3:4], scale=cw[:, c, 1:2]), [bk, cw], [T_])
                    self.op("dve", lambda e, bk=bk, T_=T_, c=c: e.scalar_tensor_tensor(T_[:, 0:n], bk[:, 0:n], cw[:, c, 0:1], T_[:, 0:n],
                                                                                      ALU.mult, ALU.add), [bk, cw, T_], [T_])
                    self.op("dve", lambda e, bk=bk, T_=T_, c=c: e.scalar_tensor_tensor(T_[:, 0:n], bk[:, 2:n + 2], cw[:, c, 2:3], T_[:, 0:n],
                                                                                      ALU.mult, ALU.add), [bk, cw, T_], [T_])
                self.op("act", lambda e: e.activation(out=SA[:, 0:n], in_=TA[:, 0:n], func=AF.Silu), [TA], [SA])
                self.op("pool", lambda e, j=j: e.tensor_tensor(VT[:, j, 0:n], SA[:, 0:n], TB[:, 0:n], ALU.mult), [SA, TB], [VT])
            self.dma("sp", AP(vts.tensor, self.t0[q] + w0, [[self.T, 128], [128 * self.T, 22], [1, n]]), VT[:, :, 0:n], [VT], [])

    def phase_c2b(self, l, w_down, modd, x1s, vts, xdst):
        wd = self.sb([128, 22, D], BF16, "w_wd")
        for j in range(22):
            self.dma("pool", wd[:, j, :], w_down[l, j * 128:(j + 1) * 128, :], [], [wd])
        rows_ = [self.sb([128, 1, D], F32, "w_rows%d" % i) for i in range(2)]
        rowq = [-1, -1]
        vwl = [self.sb([128, 22, 512], BF16, "w_vw%d" % i) for i in range(2)]
        x1l = [self.sb([128, D], F32, "w_x1%d" % i) for i in range(2)]
        x2l = [self.sb([128, D], F32, "w_x2%d" % i) for i in range(2)]
        t1_ = [self.sb([128, D], F32, "w_t1%d" % i) for i in range(2)]
        junk = self.sb([128, D], BF16, "w_junk")
        ss_ = [self.sb([128, 4], F32, "w_ss%d" % i) for i in range(2)]
        sst_ = [self.sb([128, 4], F32, "w_sst%d" % i) for i in range(2)]
        rs_ = [self.sb([128, 4], F32, "w_rs%d" % i) for i in range(2)]
        subs = []
        wi = 0
        for q in range(self.nseq):
            w0 = 0
            while w0 < self.seqs[q]:
                n = min(WIN, self.seqs[q] - w0)
                for a in range(0, n, 128):
                    subs.append((q, wi, w0, n, a, min(128, n - a)))
                w0 += n
                wi += 1
        loaded = set()

        def gen(g):
            q, wi, w0, n, a, cnt = subs[g]
            s = g % 2
            VW = vwl[wi % 2]
            if wi not in loaded:
                loaded.add(wi)
                self.dma("sp", VW[:, :, 0:n], AP(vts.tensor, self.t0[q] + w0, [[self.T, 128], [128 * self.T, 22], [1, n]]), [], [VW])
            r0 = self.t0[q] + w0 + a
            X1, X2, t1, ss, sst, rs, rw = x1l[s], x2l[s], t1_[s], ss_[s], sst_[s], rs_[s], rows_[s]
            self.dma("sp", X1[0:cnt, :], x1s[r0:r0 + cnt, :], [], [X1])
            if rowq[s] != q:
                self.load_rows(rw, modd, l, q, (5,))
                rowq[s] = q
            yield
            py = (self.bank[2 * s], self.bank[2 * s + 1])
            for nb in range(2):
                for j in range(22):
                    self.op("pe", lambda e, nb=nb, j=j: e.matmul(py[nb][0:cnt, :], VW[:, j, a:a + cnt], wd[:, j, nb * 512:(nb + 1) * 512],
                                                                start=(j == 0), stop=(j == 21)), [VW, wd], [py[nb]], inc=(j == 21))
            yield
            for nb in range(2):
                self.op("act", lambda e, nb=nb: e.activation(out=junk[0:cnt, nb * 512:(nb + 1) * 512], in_=py[nb][0:cnt, :], func=AF.Square,
                                                            accum_out=ss[0:cnt, nb:nb + 1]), [py[nb]], [ss])
            self.op("dve", lambda e: e.tensor_tensor(ss[0:cnt, 2:3], ss[0:cnt, 0:1], ss[0:cnt, 1:2], ALU.add), [ss], [ss])
            self.op("act", lambda e: e.activation(out=sst[0:cnt, 2:3], in_=ss[0:cnt, 2:3], func=AF.Sqrt, bias=self.epsb[0:cnt, 0:1],
                                                  scale=1.0 / D), [ss, self.epsb], [sst])
            self.op("dve", lambda e: e.reciprocal(rs[0:cnt, 2:3], sst[0:cnt, 2:3]), [sst], [rs])
            for nb in range(2):
                self.op("dve", lambda e, nb=nb: e.scalar_tensor_tensor(t1[0:cnt, nb * 512:(nb + 1) * 512], py[nb][0:cnt, :], rs[0:cnt, 2:3],
                                                                      rw[0:cnt, 0, nb * 512:(nb + 1) * 512], ALU.mult, ALU.mult),
                        [py[nb], rs, rw], [t1])
            self.split_add(X2, t1, X1, [t1, X1], rows=slice(0, cnt))
            self.dma("act", xdst[r0:r0 + cnt, :], X2[0:cnt, :], [X2], [])

        self.interleave((gen(g) for g in range(len(subs))), 2)


class _Row:
    def __init__(self, buf, i):
        self.buf = buf
        self.i = i

    def __getitem__(self, k):
        return self.buf[k[0], self.i, k[1]]


class _View:
    def __init__(self, ap, res):
        self.ap = ap
        self.res = res

    def __getitem__(self, k):
        return self.ap[k]


_CONST = {}


def _consts():
    if _CONST:
        return _CONST
    f32 = np.float32
    pos = np.arange(4096, dtype=f32)

    def tab(theta, rot):
        half = rot // 2
        inv = np.power(f32(theta), -np.arange(half, dtype=f32) * f32(2.0) / f32(rot)).astype(f32)
        ang = (pos[:, None] * inv[None, :]).astype(f32)
        return np.cos(ang).astype(f32), np.sin(ang).astype(f32)

    rope = np.zeros((4096, RTW), f32)
    c, s = tab(10000.0, 64)
    sc = np.array([1.0] * 4 + [0.125] * 4, f32)
    rope[:, 0:256] = (c[:, None, :] * sc[None, :, None]).reshape(4096, 256)
    rope[:, 256:512] = (s[:, None, :] * sc[None, :, None]).reshape(4096, 256)
    c, s = tab(500000.0, 16)
    rope[:, 512:704] = np.tile(c, (1, 24))
    rope[:, 704:896] = np.tile(s, (1, 24))
    c, s = tab(500000.0, 32)
    rope[:, 896:960] = np.tile(c, (1, 4))
    rope[:, 960:976] = c
    rope[:, 976:1040] = np.tile(s, (1, 4))
    rope[:, 1040:1056] = s
    _CONST["c_rope"] = rope
    _CONST["c_ident"] = np.eye(128, dtype=f32).astype(NPBF)
    a = np.arange(4096, dtype=np.int64)
    m = (a[:, None] * a[None, :]) % 4096
    ang = (2.0 * np.pi / 4096.0) * np.arange(4096, dtype=np.float64)
    ct, st = np.cos(ang).astype(f32).astype(NPBF), np.sin(ang).astype(f32).astype(NPBF)
    _CONST["c_dft"] = np.stack([ct[m], st[m]], 0)
    k = np.arange(64)
    a64 = 2.0 * np.pi * ((k[:, None] * k[None, :]) % 64) / 64.0
    c64, s64 = np.cos(a64).astype(f32), np.sin(a64).astype(f32)
    cc = np.zeros((64, 2, 128), f32)
    cc[:, 0, :] = np.concatenate([c64, c64], 1)
    cc[:, 1, :] = np.concatenate([-s64, -s64], 1)
    _CONST["c_c64"] = cc
    j = np.arange(128, dtype=f32)[:, None]
    cidx = np.arange(128, dtype=f32)[None, :]
    ret = np.zeros((128, 4, 128), f32)
    ret[:, 0, :] = np.maximum(cidx - j, 0)
    ret[:, 1, :] = np.maximum(j - cidx, 0)
    ret[:, 2, :] = (cidx >= j)
    ret[:, 3, :] = (j > cidx)
    _CONST["c_ret"] = ret
    rq = np.zeros((128, 2, 128), f32)
    rq[:, 0, :] = cidx + 1.0
    rq[:, 1, :] = 128.0 - cidx
    _CONST["c_retq"] = rq
    rw = np.zeros((128, 8), f32)
    rw[:, 0:4] = 127.0 - j
    rw[:, 4:8] = j
    _CONST["c_retw"] = rw
    kk = np.arange(128)[:, None]
    qq = np.arange(128)[None, :]
    dm = np.zeros((4, 2, 128, 512), f32)
    for v in range(4):
        for mm in range(2):
            jj = mm * 128 + kk
            ok = (jj - qq >= 0) & (jj - qq <= 128)
            if v & 1:
                ok = ok & (jj >= 64)
            if v & 2:
                ok = ok & (jj < 192)
            dm[v, mm] = np.tile(np.where(ok, 0.0, -1e30).astype(f32), (1, 4))
    dm = dm.astype(NPBF)
    _CONST["c_dmask"] = dm
    return _CONST


_WNAMES = ("w_ada", "b_ada", "norm_pre_mix", "w_in", "ret_decay_fwd", "ret_decay_bwd", "w_fmix", "mla_q_norm", "mla_w_qb",
           "mla_kv_norm", "mla_w_kvb", "w_out", "norm_post_mix", "norm_pre_ffn", "w_up", "conv_w", "conv_b", "w_down",
           "norm_post_ffn")


def run_cores(seq_lists_x, seq_lists_c, weights, seqs, n_layers=2, debug=False, trace=False, stop=1000):
    kb = KB(seqs, n_layers=n_layers, debug=debug)
    kb.stop = stop
    nc = kb.build()
    cst = _consts()
    in_maps = []
    for xs_, cs_ in zip(seq_lists_x, seq_lists_c):
        m = {"x": np.ascontiguousarray(np.concatenate(xs_, 0), dtype=np.float32),
             "cT": np.ascontiguousarray(np.stack(cs_, 1), dtype=np.float32)}
        for k in _WNAMES:
            m[k] = np.ascontiguousarray(weights[k][:n_layers], dtype=np.float32)
        m.update(cst)
        in_maps.append(m)
    res = run_bass_kernel_spmd(nc, in_maps, core_ids=list(range(len(in_maps))), trace=trace)
    return res, kb


def kernel(x_prompt, x_sample, c_prompt, c_sample, **weights):
    x_prompt = np.asarray(x_prompt, np.float32)
    x_sample = np.asarray(x_sample, np.float32)
    c_prompt = np.asarray(c_prompt, np.float32)
    c_sample = np.asarray(c_sample, np.float32)
    weights = {k: np.asarray(v, np.float32) for k, v in weights.items()}
    seqs = [4096, 2048, 2048, 2048, 2048]
    xs_, cs_ = [], []
    for i in range(8):
        xs_.append([x_prompt[i % 4]] + [x_sample[4 * i + j] for j in range(4)])
        cs_.append([c_prompt[i % 4]] + [c_sample[4 * i + j] for j in range(4)])
    res, kb = run_cores(xs_, cs_, weights, seqs)
    y_prompt = np.stack([res.results[i]["y"][0:4096] for i in range(4)], 0)
    y_sample = np.stack([res.results[i]["y"][4096 + 2048 * j:4096 + 2048 * (j + 1)] for i in range(8) for j in range(4)], 0)
    return (np.ascontiguousarray(y_prompt, dtype=np.float32), np.ascontiguousarray(y_sample, dtype=np.float32))
```
